# Optimizing a Trainium2 kernel written in Bass

```python
import math
import jax, jax.numpy as jnp
from jax import lax
import numpy as np

D_MODEL = 1024
BATCH = 8
SEQ = 2048
DEPTH = 2
DEC_BATCH = 128
DEC_SEQ = 4
PAST_LEN = 16384
PAGE_SIZE = 128

N_MIXERS = 2
GLA_HEADS = 4
GLA_DK = D_MODEL // 2
GLA_DV = D_MODEL
GLA_DK_HEAD = GLA_DK // GLA_HEADS
GLA_DV_HEAD = GLA_DV // GLA_HEADS
GLA_GATE_RANK = 16
GLA_TAU = 16.0
GLA_CHUNK = 64
GLA_IN = 2 * GLA_DK + 2 * GLA_DV + GLA_GATE_RANK
CONV_WIDTH = 3
D_FF = 4 * D_MODEL
N_GLA = (DEPTH + 1) // 2
N_CONV = DEPTH // 2
ALPHA = (2 * DEPTH) ** 0.25
BETA = (8 * DEPTH) ** -0.25
LN_EPS = 1e-5
RMS_EPS = 1e-6

kernel_name = "gla_shortconv_hybrid_step"


def layer_norm(x, g, b):
    xf = x.astype(jnp.float32)
    mu = jnp.mean(xf, axis=-1, keepdims=True)
    var = jnp.mean(jnp.square(xf - mu), axis=-1, keepdims=True)
    return ((xf - mu) * lax.rsqrt(var + LN_EPS) * g.astype(jnp.float32) + b.astype(jnp.float32)).astype(x.dtype)


def gla_core(q, k, v, g, s0):
    bsz, t, nh, dk = q.shape
    dv = v.shape[-1]
    c = GLA_CHUNK if t % GLA_CHUNK == 0 else t
    n = t // c

    def blk(a):
        return a.reshape(bsz, n, c, nh, a.shape[-1]).transpose(0, 3, 1, 2, 4)

    q, k, v, g = blk(q), blk(k), blk(v), blk(g)
    b = jnp.cumsum(g, axis=3)
    b_last = b[..., -1:, :]
    q_dec = q * jnp.exp(b)
    k_dec = k * jnp.exp(-b)
    causal = jnp.tril(jnp.ones((c, c), dtype=bool))
    scores = jnp.where(causal, jnp.einsum('bhnid,bhnjd->bhnij', q_dec, k_dec), 0.0)
    o_intra = jnp.einsum('bhnij,bhnjv->bhniv', scores, v)
    kv = jnp.einsum('bhncd,bhncv->nbhdv', k * jnp.exp(b_last - b), v)
    decay = jnp.exp(b_last[..., 0, :]).transpose(2, 0, 1, 3)

    def step(s, inp):
        dec, kv_n = inp
        return dec[..., None] * s + kv_n, s

    s_final, s_prev = lax.scan(step, s0, (decay, kv))
    o_inter = jnp.einsum('bhncd,nbhdv->bhncv', q_dec, s_prev)
    o = (o_intra + o_inter).transpose(0, 2, 3, 1, 4).reshape(bsz, t, nh, dv)
    return o, s_final


def gla_mixer(x, s0, w_in, w_gate_up, b_gate, norm_g, w_o):
    bsz, t, _ = x.shape
    proj = x @ w_in
    q, k, v, r, gl = jnp.split(proj, [GLA_DK, 2 * GLA_DK, 2 * GLA_DK + GLA_DV, 2 * GLA_DK + 2 * GLA_DV], axis=-1)
    log_a = jax.nn.log_sigmoid((gl @ w_gate_up + b_gate).astype(jnp.float32)) / GLA_TAU
    q = q.astype(jnp.float32).reshape(bsz, t, GLA_HEADS, GLA_DK_HEAD) * (GLA_DK_HEAD ** -0.5)
    k = k.astype(jnp.float32).reshape(bsz, t, GLA_HEADS, GLA_DK_HEAD)
    v = v.astype(jnp.float32).reshape(bsz, t, GLA_HEADS, GLA_DV_HEAD)
    g = log_a.reshape(bsz, t, GLA_HEADS, GLA_DK_HEAD)
    o, s = gla_core(q, k, v, g, s0.astype(jnp.float32))
    o = o * lax.rsqrt(jnp.mean(jnp.square(o), axis=-1, keepdims=True) + RMS_EPS) * norm_g.astype(jnp.float32)
    o = o.reshape(bsz, t, GLA_DV).astype(x.dtype) * jax.nn.silu(r)
    return o @ w_o, s.astype(s0.dtype)


def conv_mixer(x, buf, w_in, w_conv, w_out):
    t = x.shape[1]
    bg, cg, h = jnp.split(x @ w_in, 3, axis=-1)
    u = cg * h
    full = jnp.concatenate([buf.astype(u.dtype), u], axis=1)
    conv = full[:, 0:t, :] * w_conv[0]
    for i in range(1, CONV_WIDTH):
        conv = conv + full[:, i:i + t, :] * w_conv[i]
    return (bg * conv) @ w_out, full[:, -(CONV_WIDTH - 1):, :].astype(buf.dtype)


def mlp(x, w_up, w_down):
    return jnp.square(jax.nn.relu(x @ w_up)) @ w_down


def trunk(x, gla_states, conv_bufs, gla_w_in, gla_w_gate_up, gla_b_gate, gla_norm_g, gla_w_o,
          conv_w_in, conv_w_conv, conv_w_out, mlp_w_up, mlp_w_down, ln1_g, ln1_b, ln2_g, ln2_b):
    new_gla, new_conv = [], []
    for i in range(DEPTH):
        j = i // N_MIXERS
        if i % N_MIXERS == 0:
            h, s = gla_mixer(x, gla_states[j], gla_w_in[j], gla_w_gate_up[j], gla_b_gate[j], gla_norm_g[j], gla_w_o[j])
            new_gla.append(s)
        else:
            h, s = conv_mixer(x, conv_bufs[j], conv_w_in[j], conv_w_conv[j], conv_w_out[j])
            new_conv.append(s)
        x = layer_norm(ALPHA * x + h, ln1_g[i], ln1_b[i])
        x = layer_norm(ALPHA * x + mlp(x, mlp_w_up[i], mlp_w_down[i]), ln2_g[i], ln2_b[i])
    return x, jnp.stack(new_gla), jnp.stack(new_conv)


def setup_inputs(seed: int = 0) -> dict:
    key = jax.random.key(seed)
    ks = jax.random.split(key, 20)
    nrm = jax.random.normal
    f32 = jnp.float32
    return {
        "x_prompt": nrm(ks[0], (BATCH, SEQ, D_MODEL), f32),
        "x_sample": nrm(ks[1], (DEC_BATCH, DEC_SEQ, D_MODEL), f32),
        "state_gla": 0.5 * nrm(ks[2], (N_GLA, DEC_BATCH, GLA_HEADS, GLA_DK_HEAD, GLA_DV_HEAD), f32),
        "state_conv": nrm(ks[3], (N_CONV, DEC_BATCH, CONV_WIDTH - 1, D_MODEL), f32),
        "gla_w_in": nrm(ks[4], (N_GLA, D_MODEL, GLA_IN), f32) * D_MODEL ** -0.5,
        "gla_w_gate_up": nrm(ks[5], (N_GLA, GLA_GATE_RANK, GLA_DK), f32) * GLA_GATE_RANK ** -0.5,
        "gla_b_gate": 0.1 * nrm(ks[6], (N_GLA, GLA_DK), f32),
        "gla_norm_g": 1.0 + 0.01 * nrm(ks[7], (N_GLA, GLA_HEADS, GLA_DV_HEAD), f32),
        "gla_w_o": nrm(ks[8], (N_GLA, GLA_DV, D_MODEL), f32) * (GLA_DV ** -0.5 * BETA),
        "conv_w_in": nrm(ks[9], (N_CONV, D_MODEL, 3 * D_MODEL), f32) * D_MODEL ** -0.5,
        "conv_w_conv": nrm(ks[10], (N_CONV, CONV_WIDTH, D_MODEL), f32) * CONV_WIDTH ** -0.5,
        "conv_w_out": nrm(ks[11], (N_CONV, D_MODEL, D_MODEL), f32) * (D_MODEL ** -0.5 * BETA),
        "mlp_w_up": nrm(ks[12], (DEPTH, D_MODEL, D_FF), f32) * D_MODEL ** -0.5,
        "mlp_w_down": nrm(ks[13], (DEPTH, D_FF, D_MODEL), f32) * (D_FF ** -0.5 * BETA),
        "ln1_g": 1.0 + 0.01 * nrm(ks[14], (DEPTH, D_MODEL), f32),
        "ln1_b": 0.01 * nrm(ks[15], (DEPTH, D_MODEL), f32),
        "ln2_g": 1.0 + 0.01 * nrm(ks[16], (DEPTH, D_MODEL), f32),
        "ln2_b": 0.01 * nrm(ks[17], (DEPTH, D_MODEL), f32),
    }


def reference(x_prompt, x_sample, state_gla, state_conv, gla_w_in, gla_w_gate_up, gla_b_gate, gla_norm_g,
              gla_w_o, conv_w_in, conv_w_conv, conv_w_out, mlp_w_up, mlp_w_down, ln1_g, ln1_b, ln2_g, ln2_b):
    weights = (gla_w_in, gla_w_gate_up, gla_b_gate, gla_norm_g, gla_w_o, conv_w_in, conv_w_conv, conv_w_out,
               mlp_w_up, mlp_w_down, ln1_g, ln1_b, ln2_g, ln2_b)
    gla0 = jnp.zeros((N_GLA, BATCH, GLA_HEADS, GLA_DK_HEAD, GLA_DV_HEAD), state_gla.dtype)
    conv0 = jnp.zeros((N_CONV, BATCH, CONV_WIDTH - 1, D_MODEL), state_conv.dtype)
    y_prompt, gla_p, conv_p = trunk(x_prompt, gla0, conv0, *weights)
    y_sample, gla_s, conv_s = trunk(x_sample, state_gla, state_conv, *weights)
    return (y_prompt, y_sample, gla_p, gla_s, conv_p, conv_s)
```

```python
import os
import numpy as np
from contextlib import ExitStack
import concourse.bass as bass
import concourse.mybir as mybir
from concourse.bass_utils import run_bass_kernel_spmd

F32 = mybir.dt.float32
BF16 = mybir.dt.bfloat16
AF = mybir.ActivationFunctionType
ALU = mybir.AluOpType

D = 1024
NCH = 8
NPT = 512
NSAMP = 64
NSEQ = 16
NST = 4
NTMAX = NPT + NSAMP
ALPHA = 4.0 ** 0.25
LN_EPS = 1e-5
RMS_EPS = 1e-6
QSCALE = 128.0 ** -0.5
NSLOT = 3
NWARM_LN = int(os.environ.get("KWLN", "12"))
NWARM_WO = int(os.environ.get("KWWO", "14"))
NWARM_CORE = int(os.environ.get("KWCORE", "0"))
SPLIT = 0

C_ID, C_UP, C_M2P, C_CP, C_US, C_M2S, C_CS, C_SEQ = 0, 128, 256, 384, 512, 640, 768, 896
C_CP4 = 912
C_CS4 = 1424
C_TOT = 1680


LEVEL = int(os.environ.get("KLEVEL", "99"))


class _Stop(Exception):
    pass


def chk(l):
    if LEVEL < l:
        raise _Stop()


class Own:
    def __init__(self, sem):
        self.sem = sem
        self.n = 0


class Eng(Own):
    def __init__(self, e, sem, is_pe=False):
        super().__init__(sem)
        self.e = e
        self.seen = {}
        self.is_pe = is_pe


SCOPES = []


class T:
    __slots__ = ("w", "r", "excl")

    def __init__(self):
        self.w = None
        self.r = {}
        self.excl = False
        if SCOPES:
            SCOPES[-1].append(self)


class Scope(ExitStack):
    def __init__(self, k):
        super().__init__()
        self.k = k

    def __enter__(self):
        SCOPES.append([])
        return super().__enter__()

    def __exit__(self, *a):
        lst = SCOPES.pop()
        if a[0] is None:
            self.k.scope_barrier(lst)
        return super().__exit__(*a)


def run(*gens):
    gens = [g for g in gens if g is not None]
    while gens:
        for g in list(gens):
            try:
                next(g)
            except StopIteration:
                gens.remove(g)


def TS(*shape):
    if len(shape) == 1:
        return [T() for _ in range(shape[0])]
    return [TS(*shape[1:]) for _ in range(shape[0])]


class K:
    def __init__(self, nc, st):
        self.nc = nc
        self.st = st
        self._nm = 0
        self.pe = Eng(nc.tensor, self.sem("s_pe"), is_pe=True)
        self.act = Eng(nc.scalar, self.sem("s_act"))
        self.dve = Eng(nc.vector, self.sem("s_dve"))
        self.pool = Eng(nc.gpsimd, self.sem("s_pool"))
        self.sp = Eng(nc.sync, self.sem("s_sp"))
        self.engs = [self.pe, self.act, self.dve, self.pool, self.sp]
        self.dsems = []

    def sem(self, name):
        return self.st.enter_context(self.nc.semaphore(name))

    def dsem(self, name, bar=True):
        d = Own(self.sem(name))
        d.bar = bar
        self.dsems.append(d)
        return d

    def uname(self, base):
        self._nm += 1
        return "%s_%d" % (base, self._nm)

    def _wait(self, E, tok):
        if tok is None:
            return
        o, c = tok
        if o is E and E.is_pe:
            return
        if E.seen.get(o, 0) >= c:
            return
        E.e.wait_ge(o.sem, c)
        E.seen[o] = c

    def _deps(self, E, reads, writes):
        ex = [t for t in reads if t.excl]
        if ex:
            reads = [t for t in reads if not t.excl]
            writes = list(writes) + ex
        for t in reads:
            self._wait(E, t.w)
        for t in writes:
            self._wait(E, t.w)
            for o, c in list(t.r.items()):
                self._wait(E, (o, c))

    def _commit(self, tok, reads, writes):
        o, c = tok
        ex = [t for t in reads if t.excl]
        if ex:
            reads = [t for t in reads if not t.excl]
            writes = list(writes) + ex
        for t in reads:
            if t.r.get(o, 0) < c:
                t.r[o] = c
        for t in writes:
            t.w = tok
            t.r = {}

    def op(self, E, fn, reads=(), writes=()):
        self._deps(E, reads, writes)
        ins = fn(E.e)
        E.n += 1
        ins.then_inc(E.sem, 1)
        tok = (E, E.n)
        self._commit(tok, reads, writes)
        return tok

    def mm(self, fns, reads=(), writes=(), warm=()):
        E = self.pe
        self._deps(E, reads, writes)
        for f in warm:
            f(E.e)
        ins = None
        late = []
        for f in fns:
            if isinstance(f, tuple):
                f, lr = f
                for t in lr:
                    self._wait(E, t.w)
                late += lr
            ins = f(E.e)
        E.n += 1
        ins.then_inc(E.sem, 1)
        tok = (E, E.n)
        self._commit(tok, list(reads) + late, writes)
        return tok

    def dma(self, Q, ds, out, in_, reads=(), writes=()):
        self._deps(Q, reads, writes)
        Q.e.dma_start(out=out, in_=in_).then_inc(ds.sem, 16)
        ds.n += 16
        tok = (ds, ds.n)
        self._commit(tok, reads, writes)
        return tok

    def scope_barrier(self, lst):
        for E in self.engs:
            for t in lst:
                self._wait(E, t.w)
                for o, c in list(t.r.items()):
                    self._wait(E, (o, c))

    def barrier(self):
        comp = [self.pe, self.act, self.dve, self.pool]
        for E in self.engs:
            for P in comp:
                if P.n > 0:
                    self._wait(E, (P, P.n))
            for d in self.dsems:
                if d.n > 0 and d.bar:
                    self._wait(E, (d, d.n))


def build_program(sts=tuple(range(NST))):
    del SCOPES[:]
    nc = bass.Bass("TRN2", target_bir_lowering=False)
    din = lambda name, shape: nc.dram_tensor(name, list(shape), F32, kind="ExternalInput").ap()
    dout = lambda name, shape: nc.dram_tensor(name, list(shape), F32, kind="ExternalOutput").ap()
    xp = din("xp", (2048, D))
    xs = din("xs", (NSAMP, D))
    sg = din("sg", (NSEQ, 4, 128, 256))
    scv = din("sc", (2 * NSEQ, D))
    w_in0 = din("gla_w_in", (D, 3088))
    w_gu = din("gla_w_gate_up", (16, 512))
    b_gt = din("gla_b_gate", (1, 512))
    w_o0 = din("gla_w_o", (D, D))
    cw_in = din("conv_w_in", (D, 3 * D))
    cw_out = din("conv_w_out", (D, D))
    w_up = din("mlp_w_up", (2, D, 4 * D))
    w_dn = din("mlp_w_down", (2, 4 * D, D))
    prm_d = din("prm", (128, 98))
    cst_d = din("cst", (128, C_TOT))
    yp = dout("yp", (2048, D))
    ys = dout("ys", (NSAMP, D))
    gp = dout("gp", (4, 128, 256))
    gs = dout("gs", (NSEQ, 4, 128, 256))
    cp = dout("cp", (2, D))
    cs = dout("cs", (2 * NSEQ, D))

    with ExitStack() as st:
        k = K(nc, st)
        pe, act, dve, pool, sp = k.pe, k.act, k.dve, k.pool, k.sp

        def sb(stack, name, shape, dt):
            return stack.enter_context(nc.sbuf_tensor(k.uname(name), list(shape), dt))

        xf = sb(st, "xf", (128, NCH, NTMAX), F32)
        xb = sb(st, "xb", (128, NCH, NTMAX), BF16)
        ring = sb(st, "ring", (128, NSLOT, NCH, 1024), BF16)
        glw = sb(st, "glw", (128, NCH, 16), BF16)
        wg = sb(st, "wg", (17, 512), F32)
        bgr = sb(st, "bgr", (1, 512), F32)
        onesr = sb(st, "onesr", (1, 128), F32)
        prm = sb(st, "prm", (128, 98), F32)
        cst = sb(st, "cst", (128, C_TOT), F32)
        ones_ln = sb(st, "ones_ln", (128, 128), BF16)
        ones_rms = sb(st, "ones_rms", (128, 128), BF16)
        zsq = sb(st, "zsq", (128, NCH, NTMAX), BF16)
        ln_mean = sb(st, "ln_mean", (128, NTMAX), F32)
        ln_var = sb(st, "ln_var", (128, NTMAX), F32)
        ln_rstd = sb(st, "ln_rstd", (128, NTMAX), F32)
        NTMP = 2
        ln_tmp = [sb(st, "ln_tmp", (128, NTMAX), F32) for _ in range(NTMP)]
        S32 = sb(st, "S32", (128, 4, 256), F32)
        Sbf = [sb(st, "Sbf", (128, 4, 256), BF16) for _ in range(3)]
        tailb = sb(st, "tailb", (128, NCH, 2), F32)
        xf_t = TS(NCH, 5)
        xb_t = TS(NCH, 5)
        ring_t = TS(NSLOT)
        ring_ds = [k.dsem("d_ring%d" % i, bar=False) for i in range(NSLOT)]
        c_t = T()
        c_ds = k.dsem("d_cst")
        S32_t = TS(4)
        Sbf_t = TS(3)
        gsb = {"n": 0}
        tail_t = T()
        zsq_t = TS(NCH, 5)
        mean_t, var_t, rstd_t = T(), T(), T()
        tmp_t = TS(NTMP)

        pb = [st.enter_context(nc.psum_tensor("pb%d" % i, [128, 512], F32)) for i in range(8)]
        pb_t = TS(8)
        for t_ in pb_t:
            t_.excl = True

        ident = cst[:, C_ID:C_ID + 128]

        def prm_col(idx):
            return prm[:, idx:idx + 1]

        k.dma(sp, c_ds, cst[:], cst_d[:, :], writes=[c_t])
        k.dma(sp, c_ds, prm[:], prm_d[:, :], writes=[c_t])
        k.dma(sp, c_ds, wg[0:16, :], w_gu[:, :], writes=[c_t])
        k.dma(sp, c_ds, wg[16:17, :], b_gt[:, :], writes=[c_t])
        c2_ds = k.dsem("d_cst2")
        k.dma(pool, c2_ds, glw[:], w_in0[:, 3072:3088].rearrange("(k p) n -> p k n", p=128), writes=[c_t])
        k.op(dve, lambda e: e.memset(onesr[:], 1.0), writes=[c_t])
        k.op(dve, lambda e: e.memset(ones_ln[:], 1.0 / 1024.0), writes=[c_t])
        k.op(dve, lambda e: e.memset(ones_rms[:], 1.0 / 256.0), writes=[c_t])
        k.op(dve, lambda e: e.memset(S32[:], 0.0), writes=S32_t)
        k.op(dve, lambda e: e.memset(Sbf[0][:], 0.0), writes=[Sbf_t[0]])
        k.op(dve, lambda e: e.memset(tailb[:], 0.0), writes=[tail_t])
        k.barrier()

        def slab_list():
            L = []
            for s_ in sts:
                L += [w_in0[:, 1024:2048], w_in0[:, 0:1024], w_in0[:, 2048:3072], w_o0[:, :]]
                for q in range(4):
                    L += [w_up[0, :, q * 1024:(q + 1) * 1024], w_dn[0, q * 1024:(q + 1) * 1024, :]]
                L += [cw_in[:, 1024:2048], cw_in[:, 2048:3072], cw_in[:, 0:1024], cw_out[:, :]]
                for q in range(4):
                    L += [w_up[1, :, q * 1024:(q + 1) * 1024], w_dn[1, q * 1024:(q + 1) * 1024, :]]
            return L

        slabs = slab_list()
        sl = {"next_load": 0, "next_use": 0}

        def load_next():
            j = sl["next_load"]
            if j >= len(slabs):
                return
            s_ = j % NSLOT
            k.dma(pool, ring_ds[s_], ring[:, s_], slabs[j].rearrange("(k p) n -> p k n", p=128), writes=[ring_t[s_]])
            sl["next_load"] = j + 1

        def next_slab():
            j = sl["next_use"]
            while sl["next_load"] < min(j + NSLOT, len(slabs)):
                load_next()
            sl["next_use"] = j + 1
            s_ = j % NSLOT
            return ring[:, s_], ring_t[s_]

        misc_rr = {"i": 0}

        def misc_bank():
            i = 4 + (misc_rr["i"] % 4)
            misc_rr["i"] += 1
            return pb[i], pb_t[i]

        xin = [sb(st, "xin", (128, D), F32) for _ in range(2)]
        xin_t = TS(2)
        xin_ds = [k.dsem("d_xin%d" % i) for i in range(2)]
        x_pref = {}

        def x_load(stn_, i):
            nsamp_ = NSAMP if stn_ == NST - 1 else 0
            t0, rows = (i * 128, 128) if i < 4 else (NPT, nsamp_)
            src = xp[stn_ * NPT + t0: stn_ * NPT + t0 + rows, :] if i < 4 else xs[0:rows, :]
            k.dma(sp, xin_ds[i % 2], xin[i % 2][0:rows, :], src, writes=[xin_t[i % 2]])
            x_pref[(stn_, i)] = True

        def prefetch_x(stn_):
            for i in range(2):
                if (stn_, i) not in x_pref:
                    x_load(stn_, i)

        def stage_X(stn_):
            nsamp_ = NSAMP if stn_ == NST - 1 else 0
            ttiles_ = [(i * 128, 128) for i in range(4)] + ([(NPT, nsamp_)] if nsamp_ else [])
            if True:
                for i, (t0, rows) in enumerate(ttiles_):
                    b = i % 2
                    if (stn_, i) not in x_pref:
                        x_load(stn_, i)
                    for half in range(2):
                        pt, pt_t = misc_bank()
                        k.mm([lambda e, j=j: e.transpose(out=pt[:, j * 128:j * 128 + rows],
                                                         in_=xin[b][0:rows, (half * 4 + j) * 128:(half * 4 + j + 1) * 128],
                                                         identity=cst[0:rows, C_ID:C_ID + rows]) for j in range(4)],
                             reads=[xin_t[b], c_t], writes=[pt_t])
                        cs_ = range(half * 4, half * 4 + 4)
                        src_ps = pt[:, :].rearrange("p (a b) -> p a b", a=4)[:, :, 0:rows]
                        if half == 0:
                            k.op(act, lambda e: e.activation(out=xf[:, half * 4:half * 4 + 4, t0:t0 + rows], in_=src_ps, func=AF.Copy),
                                 reads=[pt_t], writes=[xf_t[c][i] for c in cs_])
                            k.op(dve, lambda e: e.tensor_copy(out=xb[:, half * 4:half * 4 + 4, t0:t0 + rows], in_=xf[:, half * 4:half * 4 + 4, t0:t0 + rows]),
                                 reads=[xf_t[c][i] for c in cs_], writes=[xb_t[c][i] for c in cs_])
                        else:
                            k.op(dve, lambda e: e.tensor_copy(out=xf[:, half * 4:half * 4 + 4, t0:t0 + rows], in_=src_ps),
                                 reads=[pt_t], writes=[xf_t[c][i] for c in cs_])
                            k.op(act, lambda e: e.activation(out=xb[:, half * 4:half * 4 + 4, t0:t0 + rows], in_=xf[:, half * 4:half * 4 + 4, t0:t0 + rows], func=AF.Copy),
                                 reads=[xf_t[c][i] for c in cs_], writes=[xb_t[c][i] for c in cs_])

        pend = None
        for stn in sts:
          try:
            nsamp = NSAMP if stn == NST - 1 else 0
            NT = NPT + nsamp
            ntiles = [(0, NPT)] + ([(NPT, nsamp)] if nsamp else [])
            ttiles = [(i * 128, 128) for i in range(4)] + ([(NPT, nsamp)] if nsamp else [])
            if SPLIT:
                halfA = [(0, 256)]
                halfB = [(256, 256)] + ([(NPT, nsamp)] if nsamp else [])
            else:
                halfA, halfB = ntiles, []

            def tiles_of(n0, nn):
                return [i for i, (t0_, r_) in enumerate(ttiles) if n0 <= t0_ < n0 + nn]

            def xtr(tr, cs_, n0, nn):
                return [tr[c][i] for c in cs_ for i in tiles_of(n0, nn)]

            def linear_A_gen(slab, slab_t, src, src_tr, evac, ntl=None, nsets=None, nwarm=0):
                ntl = ntiles if ntl is None else ntl
                if not ntl:
                    return
                if nsets is None:
                    nsets = 4 if len(ntl) == 1 else 2
                for m in range(NCH):
                    set_ = m % nsets
                    banks = [(pb[set_ + 2 * ni], pb_t[set_ + 2 * ni]) for ni in range(len(ntl))]
                    fns = []
                    for kk_ in range(NCH):
                        for ni, (n0, nn) in enumerate(ntl):
                            fns.append((lambda e, kk_=kk_, ni=ni, n0=n0, nn=nn, m=m: e.matmul(
                                banks[ni][0][:, 0:nn], lhsT=slab[:, kk_, m * 128:(m + 1) * 128],
                                rhs=src[:, kk_, n0:n0 + nn], start=(kk_ == 0), stop=(kk_ == NCH - 1)),
                                [src_tr[kk_][i] for i in tiles_of(n0, nn)]))
                    warm = []
                    if m == 0 and nwarm:
                        warm = [lambda e: e.matmul(banks[0][0][:, 0:512], lhsT=slab[:, 0, 0:128], rhs=slab[:, 1, 0:512], start=True, stop=True,
                                                   skip_group_check=True) for _ in range(nwarm)]
                    k.mm(fns, reads=[slab_t], writes=[b[1] for b in banks], warm=warm)
                    for ni, (n0, nn) in enumerate(ntl):
                        evac(m, ni, n0, nn, banks[ni][0], banks[ni][1])
                    yield

            def linear_A(*a, **kw):
                for _ in linear_A_gen(*a, **kw):
                    pass

            def linear_after_ln(slab, slab_t, src, src_tr, evac):
                n0, nn = ntiles[0]
                til = tiles_of(n0, nn)
                for kk_ in range(NCH):
                    warm = []
                    if kk_ == 0:
                        warm = [lambda e, j=j: e.matmul(pb[j % NCH][:, 0:512], lhsT=slab[:, 0, 0:128], rhs=slab[:, 1, 0:512], start=True, stop=True,
                                                        skip_group_check=True) for j in range(NWARM_LN)]
                    k.mm([((lambda e, kk_=kk_, m=m: e.matmul(pb[m][:, 0:nn], lhsT=slab[:, kk_, m * 128:(m + 1) * 128], rhs=src[:, kk_, n0:n0 + nn],
                                                           start=(kk_ == 0), stop=(kk_ == NCH - 1), skip_group_check=True)),
                           ([src_tr[kk_][i] for i in til] if m == 0 else [])) for m in range(NCH)],
                         reads=[slab_t], writes=pb_t, warm=warm)
                for m in range(NCH):
                    evac(m, 0, n0, nn, pb[m], pb_t[m])
                if len(ntiles) > 1:
                    linear_A(slab, slab_t, src, src_tr, evac, ntl=ntiles[1:])

            def make_z_evac(first=True, last=True):
                def evac(m, ni, n0, nn, ps, ps_t):
                    xt = xtr(xf_t, [m], n0, nn)
                    if first:
                        k.op(dve, lambda e: e.scalar_tensor_tensor(out=xf[:, m, n0:n0 + nn], in0=xf[:, m, n0:n0 + nn], scalar=ALPHA,
                                                                   in1=ps[:, 0:nn], op0=ALU.mult, op1=ALU.add),
                             reads=[ps_t], writes=xt)
                    else:
                        k.op(dve, lambda e: e.tensor_tensor(out=xf[:, m, n0:n0 + nn], in0=xf[:, m, n0:n0 + nn], in1=ps[:, 0:nn], op=ALU.add),
                             reads=[ps_t], writes=xt)
                    if last:
                        k.op(act, lambda e: e.activation(out=xb[:, m, n0:n0 + nn], in_=xf[:, m, n0:n0 + nn], func=AF.Copy),
                             reads=xt, writes=xtr(xb_t, [m], n0, nn))
                        k.op(act, lambda e: e.activation(out=zsq[:, m, n0:n0 + nn], in_=xf[:, m, n0:n0 + nn], func=AF.Square),
                             reads=xt, writes=xtr(zsq_t, [m], n0, nn))
                return evac

            def layer_norm_gen(ntl, gcol, bcol, want_bf=True):
                mean, var, rstd, tmp = ln_mean, ln_var, ln_rstd, ln_tmp
                if not ntl:
                    return
                a0 = ntl[0][0]
                tot = sum(nn for _, nn in ntl)
                for ni, (n0, nn) in enumerate(ntl):
                    o0 = n0 - a0
                    pm, pm_t = misc_bank()
                    pq, pq_t = misc_bank()
                    k.mm([(lambda e, c=c: e.matmul(pm[:, 0:nn], lhsT=ones_ln[:], rhs=xb[:, c, n0:n0 + nn], start=(c == 0), stop=(c == NCH - 1)),
                           xtr(xb_t, [c], n0, nn)) for c in range(NCH)], reads=[c_t], writes=[pm_t])
                    k.mm([(lambda e, c=c: e.matmul(pq[:, 0:nn], lhsT=ones_ln[:], rhs=zsq[:, c, n0:n0 + nn], start=(c == 0), stop=(c == NCH - 1)),
                           xtr(zsq_t, [c], n0, nn)) for c in range(NCH)], reads=[c_t], writes=[pq_t])
                    k.op(act, lambda e: e.activation(out=var[:, o0:o0 + nn], in_=pm[:, 0:nn], func=AF.Square), reads=[pm_t], writes=[var_t])
                    k.op(dve, lambda e: e.tensor_tensor(out=var[:, o0:o0 + nn], in0=pq[:, 0:nn], in1=var[:, o0:o0 + nn], op=ALU.subtract),
                         reads=[pq_t, var_t], writes=[var_t])
                    k.op(dve, lambda e: e.tensor_copy(out=mean[:, o0:o0 + nn], in_=pm[:, 0:nn]), reads=[pm_t], writes=[mean_t])
                n0, nn = a0, tot
                k.op(act, lambda e: e.activation(out=rstd[:, 0:nn], in_=var[:, 0:nn], func=AF.Ln, bias=prm_col(96)),
                     reads=[var_t, c_t], writes=[rstd_t])
                k.op(act, lambda e: e.activation(out=rstd[:, 0:nn], in_=rstd[:, 0:nn], func=AF.Exp, scale=-0.5), reads=[rstd_t], writes=[rstd_t])
                yield
                for c in range(NCH):
                    tb, tb_t = tmp[c % NTMP], tmp_t[c % NTMP]
                    xt = xtr(xf_t, [c], n0, nn)
                    k.op(dve, lambda e: e.tensor_tensor(out=tb[:, 0:nn], in0=xf[:, c, n0:n0 + nn], in1=mean[:, 0:nn], op=ALU.subtract),
                         reads=xt + [mean_t], writes=[tb_t])
                    k.op(dve, lambda e: e.scalar_tensor_tensor(out=tb[:, 0:nn], in0=tb[:, 0:nn], scalar=prm_col(gcol + c), in1=rstd[:, 0:nn],
                                                               op0=ALU.mult, op1=ALU.mult),
                         reads=[rstd_t, tb_t, c_t], writes=[tb_t])
                    k.op(act, lambda e: e.activation(out=xf[:, c, n0:n0 + nn], in_=tb[:, 0:nn], func=AF.Identity, bias=prm_col(bcol + c)),
                         reads=[tb_t, c_t], writes=xt)
                    if want_bf:
                        k.op(act, lambda e: e.activation(out=xb[:, c, n0:n0 + nn], in_=tb[:, 0:nn], func=AF.Identity, bias=prm_col(bcol + c)),
                             reads=[tb_t, c_t], writes=xtr(xb_t, [c], n0, nn))
                    yield

            def mlp_stage(layer, final=False, pre=None):
                g2, b2 = (layer * 4 + 2) * 8, (layer * 4 + 3) * 8
                with Scope(k) as ms:
                    hq = [sb(ms, "hq", (128, NCH, NTMAX), BF16) for _ in range(2)]
                    hq_t = [TS(NCH, 5) for _ in range(2)]
                    sqt = [sb(ms, "sqt", (128, 512), F32) for _ in range(2)]
                    sqt_t = TS(2)
                    cnt = {"i": 0}
                    for q in range(4):
                        hb, hb_t = hq[q % 2], hq_t[q % 2]
                        slab, slab_t = next_slab()

                        def evac_up(m, ni, n0, nn, ps, ps_t, hb=hb, hb_t=hb_t):
                            j = cnt["i"] % 2
                            cnt["i"] += 1
                            k.op(act, lambda e: e.activation(out=sqt[j][:, 0:nn], in_=ps[:, 0:nn], func=AF.Square), reads=[ps_t], writes=[sqt_t[j]])
                            k.op(dve, lambda e: e.scalar_tensor_tensor(out=hb[:, m, n0:n0 + nn], in0=ps[:, 0:nn], scalar=0.0, in1=sqt[j][:, 0:nn],
                                                                       op0=ALU.is_gt, op1=ALU.mult),
                                 reads=[ps_t, sqt_t[j]], writes=[hb_t[m][i] for i in tiles_of(n0, nn)])

                        if q == 0 and SPLIT:
                            run(linear_A_gen(slab, slab_t, xb, xb_t, evac_up, ntl=halfA), pre)
                            linear_A(slab, slab_t, xb, xb_t, evac_up, ntl=halfB)
                        elif q == 0:
                            run(pre)
                            linear_after_ln(slab, slab_t, xb, xb_t, evac_up)
                        else:
                            linear_A(slab, slab_t, xb, xb_t, evac_up)
                        slab, slab_t = next_slab()
                        if q < 3 or not SPLIT:
                            linear_A(slab, slab_t, hb, hb_t, make_z_evac(first=(q == 0), last=(q == 3)))
                        else:
                            linear_A(slab, slab_t, hb, hb_t, make_z_evac(first=False, last=True), ntl=halfA)
                            run(linear_A_gen(slab, slab_t, hb, hb_t, make_z_evac(first=False, last=True), ntl=halfB),
                                layer_norm_gen(halfA, g2, b2, want_bf=not final))
                return layer_norm_gen(halfB if SPLIT else ntiles, g2, b2, want_bf=not final)

            chk(1)
            if stn == sts[0]:
                run(pend)
                pend = None
                stage_X(stn)
            chk(2)
            with Scope(k) as ss:
                og = sb(ss, "og", (128, NCH, NTMAX), BF16)
                og_t = TS(NCH, 5)
                with Scope(k) as s2:
                    glT = sb(s2, "glT", (32, NTMAX), F32)
                    eb = sb(s2, "eb", (128, 4, NTMAX), F32)
                    enb = sb(s2, "enb", (128, 4, NTMAX), F32)
                    e1 = [sb(s2, "e1", (128, 512), F32) for _ in range(1)]
                    spt = [sb(s2, "spt", (128, 512), F32) for _ in range(1)]
                    ec = sb(s2, "ec", (128, 5, 512), F32)
                    vt = sb(s2, "vt", (128, 5, 1024), BF16)
                    qd = sb(s2, "qd", (128, 4, NTMAX), BF16)
                    kd = sb(s2, "kd", (128, 4, NTMAX), BF16)
                    qd32 = sb(s2, "qd32", (128, 4, NSAMP), F32)
                    kkt = sb(s2, "kkt", (128, 5, 512), BF16)
                    glT_t = TS(5)
                    eb_t = TS(4, 5)
                    enb_t = TS(4, 5)
                    e1_t, spt_t = TS(1), TS(1)
                    ec_t, vt_t, kkt_t = TS(5), TS(5), TS(5)
                    qd_t, kd_t = TS(4, 5), TS(4, 5)
                    qd32_t = T()

                    k.op(dve, lambda e: e.memset(glT[:], 1.0), writes=glT_t)
                    for ni, (n0, nn) in enumerate(ntiles):
                        pg, pg_t = misc_bank()
                        k.mm([lambda e, c=c: e.matmul(pg[0:16, 0:nn], lhsT=glw[:, c, :], rhs=xb[:, c, n0:n0 + nn], start=(c == 0), stop=(c == NCH - 1))
                              for c in range(NCH)], reads=[c_t] + xtr(xb_t, range(NCH), n0, nn), writes=[pg_t])
                        k.op(dve, lambda e: e.tensor_copy(out=glT[0:16, n0:n0 + nn], in_=pg[0:16, 0:nn]), reads=[pg_t],
                             writes=[glT_t[i] for i in tiles_of(n0, nn)])
                    chk(3)
                    def gate_gen():
                        for i, (t0, rows) in enumerate(ttiles):
                            b = 0
                            co_u, co_m = (C_UP, C_M2P) if i < 4 else (C_US, C_M2S)
                            pg, pg_t = misc_bank()
                            k.mm([lambda e: e.matmul(pg[0:rows, :], lhsT=glT[0:17, t0:t0 + rows], rhs=wg[0:17, :], start=True, stop=True)],
                                 reads=[glT_t[i], c_t], writes=[pg_t])
                            k.op(act, lambda e: e.activation(out=e1[b][0:rows, :], in_=pg[0:rows, :], func=AF.Exp, scale=-1.0),
                                 reads=[pg_t], writes=[e1_t[b]])
                            k.op(act, lambda e: e.activation(out=spt[b][0:rows, :], in_=e1[b][0:rows, :], func=AF.Ln, bias=1.0),
                                 reads=[e1_t[b]], writes=[spt_t[b]])
                            pbb, pbb_t = misc_bank()
                            k.mm([lambda e, h=h: e.matmul(pbb[:, h * 128:h * 128 + rows], lhsT=spt[b][0:rows, h * 128:(h + 1) * 128],
                                                          rhs=cst[0:rows, co_u:co_u + rows], start=True, stop=True) for h in range(4)],
                                 reads=[spt_t[b], c_t], writes=[pbb_t])
                            pbv = pbb[:, :].rearrange("p (a b) -> p a b", a=4)[:, :, 0:rows]
                            k.op(act, lambda e: e.activation(out=eb[:, :, t0:t0 + rows], in_=pbv, func=AF.Exp),
                                 reads=[pbb_t], writes=[eb_t[h][i] for h in range(4)])
                            k.op(act, lambda e: e.activation(out=enb[:, :, t0:t0 + rows], in_=pbv, func=AF.Exp, scale=-1.0),
                                 reads=[pbb_t], writes=[enb_t[h][i] for h in range(4)])
                            pc, pc_t = misc_bank()
                            k.mm([lambda e: e.matmul(pc[0:rows, :], lhsT=cst[0:rows, co_m:co_m + rows], rhs=spt[b][0:rows, :], start=True, stop=True)],
                                 reads=[spt_t[b], c_t], writes=[pc_t])
                            k.op(act, lambda e: e.activation(out=ec[0:rows, i, :], in_=pc[0:rows, :], func=AF.Exp),
                                 reads=[pc_t], writes=[ec_t[i]])

                            yield

                    chk(4)
                    bank_rr = {"i": 0}

                    def acc_bank():
                        i_ = bank_rr["i"] % 4
                        bank_rr["i"] += 1
                        return pb[i_], pb_t[i_]

                    slab, slab_t = next_slab()
                    def v_gen():
                        for i, (t0, rows) in enumerate(ttiles):
                            for hh in range(2):
                                pv, pv_t = acc_bank()
                                k.mm([lambda e, c=c: e.matmul(pv[0:rows, :], lhsT=xb[:, c, t0:t0 + rows], rhs=slab[:, c, hh * 512:(hh + 1) * 512],
                                                              start=(c == 0), stop=(c == NCH - 1)) for c in range(NCH)],
                                     reads=[slab_t] + [xb_t[c][i] for c in range(NCH)], writes=[pv_t])
                                eng = act if hh == 0 else dve
                                if hh == 0:
                                    k.op(act, lambda e: e.activation(out=vt[0:rows, i, 0:512], in_=pv[0:rows, :], func=AF.Copy),
                                         reads=[pv_t], writes=[vt_t[i]])
                                else:
                                    k.op(dve, lambda e: e.tensor_copy(out=vt[0:rows, i, 512:1024], in_=pv[0:rows, :]),
                                         reads=[pv_t, vt_t[i]], writes=[vt_t[i]])
                            yield

                    run(v_gen(), gate_gen())

                    chk(5)
                    slab, slab_t = next_slab()

                    def evac_qk(m, ni, n0, nn, ps, ps_t):
                        if m < 4:
                            k.op(dve, lambda e: e.scalar_tensor_tensor(out=qd[:, m, n0:n0 + nn], in0=ps[:, 0:nn], scalar=QSCALE,
                                                                       in1=eb[:, m, n0:n0 + nn], op0=ALU.mult, op1=ALU.mult),
                                 reads=[ps_t] + [eb_t[m][i] for i in tiles_of(n0, nn)], writes=[qd_t[m][i] for i in tiles_of(n0, nn)])
                            if ni == 1:
                                k.op(dve, lambda e: e.scalar_tensor_tensor(out=qd32[:, m, 0:nn], in0=ps[:, 0:nn], scalar=QSCALE,
                                                                           in1=eb[:, m, n0:n0 + nn], op0=ALU.mult, op1=ALU.mult),
                                     reads=[ps_t] + [eb_t[m][4]], writes=[qd32_t])
                        else:
                            h = m - 4
                            k.op(dve, lambda e: e.tensor_tensor(out=kd[:, h, n0:n0 + nn], in0=ps[:, 0:nn], in1=enb[:, h, n0:n0 + nn], op=ALU.mult),
                                 reads=[ps_t] + [enb_t[h][i] for i in tiles_of(n0, nn)], writes=[kd_t[h][i] for i in tiles_of(n0, nn)])

                    linear_A(slab, slab_t, xb, xb_t, evac_qk)
                    for i, (t0, rows) in enumerate(ttiles):
                        pk, pk_t = acc_bank()
                        k.mm([lambda e, c=c: e.matmul(pk[0:rows, :], lhsT=xb[:, c, t0:t0 + rows], rhs=slab[:, c, 512:1024],
                                                      start=(c == 0), stop=(c == NCH - 1)) for c in range(NCH)],
                             reads=[slab_t] + [xb_t[c][i] for c in range(NCH)], writes=[pk_t])
                        k.op(dve, lambda e: e.tensor_tensor(out=kkt[0:rows, i, :], in0=pk[0:rows, :], in1=ec[0:rows, i, :], op=ALU.mult),
                             reads=[pk_t, ec_t[i]], writes=[kkt_t[i]])

                    chk(6)
                    slab, slab_t = next_slab()

                    def evac_r(m, ni, n0, nn, ps, ps_t):
                        k.op(act, lambda e: e.activation(out=og[:, m, n0:n0 + nn], in_=ps[:, 0:nn], func=AF.Silu),
                             reads=[ps_t], writes=[og_t[m][i] for i in tiles_of(n0, nn)])

                    linear_A(slab, slab_t, xb, xb_t, evac_r, ntl=halfA)
                    r_slab, r_slab_t = slab, slab_t

                    chk(7)
                    with Scope(k) as s3:
                        sT = sb(s3, "sT", (128, 4, 128), BF16)
                        o32 = sb(s3, "o32", (128, NCH, 128), F32)
                        osq = sb(s3, "osq", (128, NCH, 128), BF16)
                        rsd = sb(s3, "rsd", (128, 512), F32)
                        otm = sb(s3, "otm", (128, NCH, 128), F32)
                        sT_t, o32_t, osq_t, rsd_t, otm_t = T(), TS(2), TS(2), T(), T()

                        def rms_gate(i, t0, rows, pO, pO_t, after=None):
                            povs = [pO[bk][:, :].rearrange("p (a b) -> p a b", a=4)[:, :, 0:rows] for bk in range(2)]
                            for bk in range(2):
                                k.op(act, lambda e: e.activation(out=osq[:, bk * 4:bk * 4 + 4, 0:rows], in_=povs[bk], func=AF.Square),
                                     reads=[pO_t[bk]], writes=[osq_t[bk]])
                            if after is not None:
                                after()
                            pr, pr_t = pb[7], pb_t[7]
                            k.mm([lambda e, h=h, vc=vc: e.matmul(pr[:, h * 128:h * 128 + rows], lhsT=ones_rms[:], rhs=osq[:, h * 2 + vc, 0:rows],
                                                                 start=(vc == 0), stop=(vc == 1)) for h in range(4) for vc in range(2)],
                                 reads=osq_t + [c_t], writes=[pr_t])
                            for bk in range(2):
                                k.op(dve, lambda e: e.tensor_tensor(out=o32[:, bk * 4:bk * 4 + 4, 0:rows], in0=povs[bk],
                                                                    in1=prm[:, 64 + bk * 4:68 + bk * 4].unsqueeze(2).broadcast_to([128, 4, rows]), op=ALU.mult),
                                     reads=[pO_t[bk], c_t], writes=[o32_t[bk]])
                            prv = pr[:, :].rearrange("p (a b) -> p a b", a=4)[:, :, 0:rows]
                            rsv = rsd[:, :].rearrange("p (a b) -> p a b", a=4)[:, :, 0:rows]
                            k.op(act, lambda e: e.activation(out=rsv, in_=prv, func=AF.Ln, bias=prm_col(97)), reads=[pr_t, c_t], writes=[rsd_t])
                            k.op(act, lambda e: e.activation(out=rsv, in_=rsv, func=AF.Exp, scale=-0.5), reads=[rsd_t], writes=[rsd_t])
                            k.op(dve, lambda e: e.tensor_tensor(out=otm[:, :, 0:rows].rearrange("p (h v) i -> p h v i", h=4),
                                                                in0=o32[:, :, 0:rows].rearrange("p (h v) i -> p h v i", h=4),
                                                                in1=rsv.unsqueeze(2).broadcast_to([128, 4, 2, rows]), op=ALU.mult),
                                 reads=o32_t + [rsd_t], writes=[otm_t])
                            k.op(pool, lambda e: e.tensor_tensor(out=og[:, :, t0:t0 + rows], in0=otm[:, :, 0:rows], in1=og[:, :, t0:t0 + rows], op=ALU.mult),
                                 reads=[otm_t], writes=[og_t[c][i] for c in range(NCH)])

                        def core_tile_gen(i):
                            t0, rows = i * 128, 128
                            pS, pS_t = pb[2], pb_t[2]
                            k.mm([lambda e, h=h: e.matmul(pS[:, h * 128:(h + 1) * 128], lhsT=kd[:, h, t0:t0 + 128], rhs=qd[:, h, t0:t0 + 128],
                                                          start=True, stop=True) for h in range(4)],
                                 reads=[kd_t[h][i] for h in range(4)] + [qd_t[h][i] for h in range(4)], writes=[pS_t])
                            k.op(dve, lambda e: e.tensor_tensor(out=sT[:, :, :].rearrange("p h j -> p (h j)"), in0=pS[:, :], in1=cst[:, C_CP4:C_CP4 + 512], op=ALU.mult),
                                 reads=[pS_t, c_t], writes=[sT_t])
                            yield
                            kvb = [(3, 4), (0, 1)]
                            for cc in range(2):
                                r0 = cc * 64
                                for hb in range(2):
                                    bk = kvb[cc][hb]
                                    k.mm([lambda e, h=h: e.matmul(pb[bk][:, (h % 2) * 256:(h % 2 + 1) * 256], lhsT=kkt[r0:r0 + 64, i, h * 128:(h + 1) * 128],
                                                                  rhs=vt[r0:r0 + 64, i, h * 256:(h + 1) * 256], start=True, stop=True)
                                          for h in (2 * hb, 2 * hb + 1)],
                                         reads=[kkt_t[i], vt_t[i]], writes=[pb_t[bk]])
                            nA = gsb["n"]
                            pO = [pb[5], pb[6]]
                            pO_t = [pb_t[5], pb_t[6]]
                            for bk in range(2):
                                fns = []
                                first = True
                                for c in range(bk * 4, bk * 4 + 4):
                                    h, vc = c // 2, c % 2
                                    blk = pO[bk][:, (c % 4) * 128:(c % 4 + 1) * 128]
                                    fns.append(lambda e, blk=blk, h=h, vc=vc, first=first: e.matmul(
                                        blk[:, 0:128], lhsT=vt[:, i, h * 256 + vc * 128:h * 256 + (vc + 1) * 128], rhs=sT[:, h, :],
                                        start=first, stop=False, skip_group_check=True))
                                    first = False
                                    fns.append(lambda e, blk=blk, h=h, vc=vc: e.matmul(
                                        blk[:, 0:64], lhsT=Sbf[nA % 3][:, h, vc * 128:(vc + 1) * 128],
                                        rhs=qd[:, h, t0:t0 + 64], start=False, stop=False, skip_group_check=True))
                                k.mm(fns, reads=[vt_t[i], sT_t, Sbf_t[nA % 3]] + [qd_t[h][i] for h in range(4)], writes=[pO_t[bk]])
                            yield

                            def upd(cc):
                                n_ = gsb["n"]
                                tl = t0 + cc * 64 + 63
                                for h in range(4):
                                    bk = kvb[cc][h // 2]
                                    k.op(dve, lambda e: e.scalar_tensor_tensor(out=S32[:, h, :], in0=S32[:, h, :], scalar=eb[:, h, tl:tl + 1],
                                                                               in1=pb[bk][:, (h % 2) * 256:(h % 2 + 1) * 256], op0=ALU.mult, op1=ALU.add),
                                         reads=[pb_t[bk], eb_t[h][i]], writes=[S32_t[h]])
                                k.op(act, lambda e: e.activation(out=Sbf[(n_ + 1) % 3][:], in_=S32[:], func=AF.Copy),
                                     reads=S32_t, writes=[Sbf_t[(n_ + 1) % 3]])
                                gsb["n"] = n_ + 1

                            upd(0)
                            for bk in range(2):
                                fns = []
                                for c in range(bk * 4, bk * 4 + 4):
                                    h, vc = c // 2, c % 2
                                    blk = pO[bk][:, (c % 4) * 128:(c % 4 + 1) * 128]
                                    fns.append(lambda e, blk=blk, h=h, vc=vc, c=c: e.matmul(
                                        blk[:, 64:128], lhsT=Sbf[(nA + 1) % 3][:, h, vc * 128:(vc + 1) * 128],
                                        rhs=qd[:, h, t0 + 64:t0 + 128], start=False, stop=(c % 4 == 3), skip_group_check=True))
                                if bk == 0 and NWARM_CORE:
                                    fns[0] = (fns[0], [Sbf_t[(nA + 1) % 3]])
                                    warm = [lambda e: e.matmul(pb[7][:, 0:512], lhsT=kd[:, 0, t0:t0 + 128], rhs=qd[:, 0, 0:512], start=True, stop=True,
                                                               skip_group_check=True) for _ in range(NWARM_CORE)]
                                    k.mm(fns, reads=[qd_t[h][j] for h in range(4) for j in range(4)] + [kd_t[0][i]], writes=[pO_t[bk], pb_t[7]], warm=warm)
                                else:
                                    k.mm(fns, reads=[Sbf_t[(nA + 1) % 3]] + [qd_t[h][i] for h in range(4)], writes=[pO_t[bk]])
                            yield
                            rms_gate(i, t0, rows, pO, pO_t, after=lambda: upd(1))
                            yield

                        def core_tiles_gen(tl_):
                            for i_ in tl_:
                                yield from core_tile_gen(i_)

                        if not SPLIT:
                            run(core_tiles_gen([0, 1, 2, 3]))
                        else:
                            if nsamp:
                                linear_A(r_slab, r_slab_t, xb, xb_t, evac_r, ntl=halfB)
                                run(core_tiles_gen([0, 1]))
                            else:
                                run(linear_A_gen(r_slab, r_slab_t, xb, xb_t, evac_r, ntl=halfB, nsets=2), core_tiles_gen([0, 1]))
                            wo_slab, wo_slab_t = next_slab()
                            run(linear_A_gen(wo_slab, wo_slab_t, og, og_t, make_z_evac(), ntl=halfA, nsets=2), core_tiles_gen([2, 3]))
                        chk(8)

                        if nsamp:
                            i, t0, rows = 4, NPT, nsamp
                            k.scope_barrier([t_ for row in enb_t for t_ in row] + ec_t + e1_t + spt_t + vt_t[0:4])
                            enb_fl = enb[:, :, :].rearrange("p a n -> p (a n)")
                            ec_fl = ec[:, :, :].rearrange("p a n -> p (a n)")
                            vt_fl = vt[:, 0:4, :].rearrange("p a n -> p (a n)").bitcast(F32)
                            NB0 = 4
                            S0 = [enb_fl[:, j * 1024:(j + 1) * 1024].rearrange("p (h v) -> p h v", h=4) for j in range(2)] + \
                                 [vt_fl[:, j * 1024:(j + 1) * 1024].rearrange("p (h v) -> p h v", h=4) for j in range(2)]
                            Sn = [ec_fl[:, j * 1024:(j + 1) * 1024].rearrange("p (h v) -> p h v", h=4) for j in range(2)]
                            kkm = [e1[0][:, :].bitcast(BF16), spt[0][:, :].bitcast(BF16)]
                            S0_t, Sn_t, kkm_t = TS(4), TS(2), TS(2)
                            S0_ds = [k.dsem("d_S0_%d" % j) for j in range(4)]
                            Sn_ds = [k.dsem("d_Sn_%d" % j) for j in range(2)]
                            pS, pS_t = pb[2], pb_t[2]
                            k.mm([lambda e, h=h: e.matmul(pS[0:rows, h * 64:(h + 1) * 64], lhsT=kd[:, h, t0:t0 + rows], rhs=qd[:, h, t0:t0 + rows],
                                                          start=True, stop=True) for h in range(4)],
                                 reads=[kd_t[h][i] for h in range(4)] + [qd_t[h][i] for h in range(4)], writes=[pS_t])
                            k.op(dve, lambda e: e.tensor_tensor(out=sT[0:rows, :, 0:rows], in0=pS[0:rows, 0:4 * rows].rearrange("p (h j) -> p h j", h=4),
                                                                in1=cst[0:rows, C_CS4:C_CS4 + 4 * rows].rearrange("p (h j) -> p h j", h=4), op=ALU.mult),
                                 reads=[pS_t, c_t], writes=[sT_t])
                            pO1, pO1_t = pb[5], pb_t[5]
                            fns = []
                            for c in range(NCH):
                                h, vc = c // 2, c % 2
                                fns.append(lambda e, c=c, h=h, vc=vc: e.matmul(
                                    pO1[:, c * 64:(c + 1) * 64], lhsT=vt[0:rows, i, h * 256 + vc * 128:h * 256 + (vc + 1) * 128], rhs=sT[0:rows, h, 0:rows],
                                    start=(c == 0), stop=False, skip_group_check=True))
                            k.mm(fns, reads=[vt_t[i], sT_t], writes=[pO1_t])
                            for s_ in range(NB0):
                                k.dma(sp, S0_ds[s_], S0[s_], sg[s_].rearrange("h d v -> d h v"), writes=[S0_t[s_]])
                            def sample_seq_gen():
                                for s_ in range(NSEQ):
                                    b = s_ % 2
                                    b0 = s_ % NB0
                                    fns = []
                                    for c in range(NCH):
                                        h, vc = c // 2, c % 2
                                        fns.append(lambda e, c=c, h=h, vc=vc: e.matmul(
                                            pO1[:, c * 64 + 4 * s_:c * 64 + 4 * s_ + 4], lhsT=S0[b0][:, h, vc * 128:(vc + 1) * 128],
                                            rhs=qd32[:, h, 4 * s_:4 * s_ + 4], start=False, stop=(s_ == NSEQ - 1 and c == NCH - 1),
                                            skip_group_check=True))
                                    k.mm(fns, reads=[S0_t[b0], qd32_t], writes=[pO1_t])
                                    k.op(act, lambda e: e.activation(out=kkm[b][0:rows, 0:512], in_=kkt[0:rows, i, :], func=AF.Identity,
                                                                     scale=cst[0:rows, C_SEQ + s_:C_SEQ + s_ + 1]),
                                         reads=[kkt_t[i], c_t], writes=[kkm_t[b]])
                                    for hb in range(2):
                                        bk = 3 + hb
                                        k.mm([lambda e, h=h: e.matmul(pb[bk][:, (h % 2) * 256:(h % 2 + 1) * 256], lhsT=kkm[b][0:rows, h * 128:(h + 1) * 128],
                                                                      rhs=vt[0:rows, i, h * 256:(h + 1) * 256], start=True, stop=True)
                                              for h in (2 * hb, 2 * hb + 1)],
                                             reads=[kkm_t[b], vt_t[i]], writes=[pb_t[bk]])
                                    tl = t0 + 4 * s_ + 3
                                    for h in range(4):
                                        bk = 3 + h // 2
                                        k.op(dve, lambda e: e.scalar_tensor_tensor(out=Sn[b][:, h, :], in0=S0[b0][:, h, :], scalar=eb[:, h, tl:tl + 1],
                                                                                   in1=pb[bk][:, (h % 2) * 256:(h % 2 + 1) * 256], op0=ALU.mult, op1=ALU.add),
                                             reads=[pb_t[bk], eb_t[h][i], S0_t[b0]], writes=[Sn_t[b]])
                                    k.dma(pool, Sn_ds[b], gs[s_].rearrange("h d v -> d h v"), Sn[b], reads=[Sn_t[b]])
                                    if s_ + NB0 < NSEQ:
                                        k.dma(sp, S0_ds[b0], S0[b0], sg[s_ + NB0].rearrange("h d v -> d h v"), writes=[S0_t[b0]])
                                    yield

                            wo_slab, wo_slab_t = next_slab()
                            run(linear_A_gen(wo_slab, wo_slab_t, og, og_t, make_z_evac(), ntl=ntiles[0:1], nsets=2, nwarm=NWARM_WO), sample_seq_gen())
                            pov = pO1[:, :].rearrange("p (a b) -> p a b", a=8)
                            k.op(act, lambda e: e.activation(out=osq[:, :, 0:rows], in_=pov, func=AF.Square), reads=[pO1_t], writes=osq_t)
                            pr, pr_t = pb[7], pb_t[7]
                            k.mm([lambda e, h=h, vc=vc: e.matmul(pr[:, h * 128:h * 128 + rows], lhsT=ones_rms[:], rhs=osq[:, h * 2 + vc, 0:rows],
                                                                 start=(vc == 0), stop=(vc == 1)) for h in range(4) for vc in range(2)],
                                 reads=osq_t + [c_t], writes=[pr_t])
                            k.op(dve, lambda e: e.tensor_tensor(out=o32[:, :, 0:rows], in0=pov, in1=prm[:, 64:72].unsqueeze(2).broadcast_to([128, NCH, rows]), op=ALU.mult),
                                 reads=[pO1_t, c_t], writes=o32_t)
                            prv = pr[:, :].rearrange("p (a b) -> p a b", a=4)[:, :, 0:rows]
                            rsv = rsd[:, :].rearrange("p (a b) -> p a b", a=4)[:, :, 0:rows]
                            k.op(act, lambda e: e.activation(out=rsv, in_=prv, func=AF.Ln, bias=prm_col(97)), reads=[pr_t, c_t], writes=[rsd_t])
                            k.op(act, lambda e: e.activation(out=rsv, in_=rsv, func=AF.Exp, scale=-0.5), reads=[rsd_t], writes=[rsd_t])
                            k.op(dve, lambda e: e.tensor_tensor(out=otm[:, :, 0:rows].rearrange("p (h v) i -> p h v i", h=4),
                                                                in0=o32[:, :, 0:rows].rearrange("p (h v) i -> p h v i", h=4),
                                                                in1=rsv.unsqueeze(2).broadcast_to([128, 4, 2, rows]), op=ALU.mult),
                                 reads=o32_t + [rsd_t], writes=[otm_t])
                            k.op(pool, lambda e: e.tensor_tensor(out=og[:, :, t0:t0 + rows], in0=otm[:, :, 0:rows], in1=og[:, :, t0:t0 + rows], op=ALU.mult),
                                 reads=[otm_t], writes=[og_t[c][i] for c in range(NCH)])
                        if stn == NST - 1:
                            gp_ds = k.dsem("d_gp")
                            k.dma(sp, gp_ds, gp.rearrange("h d v -> d h v"), S32[:], reads=S32_t)
                if SPLIT:
                    run(linear_A_gen(wo_slab, wo_slab_t, og, og_t, make_z_evac(), ntl=halfB), layer_norm_gen(halfA, 0 * 8, 1 * 8))
                    pend = layer_norm_gen(halfB, 0 * 8, 1 * 8)
                else:
                    if nsamp:
                        linear_A(wo_slab, wo_slab_t, og, og_t, make_z_evac(), ntl=ntiles[1:])
                    else:
                        wo_slab, wo_slab_t = next_slab()
                        linear_A(wo_slab, wo_slab_t, og, og_t, make_z_evac(), nwarm=NWARM_WO)
                    pend = layer_norm_gen(ntiles, 0 * 8, 1 * 8)
                chk(9)
                pend = mlp_stage(0, pre=pend)
                chk(10)

            with Scope(k) as ss:
                yb = sb(ss, "yb", (128, NCH, NTMAX), BF16)
                yb_t = TS(NCH, 5)
                with Scope(k) as s2:
                    cgs = sb(s2, "cgs", (128, NCH, NTMAX), F32)
                    Ub = sb(s2, "Ub", (128, NCH, NPT + 2), F32)
                    Us = sb(s2, "Us", (128, NCH, NSEQ, 6), F32)
                    acc = [sb(s2, "acc", (128, NTMAX), F32) for _ in range(NCH)]
                    cgs_t = TS(NCH, 5)
                    Ub_t, Us_t = TS(NCH), TS(NCH)
                    acc_t = TS(NCH)
                    k.op(pool, lambda e: e.tensor_copy(out=Ub[:, :, 0:2], in_=tailb[:]), reads=[tail_t], writes=Ub_t)
                    if nsamp:
                        scin = sb(s2, "scin", (2 * NSEQ, D), F32)
                        scin_t = T()
                        sc_ds = k.dsem("d_scin")
                        k.dma(sp, sc_ds, scin[:], scv[:, :], writes=[scin_t])
                        for half in range(2):
                            pt, pt_t = misc_bank()
                            k.mm([lambda e, j=j: e.transpose(out=pt[:, j * 32:(j + 1) * 32], in_=scin[:, (half * 4 + j) * 128:(half * 4 + j + 1) * 128],
                                                             identity=cst[0:32, C_ID:C_ID + 32]) for j in range(4)],
                                 reads=[scin_t, c_t], writes=[pt_t])
                            src_ps = pt[:, 0:128].rearrange("p (a s r) -> p a s r", a=4, s=NSEQ)
                            k.op(dve, lambda e: e.tensor_copy(out=Us[:, half * 4:half * 4 + 4, :, 0:2], in_=src_ps),
                                 reads=[pt_t], writes=[Us_t[c] for c in range(half * 4, half * 4 + 4)])
                    slab, slab_t = next_slab()

                    def evac_cg(m, ni, n0, nn, ps, ps_t):
                        k.op(act, lambda e: e.activation(out=cgs[:, m, n0:n0 + nn], in_=ps[:, 0:nn], func=AF.Copy), reads=[ps_t], writes=xtr(cgs_t, [m], n0, nn))

                    if SPLIT:
                        run(linear_A_gen(slab, slab_t, xb, xb_t, evac_cg, ntl=halfA), pend)
                        linear_A(slab, slab_t, xb, xb_t, evac_cg, ntl=halfB)
                    else:
                        run(pend)
                        linear_after_ln(slab, slab_t, xb, xb_t, evac_cg)
                    pend = None
                    slab, slab_t = next_slab()

                    def evac_h(m, ni, n0, nn, ps, ps_t):
                        a_, a_t = acc[m], acc_t[m]
                        w0, w1, w2 = prm_col(72 + m), prm_col(80 + m), prm_col(88 + m)
                        if ni == 0:
                            k.op(dve, lambda e: e.tensor_tensor(out=Ub[:, m, 2:2 + NPT], in0=ps[:, 0:nn], in1=cgs[:, m, 0:NPT], op=ALU.mult),
                                 reads=[ps_t] + xtr(cgs_t, [m], n0, nn), writes=[Ub_t[m]])
                            k.op(act, lambda e: e.activation(out=a_[:, 0:NPT], in_=Ub[:, m, 2:2 + NPT], func=AF.Identity, scale=w2), reads=[Ub_t[m], c_t], writes=[a_t])
                            k.op(dve, lambda e: e.scalar_tensor_tensor(out=a_[:, 0:NPT], in0=Ub[:, m, 1:1 + NPT], scalar=w1, in1=a_[:, 0:NPT], op0=ALU.mult, op1=ALU.add),
                                 reads=[Ub_t[m], c_t, a_t], writes=[a_t])
                            k.op(dve, lambda e: e.scalar_tensor_tensor(out=a_[:, 0:NPT], in0=Ub[:, m, 0:NPT], scalar=w0, in1=a_[:, 0:NPT], op0=ALU.mult, op1=ALU.add),
                                 reads=[Ub_t[m], c_t, a_t], writes=[a_t])
                        else:
                            k.op(dve, lambda e: e.tensor_tensor(out=Us[:, m, :, 2:6], in0=ps[:, 0:nn].rearrange("p (s r) -> p s r", s=NSEQ),
                                                                in1=cgs[:, m, n0:n0 + nn].rearrange("p (s r) -> p s r", s=NSEQ), op=ALU.mult),
                                 reads=[ps_t] + xtr(cgs_t, [m], n0, nn), writes=[Us_t[m]])
                            av = a_[:, NPT:NPT + nsamp].rearrange("p (s r) -> p s r", s=NSEQ)
                            k.op(act, lambda e: e.activation(out=av, in_=Us[:, m, :, 2:6], func=AF.Identity, scale=w2), reads=[Us_t[m], c_t], writes=[a_t])
                            k.op(dve, lambda e: e.scalar_tensor_tensor(out=av, in0=Us[:, m, :, 1:5], scalar=w1, in1=av, op0=ALU.mult, op1=ALU.add),
                                 reads=[Us_t[m], c_t, a_t], writes=[a_t])
                            k.op(dve, lambda e: e.scalar_tensor_tensor(out=av, in0=Us[:, m, :, 0:4], scalar=w0, in1=av, op0=ALU.mult, op1=ALU.add),
                                 reads=[Us_t[m], c_t, a_t], writes=[a_t])

                    linear_A(slab, slab_t, xb, xb_t, evac_h)
                    slab, slab_t = next_slab()

                    def evac_bg(m, ni, n0, nn, ps, ps_t):
                        k.op(dve, lambda e: e.tensor_tensor(out=yb[:, m, n0:n0 + nn], in0=acc[m][:, n0:n0 + nn], in1=ps[:, 0:nn], op=ALU.mult),
                             reads=[ps_t, acc_t[m]], writes=xtr(yb_t, [m], n0, nn))

                    linear_A(slab, slab_t, xb, xb_t, evac_bg)
                    if stn < NST - 1:
                        k.op(pool, lambda e: e.tensor_copy(out=tailb[:], in_=Ub[:, :, NPT:NPT + 2]), reads=Ub_t, writes=[tail_t])
                    else:
                        cpo = sb(s2, "cpo", (2, D), F32)
                        cso = sb(s2, "cso", (2 * NSEQ, D), F32)
                        ustg = sb(s2, "ustg", (128, NCH, 2 * NSEQ), F32)
                        cpo_t, cso_t, ustg_t = T(), T(), T()
                        k.op(pool, lambda e: e.tensor_copy(out=ustg[:, :, :].rearrange("p c (s r) -> p c s r", s=NSEQ), in_=Us[:, :, :, 4:6]),
                             reads=Us_t, writes=[ustg_t])
                        for half in range(2):
                            pt, pt_t = misc_bank()
                            k.mm([lambda e, j=j: e.transpose(out=pt[0:2, j * 128:(j + 1) * 128], in_=Ub[:, half * 4 + j, NPT:NPT + 2], identity=ident)
                                  for j in range(4)], reads=Ub_t + [c_t], writes=[pt_t])
                            k.op(dve, lambda e: e.tensor_copy(out=cpo[:, half * 512:(half + 1) * 512], in_=pt[0:2, :]), reads=[pt_t], writes=[cpo_t])
                            pt2, pt2_t = misc_bank()
                            k.mm([lambda e, j=j: e.transpose(out=pt2[0:32, j * 128:(j + 1) * 128], in_=ustg[:, half * 4 + j, :], identity=ident)
                                  for j in range(4)], reads=[ustg_t, c_t], writes=[pt2_t])
                            k.op(dve, lambda e: e.tensor_copy(out=cso[:, half * 512:(half + 1) * 512], in_=pt2[0:32, :]), reads=[pt2_t], writes=[cso_t])
                        co_ds = k.dsem("d_co")
                        co2_ds = k.dsem("d_co2")
                        k.dma(sp, co_ds, cp[:, :], cpo[:], reads=[cpo_t])
                        k.dma(sp, co2_ds, cs[:, :], cso[:], reads=[cso_t])
                slab, slab_t = next_slab()
                if SPLIT:
                    linear_A(slab, slab_t, yb, yb_t, make_z_evac(), ntl=halfA)
                    run(linear_A_gen(slab, slab_t, yb, yb_t, make_z_evac(), ntl=halfB), layer_norm_gen(halfA, 4 * 8, 5 * 8))
                    pend = layer_norm_gen(halfB, 4 * 8, 5 * 8)
                else:
                    linear_A(slab, slab_t, yb, yb_t, make_z_evac())
                    pend = layer_norm_gen(ntiles, 4 * 8, 5 * 8)
                chk(11)
                pend = mlp_stage(1, final=True, pre=pend)
                chk(12)

            with Scope(k) as ss:
                nxt = [s_ for s_ in sts if s_ > stn]
                if nxt:
                    prefetch_x(nxt[0])
                yo = sb(ss, "yo", (128, 5, D), F32)
                yo_t = TS(5)
                yo_ds = [k.dsem("d_yo%d_%d" % (stn, i)) for i in range(5)]
                run(pend)
                pend = None
                for c in range(NCH):
                    pt, pt_t = misc_bank()
                    k.mm([lambda e, i=i: e.transpose(out=pt[:, i * 128:(i + 1) * 128], in_=xf[:, c, i * 128:(i + 1) * 128], identity=ident)
                          for i in range(4)], reads=[xf_t[c][i] for i in range(4)] + [c_t], writes=[pt_t])
                    ptv = pt[:, :].rearrange("p (a b) -> p a b", a=4)
                    if c % 2 == 0:
                        k.op(dve, lambda e: e.tensor_copy(out=yo[:, 0:4, c * 128:(c + 1) * 128], in_=ptv), reads=[pt_t], writes=yo_t[0:4])
                    else:
                        k.op(act, lambda e: e.activation(out=yo[:, 0:4, c * 128:(c + 1) * 128], in_=ptv, func=AF.Copy), reads=[pt_t], writes=yo_t[0:4])
                    if nsamp:
                        pt2, pt2_t = misc_bank()
                        k.mm([lambda e: e.transpose(out=pt2[0:nsamp, 0:128], in_=xf[:, c, NPT:NPT + nsamp], identity=ident)],
                             reads=[xf_t[c][4], c_t], writes=[pt2_t])
                        k.op(dve, lambda e: e.tensor_copy(out=yo[0:nsamp, 4, c * 128:(c + 1) * 128], in_=pt2[0:nsamp, 0:128]), reads=[pt2_t], writes=[yo_t[4]])
                nxt = [s_ for s_ in sts if s_ > stn]
                if nxt:
                    stage_X(nxt[0])
                for i, (t0, rows) in enumerate(ttiles):
                    dst = yp[stn * NPT + t0: stn * NPT + t0 + rows, :] if i < 4 else ys[0:rows, :]
                    k.dma(sp, yo_ds[i], dst, yo[0:rows, i, :], reads=[yo_t[i]])

          except _Stop:
            break

        for d in k.dsems:
            if d.n > 0:
                sp.e.wait_ge(d.sem, d.n)
    return nc


def _consts():
    cst = np.zeros((128, C_TOT), np.float32)
    cst[:, C_ID:C_ID + 128] = np.eye(128, dtype=np.float32)
    s = np.arange(128)[:, None]
    t = np.arange(128)[None, :]
    same = (s // 64) == (t // 64)
    cst[:, C_UP:C_UP + 128] = np.where(same & (s <= t), -1.0 / 16.0, 0.0)
    cst[:, C_M2P:C_M2P + 128] = np.where(same & (s > t), -1.0 / 16.0, 0.0)
    cst[:, C_CP:C_CP + 128] = np.where(same & (s <= t), 1.0, 0.0)
    same4 = ((s // 4) == (t // 4)) & (s < 64) & (t < 64)
    cst[:, C_US:C_US + 128] = np.where(same4 & (s <= t), -1.0 / 16.0, 0.0)
    cst[:, C_M2S:C_M2S + 128] = np.where(same4 & (s > t), -1.0 / 16.0, 0.0)
    cst[:, C_CS:C_CS + 128] = np.where(same4 & (s <= t), 1.0, 0.0)
    tt = np.arange(128)[:, None]
    ss = np.arange(16)[None, :]
    cst[:, C_SEQ:C_SEQ + 16] = np.where(((tt // 4) == ss) & (tt < 64), 1.0, 0.0)
    for h in range(4):
        cst[:, C_CP4 + h * 128:C_CP4 + (h + 1) * 128] = cst[:, C_CP:C_CP + 128]
        cst[:, C_CS4 + h * 64:C_CS4 + (h + 1) * 64] = cst[:, C_CS:C_CS + 64]
    return cst


def _fm(v):
    return np.ascontiguousarray(np.asarray(v, np.float32).reshape(8, 128).T)


_NC_CACHE = {}


def kernel(x_prompt, x_sample, state_gla, state_conv, gla_w_in, gla_w_gate_up, gla_b_gate, gla_norm_g,
           gla_w_o, conv_w_in, conv_w_conv, conv_w_out, mlp_w_up, mlp_w_down, ln1_g, ln1_b, ln2_g, ln2_b):
    f = lambda a: np.ascontiguousarray(np.asarray(a, dtype=np.float32))
    prm = np.zeros((128, 98), np.float32)
    prm[:, 96] = LN_EPS
    prm[:, 97] = RMS_EPS
    for l in range(2):
        for w_, arr in enumerate((ln1_g, ln1_b, ln2_g, ln2_b)):
            prm[:, (l * 4 + w_) * 8:(l * 4 + w_ + 1) * 8] = _fm(np.asarray(arr)[l])
    prm[:, 64:72] = _fm(np.asarray(gla_norm_g)[0].reshape(-1))
    for j in range(3):
        prm[:, 72 + j * 8:72 + (j + 1) * 8] = _fm(np.asarray(conv_w_conv)[0, j])
    cst = _consts()
    shared = {
        "gla_w_in": f(gla_w_in[0]), "gla_w_gate_up": f(gla_w_gate_up[0]), "gla_b_gate": f(gla_b_gate[0]).reshape(1, 512),
        "gla_w_o": f(gla_w_o[0]), "conv_w_in": f(conv_w_in[0]), "conv_w_out": f(conv_w_out[0]),
        "mlp_w_up": f(mlp_w_up), "mlp_w_down": f(mlp_w_down), "prm": prm, "cst": cst,
    }
    xpr = f(x_prompt)
    xsm = f(x_sample)
    sgl = f(state_gla)
    scn = f(state_conv)
    in_maps = []
    for c in range(8):
        m = dict(shared)
        m["xp"] = xpr[c]
        m["xs"] = xsm[16 * c:16 * c + 16].reshape(64, D)
        m["sg"] = sgl[0, 16 * c:16 * c + 16]
        m["sc"] = scn[0, 16 * c:16 * c + 16].reshape(32, D)
        in_maps.append(m)
    if "nc" not in _NC_CACHE:
        _NC_CACHE["nc"] = build_program()
    nc = _NC_CACHE["nc"]
    res = run_bass_kernel_spmd(nc, in_maps, core_ids=list(range(8)))
    r = res.results
    y_prompt = np.stack([r[c]["yp"] for c in range(8)], 0).astype(np.float32)
    y_sample = np.concatenate([r[c]["ys"].reshape(16, 4, D) for c in range(8)], 0).astype(np.float32)
    gla_p = np.stack([r[c]["gp"] for c in range(8)], 0)[None].astype(np.float32)
    gla_s = np.concatenate([r[c]["gs"] for c in range(8)], 0)[None].astype(np.float32)
    conv_p = np.stack([r[c]["cp"] for c in range(8)], 0)[None].astype(np.float32)
    conv_s = np.concatenate([r[c]["cs"].reshape(16, 2, D) for c in range(8)], 0)[None].astype(np.float32)
    return (y_prompt, y_sample, gla_p, gla_s, conv_p, conv_s)
```

```python
import os
import numpy as np
from contextlib import ExitStack
import concourse.bass as bass
import concourse.mybir as mybir
from concourse.bass_utils import run_bass_kernel_spmd

F32 = mybir.dt.float32
BF16 = mybir.dt.bfloat16
AF = mybir.ActivationFunctionType
ALU = mybir.AluOpType

D = 1024
NCH = 8
NPT = 512
NSAMP = 64
NSEQ = 16
NST = 4
NTMAX = NPT + NSAMP
ALPHA = 4.0 ** 0.25
LN_EPS = 1e-5
RMS_EPS = 1e-6
QSCALE = 128.0 ** -0.5
NSLOT = 3
NWARM_LN = int(os.environ.get("KWLN", "12"))
NWARM_WO = int(os.environ.get("KWWO", "14"))
NWARM_CORE = int(os.environ.get("KWCORE", "0"))
SPLIT = 0

C_ID, C_UP, C_M2P, C_CP, C_US, C_M2S, C_CS, C_SEQ = 0, 128, 256, 384, 512, 640, 768, 896
C_CP4 = 912
C_CS4 = 1424
C_TOT = 1680


LEVEL = int(os.environ.get("KLEVEL", "99"))


class _Stop(Exception):
    pass


def chk(l):
    if LEVEL < l:
        raise _Stop()


class Own:
    def __init__(self, sem):
        self.sem = sem
        self.n = 0


class Eng(Own):
    def __init__(self, e, sem, is_pe=False):
        super().__init__(sem)
        self.e = e
        self.seen = {}
        self.is_pe = is_pe


SCOPES = []


class T:
    __slots__ = ("w", "r", "excl")

    def __init__(self):
        self.w = None
        self.r = {}
        self.excl = False
        if SCOPES:
            SCOPES[-1].append(self)


class Scope(ExitStack):
    def __init__(self, k):
        super().__init__()
        self.k = k

    def __enter__(self):
        SCOPES.append([])
        return super().__enter__()

    def __exit__(self, *a):
        lst = SCOPES.pop()
        if a[0] is None:
            self.k.scope_barrier(lst)
        return super().__exit__(*a)


def run(*gens):
    gens = [g for g in gens if g is not None]
    while gens:
        for g in list(gens):
            try:
                next(g)
            except StopIteration:
                gens.remove(g)


def TS(*shape):
    if len(shape) == 1:
        return [T() for _ in range(shape[0])]
    return [TS(*shape[1:]) for _ in range(shape[0])]


class K:
    def __init__(self, nc, st):
        self.nc = nc
        self.st = st
        self._nm = 0
        self.pe = Eng(nc.tensor, self.sem("s_pe"), is_pe=True)
        self.act = Eng(nc.scalar, self.sem("s_act"))
        self.dve = Eng(nc.vector, self.sem("s_dve"))
        self.pool = Eng(nc.gpsimd, self.sem("s_pool"))
        self.sp = Eng(nc.sync, self.sem("s_sp"))
        self.engs = [self.pe, self.act, self.dve, self.pool, self.sp]
        self.dsems = []

    def sem(self, name):
        return self.st.enter_context(self.nc.semaphore(name))

    def dsem(self, name, bar=True):
        d = Own(self.sem(name))
        d.bar = bar
        self.dsems.append(d)
        return d

    def uname(self, base):
        self._nm += 1
        return "%s_%d" % (base, self._nm)

    def _wait(self, E, tok):
        if tok is None:
            return
        o, c = tok
        if o is E and E.is_pe:
            return
        if E.seen.get(o, 0) >= c:
            return
        E.e.wait_ge(o.sem, c)
        E.seen[o] = c

    def _deps(self, E, reads, writes):
        ex = [t for t in reads if t.excl]
        if ex:
            reads = [t for t in reads if not t.excl]
            writes = list(writes) + ex
        for t in reads:
            self._wait(E, t.w)
        for t in writes:
            self._wait(E, t.w)
            for o, c in list(t.r.items()):
                self._wait(E, (o, c))

    def _commit(self, tok, reads, writes):
        o, c = tok
        ex = [t for t in reads if t.excl]
        if ex:
            reads = [t for t in reads if not t.excl]
            writes = list(writes) + ex
        for t in reads:
            if t.r.get(o, 0) < c:
                t.r[o] = c
        for t in writes:
            t.w = tok
            t.r = {}

    def op(self, E, fn, reads=(), writes=()):
        self._deps(E, reads, writes)
        ins = fn(E.e)
        E.n += 1
        ins.then_inc(E.sem, 1)
        tok = (E, E.n)
        self._commit(tok, reads, writes)
        return tok

    def mm(self, fns, reads=(), writes=(), warm=()):
        E = self.pe
        self._deps(E, reads, writes)
        for f in warm:
            f(E.e)
        ins = None
        late = []
        for f in fns:
            if isinstance(f, tuple):
                f, lr = f
                for t in lr:
                    self._wait(E, t.w)
                late += lr
            ins = f(E.e)
        E.n += 1
        ins.then_inc(E.sem, 1)
        tok = (E, E.n)
        self._commit(tok, list(reads) + late, writes)
        return tok

    def dma(self, Q, ds, out, in_, reads=(), writes=()):
        self._deps(Q, reads, writes)
        Q.e.dma_start(out=out, in_=in_).then_inc(ds.sem, 16)
        ds.n += 16
        tok = (ds, ds.n)
        self._commit(tok, reads, writes)
        return tok

    def scope_barrier(self, lst):
        for E in self.engs:
            for t in lst:
                self._wait(E, t.w)
                for o, c in list(t.r.items()):
                    self._wait(E, (o, c))

    def barrier(self):
        comp = [self.pe, self.act, self.dve, self.pool]
        for E in [self.pe, self.act, self.dve, self.pool]:
            for P in comp:
                if P.n > 0:
                    self._wait(E, (P, P.n))
            for d in self.dsems:
                if d.n > 0 and d.bar:
                    self._wait(E, (d, d.n))


def build_program(sts=tuple(range(NST))):
    del SCOPES[:]
    nc = bass.Bass("TRN2", target_bir_lowering=False)
    din = lambda name, shape: nc.dram_tensor(name, list(shape), F32, kind="ExternalInput").ap()
    dout = lambda name, shape: nc.dram_tensor(name, list(shape), F32, kind="ExternalOutput").ap()
    xp = din("xp", (2048, D))
    xs = din("xs", (NSAMP, D))
    sg = din("sg", (NSEQ, 4, 128, 256))
    scv = din("sc", (2 * NSEQ, D))
    w_in0 = din("gla_w_in", (D, 3088))
    w_gu = din("gla_w_gate_up", (16, 512))
    b_gt = din("gla_b_gate", (1, 512))
    w_o0 = din("gla_w_o", (D, D))
    cw_in = din("conv_w_in", (D, 3 * D))
    cw_out = din("conv_w_out", (D, D))
    w_up = din("mlp_w_up", (2, D, 4 * D))
    w_dn = din("mlp_w_down", (2, 4 * D, D))
    prm_d = din("prm", (128, 98))
    cst_d = din("cst", (128, C_TOT))
    yp = dout("yp", (2048, D))
    ys = dout("ys", (NSAMP, D))
    gp = dout("gp", (4, 128, 256))
    gs = dout("gs", (NSEQ, 4, 128, 256))
    cp = dout("cp", (2, D))
    cs = dout("cs", (2 * NSEQ, D))

    with ExitStack() as st:
        k = K(nc, st)
        pe, act, dve, pool, sp = k.pe, k.act, k.dve, k.pool, k.sp

        def sb(stack, name, shape, dt):
            return stack.enter_context(nc.sbuf_tensor(k.uname(name), list(shape), dt))

        xf = sb(st, "xf", (128, NCH, NTMAX), F32)
        xb = sb(st, "xb", (128, NCH, NTMAX), BF16)
        ring = sb(st, "ring", (128, NSLOT, NCH, 1024), BF16)
        glw = sb(st, "glw", (128, NCH, 16), BF16)
        wg = sb(st, "wg", (17, 512), F32)
        bgr = sb(st, "bgr", (1, 512), F32)
        onesr = sb(st, "onesr", (1, 128), F32)
        prm = sb(st, "prm", (128, 98), F32)
        cst = sb(st, "cst", (128, C_TOT), F32)
        ones_ln = sb(st, "ones_ln", (128, 128), BF16)
        ones_rms = sb(st, "ones_rms", (128, 128), BF16)
        zsq = sb(st, "zsq", (128, NCH, NTMAX), BF16)
        ln_mean = sb(st, "ln_mean", (128, NTMAX), F32)
        ln_var = sb(st, "ln_var", (128, NTMAX), F32)
        ln_rstd = sb(st, "ln_rstd", (128, NTMAX), F32)
        NTMP = 2
        ln_tmp = [sb(st, "ln_tmp", (128, NTMAX), F32) for _ in range(NTMP)]
        S32 = sb(st, "S32", (128, 4, 256), F32)
        Sbf = [sb(st, "Sbf", (128, 4, 256), BF16) for _ in range(3)]
        tailb = sb(st, "tailb", (128, NCH, 2), F32)
        xf_t = TS(NCH, 5)
        xb_t = TS(NCH, 5)
        ring_t = TS(NSLOT)
        ring_ds = [k.dsem("d_ring%d" % i, bar=False) for i in range(NSLOT)]
        c_t = T()
        c_ds = k.dsem("d_cst")
        S32_t = TS(4)
        Sbf_t = TS(3)
        gsb = {"n": 0}
        tail_t = T()
        zsq_t = TS(NCH, 5)
        mean_t, var_t, rstd_t = T(), T(), T()
        tmp_t = TS(NTMP)

        pb = [st.enter_context(nc.psum_tensor("pb%d" % i, [128, 512], F32)) for i in range(8)]
        pb_t = TS(8)
        for t_ in pb_t:
            t_.excl = True

        ident = cst[:, C_ID:C_ID + 128]

        def prm_col(idx):
            return prm[:, idx:idx + 1]

        k.dma(sp, c_ds, cst[:], cst_d[:, :], writes=[c_t])
        k.dma(sp, c_ds, prm[:], prm_d[:, :], writes=[c_t])
        k.dma(sp, c_ds, wg[0:16, :], w_gu[:, :], writes=[c_t])
        k.dma(sp, c_ds, wg[16:17, :], b_gt[:, :], writes=[c_t])
        c2_ds = k.dsem("d_cst2")
        k.dma(pool, c2_ds, glw[:], w_in0[:, 3072:3088].rearrange("(k p) n -> p k n", p=128), writes=[c_t])
        k.op(dve, lambda e: e.memset(onesr[:], 1.0), writes=[c_t])
        k.op(dve, lambda e: e.memset(ones_ln[:], 1.0 / 1024.0), writes=[c_t])
        k.op(dve, lambda e: e.memset(ones_rms[:], 1.0 / 256.0), writes=[c_t])
        k.op(dve, lambda e: e.memset(S32[:], 0.0), writes=S32_t)
        k.op(dve, lambda e: e.memset(Sbf[0][:], 0.0), writes=[Sbf_t[0]])
        k.op(dve, lambda e: e.memset(tailb[:], 0.0), writes=[tail_t])
        k.barrier()

        def slab_list():
            L = []
            for s_ in sts:
                L += [w_in0[:, 1024:2048], w_in0[:, 0:1024], w_in0[:, 2048:3072], w_o0[:, :]]
                for q in range(4):
                    L += [w_up[0, :, q * 1024:(q + 1) * 1024], w_dn[0, q * 1024:(q + 1) * 1024, :]]
                L += [cw_in[:, 1024:2048], cw_in[:, 2048:3072], cw_in[:, 0:1024], cw_out[:, :]]
                for q in range(4):
                    L += [w_up[1, :, q * 1024:(q + 1) * 1024], w_dn[1, q * 1024:(q + 1) * 1024, :]]
            return L

        slabs = slab_list()
        sl = {"next_load": 0, "next_use": 0}

        def load_next():
            j = sl["next_load"]
            if j >= len(slabs):
                return
            s_ = j % NSLOT
            k.dma(pool, ring_ds[s_], ring[:, s_], slabs[j].rearrange("(k p) n -> p k n", p=128), writes=[ring_t[s_]])
            sl["next_load"] = j + 1

        def next_slab():
            j = sl["next_use"]
            while sl["next_load"] < min(j + NSLOT, len(slabs)):
                load_next()
            sl["next_use"] = j + 1
            s_ = j % NSLOT
            return ring[:, s_], ring_t[s_]

        misc_rr = {"i": 0}

        def misc_bank():
            i = 4 + (misc_rr["i"] % 4)
            misc_rr["i"] += 1
            return pb[i], pb_t[i]

        xin = [sb(st, "xin", (128, D), F32) for _ in range(2)]
        xin_t = TS(2)
        xin_ds = [k.dsem("d_xin%d" % i) for i in range(2)]
        x_pref = {}

        def x_load(stn_, i):
            nsamp_ = NSAMP if stn_ == NST - 1 else 0
            t0, rows = (i * 128, 128) if i < 4 else (NPT, nsamp_)
            src = xp[stn_ * NPT + t0: stn_ * NPT + t0 + rows, :] if i < 4 else xs[0:rows, :]
            k.dma(sp, xin_ds[i % 2], xin[i % 2][0:rows, :], src, writes=[xin_t[i % 2]])
            x_pref[(stn_, i)] = True

        def prefetch_x(stn_):
            for i in range(2):
                if (stn_, i) not in x_pref:
                    x_load(stn_, i)

        def stage_X(stn_):
            nsamp_ = NSAMP if stn_ == NST - 1 else 0
            ttiles_ = [(i * 128, 128) for i in range(4)] + ([(NPT, nsamp_)] if nsamp_ else [])
            if True:
                for i, (t0, rows) in enumerate(ttiles_):
                    b = i % 2
                    if (stn_, i) not in x_pref:
                        x_load(stn_, i)
                    for half in range(2):
                        pt, pt_t = misc_bank()
                        k.mm([lambda e, j=j: e.transpose(out=pt[:, j * 128:j * 128 + rows],
                                                         in_=xin[b][0:rows, (half * 4 + j) * 128:(half * 4 + j + 1) * 128],
                                                         identity=cst[0:rows, C_ID:C_ID + rows]) for j in range(4)],
                             reads=[xin_t[b], c_t], writes=[pt_t])
                        cs_ = range(half * 4, half * 4 + 4)
                        src_ps = pt[:, :].rearrange("p (a b) -> p a b", a=4)[:, :, 0:rows]
                        if half == 0:
                            k.op(act, lambda e: e.activation(out=xf[:, half * 4:half * 4 + 4, t0:t0 + rows], in_=src_ps, func=AF.Copy),
                                 reads=[pt_t], writes=[xf_t[c][i] for c in cs_])
                            k.op(dve, lambda e: e.tensor_copy(out=xb[:, half * 4:half * 4 + 4, t0:t0 + rows], in_=xf[:, half * 4:half * 4 + 4, t0:t0 + rows]),
                                 reads=[xf_t[c][i] for c in cs_], writes=[xb_t[c][i] for c in cs_])
                        else:
                            k.op(dve, lambda e: e.tensor_copy(out=xf[:, half * 4:half * 4 + 4, t0:t0 + rows], in_=src_ps),
                                 reads=[pt_t], writes=[xf_t[c][i] for c in cs_])
                            k.op(act, lambda e: e.activation(out=xb[:, half * 4:half * 4 + 4, t0:t0 + rows], in_=xf[:, half * 4:half * 4 + 4, t0:t0 + rows], func=AF.Copy),
                                 reads=[xf_t[c][i] for c in cs_], writes=[xb_t[c][i] for c in cs_])

        pend = None
        for stn in sts:
          try:
            nsamp = NSAMP if stn == NST - 1 else 0
            NT = NPT + nsamp
            ntiles = [(0, NPT)] + ([(NPT, nsamp)] if nsamp else [])
            ttiles = [(i * 128, 128) for i in range(4)] + ([(NPT, nsamp)] if nsamp else [])
            if SPLIT:
                halfA = [(0, 256)]
                halfB = [(256, 256)] + ([(NPT, nsamp)] if nsamp else [])
            else:
                halfA, halfB = ntiles, []

            def tiles_of(n0, nn):
                return [i for i, (t0_, r_) in enumerate(ttiles) if n0 <= t0_ < n0 + nn]

            def xtr(tr, cs_, n0, nn):
                return [tr[c][i] for c in cs_ for i in tiles_of(n0, nn)]

            def linear_A_gen(slab, slab_t, src, src_tr, evac, ntl=None, nsets=None, nwarm=0):
                ntl = ntiles if ntl is None else ntl
                if not ntl:
                    return
                if nsets is None:
                    nsets = 4 if len(ntl) == 1 else 2
                for m in range(NCH):
                    set_ = m % nsets
                    banks = [(pb[set_ + 2 * ni], pb_t[set_ + 2 * ni]) for ni in range(len(ntl))]
                    fns = []
                    for kk_ in range(NCH):
                        for ni, (n0, nn) in enumerate(ntl):
                            fns.append((lambda e, kk_=kk_, ni=ni, n0=n0, nn=nn, m=m: e.matmul(
                                banks[ni][0][:, 0:nn], lhsT=slab[:, kk_, m * 128:(m + 1) * 128],
                                rhs=src[:, kk_, n0:n0 + nn], start=(kk_ == 0), stop=(kk_ == NCH - 1)),
                                [src_tr[kk_][i] for i in tiles_of(n0, nn)]))
                    warm = []
                    if m == 0 and nwarm:
                        warm = [lambda e: e.matmul(banks[0][0][:, 0:512], lhsT=slab[:, 0, 0:128], rhs=slab[:, 1, 0:512], start=True, stop=True,
                                                   skip_group_check=True) for _ in range(nwarm)]
                    k.mm(fns, reads=[slab_t], writes=[b[1] for b in banks], warm=warm)
                    for ni, (n0, nn) in enumerate(ntl):
                        evac(m, ni, n0, nn, banks[ni][0], banks[ni][1])
                    yield

            def linear_A(*a, **kw):
                for _ in linear_A_gen(*a, **kw):
                    pass

            def linear_after_ln(slab, slab_t, src, src_tr, evac):
                n0, nn = ntiles[0]
                til = tiles_of(n0, nn)
                for kk_ in range(NCH):
                    warm = []
                    if kk_ == 0:
                        warm = [lambda e, j=j: e.matmul(pb[j % NCH][:, 0:512], lhsT=slab[:, 0, 0:128], rhs=slab[:, 1, 0:512], start=True, stop=True,
                                                        skip_group_check=True) for j in range(NWARM_LN)]
                    k.mm([((lambda e, kk_=kk_, m=m: e.matmul(pb[m][:, 0:nn], lhsT=slab[:, kk_, m * 128:(m + 1) * 128], rhs=src[:, kk_, n0:n0 + nn],
                                                           start=(kk_ == 0), stop=(kk_ == NCH - 1), skip_group_check=True)),
                           ([src_tr[kk_][i] for i in til] if m == 0 else [])) for m in range(NCH)],
                         reads=[slab_t], writes=pb_t, warm=warm)
                for m in range(NCH):
                    evac(m, 0, n0, nn, pb[m], pb_t[m])
                if len(ntiles) > 1:
                    linear_A(slab, slab_t, src, src_tr, evac, ntl=ntiles[1:])

            def make_z_evac(first=True, last=True):
                def evac(m, ni, n0, nn, ps, ps_t):
                    xt = xtr(xf_t, [m], n0, nn)
                    if first:
                        k.op(dve, lambda e: e.scalar_tensor_tensor(out=xf[:, m, n0:n0 + nn], in0=xf[:, m, n0:n0 + nn], scalar=ALPHA,
                                                                   in1=ps[:, 0:nn], op0=ALU.mult, op1=ALU.add),
                             reads=[ps_t], writes=xt)
                    else:
                        k.op(dve, lambda e: e.tensor_tensor(out=xf[:, m, n0:n0 + nn], in0=xf[:, m, n0:n0 + nn], in1=ps[:, 0:nn], op=ALU.add),
                             reads=[ps_t], writes=xt)
                    if last:
                        k.op(act, lambda e: e.activation(out=xb[:, m, n0:n0 + nn], in_=xf[:, m, n0:n0 + nn], func=AF.Copy),
                             reads=xt, writes=xtr(xb_t, [m], n0, nn))
                        k.op(act, lambda e: e.activation(out=zsq[:, m, n0:n0 + nn], in_=xf[:, m, n0:n0 + nn], func=AF.Square),
                             reads=xt, writes=xtr(zsq_t, [m], n0, nn))
                return evac

            def layer_norm_gen(ntl, gcol, bcol, want_bf=True):
                mean, var, rstd, tmp = ln_mean, ln_var, ln_rstd, ln_tmp
                if not ntl:
                    return
                a0 = ntl[0][0]
                tot = sum(nn for _, nn in ntl)
                for ni, (n0, nn) in enumerate(ntl):
                    o0 = n0 - a0
                    pm, pm_t = misc_bank()
                    pq, pq_t = misc_bank()
                    k.mm([(lambda e, c=c: e.matmul(pm[:, 0:nn], lhsT=ones_ln[:], rhs=xb[:, c, n0:n0 + nn], start=(c == 0), stop=(c == NCH - 1)),
                           xtr(xb_t, [c], n0, nn)) for c in range(NCH)], reads=[c_t], writes=[pm_t])
                    k.mm([(lambda e, c=c: e.matmul(pq[:, 0:nn], lhsT=ones_ln[:], rhs=zsq[:, c, n0:n0 + nn], start=(c == 0), stop=(c == NCH - 1)),
                           xtr(zsq_t, [c], n0, nn)) for c in range(NCH)], reads=[c_t], writes=[pq_t])
                    k.op(act, lambda e: e.activation(out=var[:, o0:o0 + nn], in_=pm[:, 0:nn], func=AF.Square), reads=[pm_t], writes=[var_t])
                    k.op(dve, lambda e: e.tensor_tensor(out=var[:, o0:o0 + nn], in0=pq[:, 0:nn], in1=var[:, o0:o0 + nn], op=ALU.subtract),
                         reads=[pq_t, var_t], writes=[var_t])
                    k.op(dve, lambda e: e.tensor_copy(out=mean[:, o0:o0 + nn], in_=pm[:, 0:nn]), reads=[pm_t], writes=[mean_t])
                n0, nn = a0, tot
                k.op(act, lambda e: e.activation(out=rstd[:, 0:nn], in_=var[:, 0:nn], func=AF.Ln, bias=prm_col(96)),
                     reads=[var_t, c_t], writes=[rstd_t])
                k.op(act, lambda e: e.activation(out=rstd[:, 0:nn], in_=rstd[:, 0:nn], func=AF.Exp, scale=-0.5), reads=[rstd_t], writes=[rstd_t])
                yield
                for c in range(NCH):
                    tb, tb_t = tmp[c % NTMP], tmp_t[c % NTMP]
                    xt = xtr(xf_t, [c], n0, nn)
                    k.op(dve, lambda e: e.tensor_tensor(out=tb[:, 0:nn], in0=xf[:, c, n0:n0 + nn], in1=mean[:, 0:nn], op=ALU.subtract),
                         reads=xt + [mean_t], writes=[tb_t])
                    k.op(dve, lambda e: e.scalar_tensor_tensor(out=tb[:, 0:nn], in0=tb[:, 0:nn], scalar=prm_col(gcol + c), in1=rstd[:, 0:nn],
                                                               op0=ALU.mult, op1=ALU.mult),
                         reads=[rstd_t, tb_t, c_t], writes=[tb_t])
                    k.op(act, lambda e: e.activation(out=xf[:, c, n0:n0 + nn], in_=tb[:, 0:nn], func=AF.Identity, bias=prm_col(bcol + c)),
                         reads=[tb_t, c_t], writes=xt)
                    if want_bf:
                        k.op(act, lambda e: e.activation(out=xb[:, c, n0:n0 + nn], in_=tb[:, 0:nn], func=AF.Identity, bias=prm_col(bcol + c)),
                             reads=[tb_t, c_t], writes=xtr(xb_t, [c], n0, nn))
                    yield

            def mlp_stage(layer, final=False, pre=None):
                g2, b2 = (layer * 4 + 2) * 8, (layer * 4 + 3) * 8
                with Scope(k) as ms:
                    hq = [sb(ms, "hq", (128, NCH, NTMAX), BF16) for _ in range(2)]
                    hq_t = [TS(NCH, 5) for _ in range(2)]
                    sqt = [sb(ms, "sqt", (128, 512), F32) for _ in range(2)]
                    sqt_t = TS(2)
                    cnt = {"i": 0}
                    for q in range(4):
                        hb, hb_t = hq[q % 2], hq_t[q % 2]
                        slab, slab_t = next_slab()

                        def evac_up(m, ni, n0, nn, ps, ps_t, hb=hb, hb_t=hb_t):
                            j = cnt["i"] % 2
                            cnt["i"] += 1
                            k.op(act, lambda e: e.activation(out=sqt[j][:, 0:nn], in_=ps[:, 0:nn], func=AF.Square), reads=[ps_t], writes=[sqt_t[j]])
                            k.op(dve, lambda e: e.scalar_tensor_tensor(out=hb[:, m, n0:n0 + nn], in0=ps[:, 0:nn], scalar=0.0, in1=sqt[j][:, 0:nn],
                                                                       op0=ALU.is_gt, op1=ALU.mult),
                                 reads=[ps_t, sqt_t[j]], writes=[hb_t[m][i] for i in tiles_of(n0, nn)])

                        if q == 0 and SPLIT:
                            run(linear_A_gen(slab, slab_t, xb, xb_t, evac_up, ntl=halfA), pre)
                            linear_A(slab, slab_t, xb, xb_t, evac_up, ntl=halfB)
                        elif q == 0:
                            run(pre)
                            linear_after_ln(slab, slab_t, xb, xb_t, evac_up)
                        else:
                            linear_A(slab, slab_t, xb, xb_t, evac_up)
                        slab, slab_t = next_slab()
                        if q < 3 or not SPLIT:
                            linear_A(slab, slab_t, hb, hb_t, make_z_evac(first=(q == 0), last=(q == 3)))
                        else:
                            linear_A(slab, slab_t, hb, hb_t, make_z_evac(first=False, last=True), ntl=halfA)
                            run(linear_A_gen(slab, slab_t, hb, hb_t, make_z_evac(first=False, last=True), ntl=halfB),
                                layer_norm_gen(halfA, g2, b2, want_bf=not final))
                return layer_norm_gen(halfB if SPLIT else ntiles, g2, b2, want_bf=not final)

            chk(1)
            if stn == sts[0]:
                run(pend)
                pend = None
                stage_X(stn)
            chk(2)
            with Scope(k) as ss:
                og = sb(ss, "og", (128, NCH, NTMAX), BF16)
                og_t = TS(NCH, 5)
                with Scope(k) as s2:
                    glT = sb(s2, "glT", (32, NTMAX), F32)
                    eb = sb(s2, "eb", (128, 4, NTMAX), F32)
                    enb = sb(s2, "enb", (128, 4, NTMAX), F32)
                    e1 = [sb(s2, "e1", (128, 512), F32) for _ in range(1)]
                    spt = [sb(s2, "spt", (128, 512), F32) for _ in range(1)]
                    ec = sb(s2, "ec", (128, 5, 512), F32)
                    vt = sb(s2, "vt", (128, 5, 1024), BF16)
                    qd = sb(s2, "qd", (128, 4, NTMAX), BF16)
                    kd = sb(s2, "kd", (128, 4, NTMAX), BF16)
                    qd32 = sb(s2, "qd32", (128, 4, NSAMP), F32)
                    kkt = sb(s2, "kkt", (128, 5, 512), BF16)
                    glT_t = TS(5)
                    eb_t = TS(4, 5)
                    enb_t = TS(4, 5)
                    e1_t, spt_t = TS(1), TS(1)
                    ec_t, vt_t, kkt_t = TS(5), TS(5), TS(5)
                    qd_t, kd_t = TS(4, 5), TS(4, 5)
                    qd32_t = T()

                    k.op(dve, lambda e: e.memset(glT[:], 1.0), writes=glT_t)
                    for ni, (n0, nn) in enumerate(ntiles):
                        pg, pg_t = misc_bank()
                        k.mm([lambda e, c=c: e.matmul(pg[0:16, 0:nn], lhsT=glw[:, c, :], rhs=xb[:, c, n0:n0 + nn], start=(c == 0), stop=(c == NCH - 1))
                              for c in range(NCH)], reads=[c_t] + xtr(xb_t, range(NCH), n0, nn), writes=[pg_t])
                        k.op(dve, lambda e: e.tensor_copy(out=glT[0:16, n0:n0 + nn], in_=pg[0:16, 0:nn]), reads=[pg_t],
                             writes=[glT_t[i] for i in tiles_of(n0, nn)])
                    chk(3)
                    def gate_gen():
                        for i, (t0, rows) in enumerate(ttiles):
                            b = 0
                            co_u, co_m = (C_UP, C_M2P) if i < 4 else (C_US, C_M2S)
                            pg, pg_t = misc_bank()
                            k.mm([lambda e: e.matmul(pg[0:rows, :], lhsT=glT[0:17, t0:t0 + rows], rhs=wg[0:17, :], start=True, stop=True)],
                                 reads=[glT_t[i], c_t], writes=[pg_t])
                            k.op(act, lambda e: e.activation(out=e1[b][0:rows, :], in_=pg[0:rows, :], func=AF.Exp, scale=-1.0),
                                 reads=[pg_t], writes=[e1_t[b]])
                            k.op(act, lambda e: e.activation(out=spt[b][0:rows, :], in_=e1[b][0:rows, :], func=AF.Ln, bias=1.0),
                                 reads=[e1_t[b]], writes=[spt_t[b]])
                            pbb, pbb_t = misc_bank()
                            k.mm([lambda e, h=h: e.matmul(pbb[:, h * 128:h * 128 + rows], lhsT=spt[b][0:rows, h * 128:(h + 1) * 128],
                                                          rhs=cst[0:rows, co_u:co_u + rows], start=True, stop=True) for h in range(4)],
                                 reads=[spt_t[b], c_t], writes=[pbb_t])
                            pbv = pbb[:, :].rearrange("p (a b) -> p a b", a=4)[:, :, 0:rows]
                            k.op(act, lambda e: e.activation(out=eb[:, :, t0:t0 + rows], in_=pbv, func=AF.Exp),
                                 reads=[pbb_t], writes=[eb_t[h][i] for h in range(4)])
                            k.op(act, lambda e: e.activation(out=enb[:, :, t0:t0 + rows], in_=pbv, func=AF.Exp, scale=-1.0),
                                 reads=[pbb_t], writes=[enb_t[h][i] for h in range(4)])
                            pc, pc_t = misc_bank()
                            k.mm([lambda e: e.matmul(pc[0:rows, :], lhsT=cst[0:rows, co_m:co_m + rows], rhs=spt[b][0:rows, :], start=True, stop=True)],
                                 reads=[spt_t[b], c_t], writes=[pc_t])
                            k.op(act, lambda e: e.activation(out=ec[0:rows, i, :], in_=pc[0:rows, :], func=AF.Exp),
                                 reads=[pc_t], writes=[ec_t[i]])

                            yield

                    chk(4)
                    bank_rr = {"i": 0}

                    def acc_bank():
                        i_ = bank_rr["i"] % 4
                        bank_rr["i"] += 1
                        return pb[i_], pb_t[i_]

                    slab, slab_t = next_slab()
                    def v_gen():
                        for i, (t0, rows) in enumerate(ttiles):
                            for hh in range(2):
                                pv, pv_t = acc_bank()
                                k.mm([lambda e, c=c: e.matmul(pv[0:rows, :], lhsT=xb[:, c, t0:t0 + rows], rhs=slab[:, c, hh * 512:(hh + 1) * 512],
                                                              start=(c == 0), stop=(c == NCH - 1)) for c in range(NCH)],
                                     reads=[slab_t] + [xb_t[c][i] for c in range(NCH)], writes=[pv_t])
                                eng = act if hh == 0 else dve
                                if hh == 0:
                                    k.op(act, lambda e: e.activation(out=vt[0:rows, i, 0:512], in_=pv[0:rows, :], func=AF.Copy),
                                         reads=[pv_t], writes=[vt_t[i]])
                                else:
                                    k.op(dve, lambda e: e.tensor_copy(out=vt[0:rows, i, 512:1024], in_=pv[0:rows, :]),
                                         reads=[pv_t, vt_t[i]], writes=[vt_t[i]])
                            yield

                    run(v_gen(), gate_gen())

                    chk(5)
                    slab, slab_t = next_slab()

                    def evac_qk(m, ni, n0, nn, ps, ps_t):
                        if m < 4:
                            k.op(dve, lambda e: e.scalar_tensor_tensor(out=qd[:, m, n0:n0 + nn], in0=ps[:, 0:nn], scalar=QSCALE,
                                                                       in1=eb[:, m, n0:n0 + nn], op0=ALU.mult, op1=ALU.mult),
                                 reads=[ps_t] + [eb_t[m][i] for i in tiles_of(n0, nn)], writes=[qd_t[m][i] for i in tiles_of(n0, nn)])
                            if ni == 1:
                                k.op(dve, lambda e: e.scalar_tensor_tensor(out=qd32[:, m, 0:nn], in0=ps[:, 0:nn], scalar=QSCALE,
                                                                           in1=eb[:, m, n0:n0 + nn], op0=ALU.mult, op1=ALU.mult),
                                     reads=[ps_t] + [eb_t[m][4]], writes=[qd32_t])
                        else:
                            h = m - 4
                            k.op(dve, lambda e: e.tensor_tensor(out=kd[:, h, n0:n0 + nn], in0=ps[:, 0:nn], in1=enb[:, h, n0:n0 + nn], op=ALU.mult),
                                 reads=[ps_t] + [enb_t[h][i] for i in tiles_of(n0, nn)], writes=[kd_t[h][i] for i in tiles_of(n0, nn)])

                    linear_A(slab, slab_t, xb, xb_t, evac_qk)
                    for i, (t0, rows) in enumerate(ttiles):
                        pk, pk_t = acc_bank()
                        k.mm([lambda e, c=c: e.matmul(pk[0:rows, :], lhsT=xb[:, c, t0:t0 + rows], rhs=slab[:, c, 512:1024],
                                                      start=(c == 0), stop=(c == NCH - 1)) for c in range(NCH)],
                             reads=[slab_t] + [xb_t[c][i] for c in range(NCH)], writes=[pk_t])
                        k.op(dve, lambda e: e.tensor_tensor(out=kkt[0:rows, i, :], in0=pk[0:rows, :], in1=ec[0:rows, i, :], op=ALU.mult),
                             reads=[pk_t, ec_t[i]], writes=[kkt_t[i]])

                    chk(6)
                    slab, slab_t = next_slab()

                    def evac_r(m, ni, n0, nn, ps, ps_t):
                        k.op(act, lambda e: e.activation(out=og[:, m, n0:n0 + nn], in_=ps[:, 0:nn], func=AF.Silu),
                             reads=[ps_t], writes=[og_t[m][i] for i in tiles_of(n0, nn)])

                    linear_A(slab, slab_t, xb, xb_t, evac_r, ntl=halfA)
                    r_slab, r_slab_t = slab, slab_t

                    chk(7)
                    with Scope(k) as s3:
                        sT = sb(s3, "sT", (128, 4, 128), BF16)
                        o32 = sb(s3, "o32", (128, NCH, 128), F32)
                        osq = sb(s3, "osq", (128, NCH, 128), BF16)
                        rsd = sb(s3, "rsd", (128, 512), F32)
                        otm = sb(s3, "otm", (128, NCH, 128), F32)
                        sT_t, o32_t, osq_t, rsd_t, otm_t = T(), TS(2), TS(2), T(), T()

                        def rms_gate(i, t0, rows, pO, pO_t, after=None):
                            povs = [pO[bk][:, :].rearrange("p (a b) -> p a b", a=4)[:, :, 0:rows] for bk in range(2)]
                            for bk in range(2):
                                k.op(act, lambda e: e.activation(out=osq[:, bk * 4:bk * 4 + 4, 0:rows], in_=povs[bk], func=AF.Square),
                                     reads=[pO_t[bk]], writes=[osq_t[bk]])
                            if after is not None:
                                after()
                            pr, pr_t = pb[7], pb_t[7]
                            k.mm([lambda e, h=h, vc=vc: e.matmul(pr[:, h * 128:h * 128 + rows], lhsT=ones_rms[:], rhs=osq[:, h * 2 + vc, 0:rows],
                                                                 start=(vc == 0), stop=(vc == 1)) for h in range(4) for vc in range(2)],
                                 reads=osq_t + [c_t], writes=[pr_t])
                            for bk in range(2):
                                k.op(dve, lambda e: e.tensor_tensor(out=o32[:, bk * 4:bk * 4 + 4, 0:rows], in0=povs[bk],
                                                                    in1=prm[:, 64 + bk * 4:68 + bk * 4].unsqueeze(2).broadcast_to([128, 4, rows]), op=ALU.mult),
                                     reads=[pO_t[bk], c_t], writes=[o32_t[bk]])
                            prv = pr[:, :].rearrange("p (a b) -> p a b", a=4)[:, :, 0:rows]
                            rsv = rsd[:, :].rearrange("p (a b) -> p a b", a=4)[:, :, 0:rows]
                            k.op(act, lambda e: e.activation(out=rsv, in_=prv, func=AF.Ln, bias=prm_col(97)), reads=[pr_t, c_t], writes=[rsd_t])
                            k.op(act, lambda e: e.activation(out=rsv, in_=rsv, func=AF.Exp, scale=-0.5), reads=[rsd_t], writes=[rsd_t])
                            k.op(dve, lambda e: e.tensor_tensor(out=otm[:, :, 0:rows].rearrange("p (h v) i -> p h v i", h=4),
                                                                in0=o32[:, :, 0:rows].rearrange("p (h v) i -> p h v i", h=4),
                                                                in1=rsv.unsqueeze(2).broadcast_to([128, 4, 2, rows]), op=ALU.mult),
                                 reads=o32_t + [rsd_t], writes=[otm_t])
                            k.op(pool, lambda e: e.tensor_tensor(out=og[:, :, t0:t0 + rows], in0=otm[:, :, 0:rows], in1=og[:, :, t0:t0 + rows], op=ALU.mult),
                                 reads=[otm_t], writes=[og_t[c][i] for c in range(NCH)])

                        def core_tile_gen(i):
                            t0, rows = i * 128, 128
                            pS, pS_t = pb[2], pb_t[2]
                            k.mm([lambda e, h=h: e.matmul(pS[:, h * 128:(h + 1) * 128], lhsT=kd[:, h, t0:t0 + 128], rhs=qd[:, h, t0:t0 + 128],
                                                          start=True, stop=True) for h in range(4)],
                                 reads=[kd_t[h][i] for h in range(4)] + [qd_t[h][i] for h in range(4)], writes=[pS_t])
                            k.op(dve, lambda e: e.tensor_tensor(out=sT[:, :, :].rearrange("p h j -> p (h j)"), in0=pS[:, :], in1=cst[:, C_CP4:C_CP4 + 512], op=ALU.mult),
                                 reads=[pS_t, c_t], writes=[sT_t])
                            yield
                            kvb = [(3, 4), (0, 1)]
                            for cc in range(2):
                                r0 = cc * 64
                                for hb in range(2):
                                    bk = kvb[cc][hb]
                                    k.mm([lambda e, h=h: e.matmul(pb[bk][:, (h % 2) * 256:(h % 2 + 1) * 256], lhsT=kkt[r0:r0 + 64, i, h * 128:(h + 1) * 128],
                                                                  rhs=vt[r0:r0 + 64, i, h * 256:(h + 1) * 256], start=True, stop=True)
                                          for h in (2 * hb, 2 * hb + 1)],
                                         reads=[kkt_t[i], vt_t[i]], writes=[pb_t[bk]])
                            nA = gsb["n"]
                            pO = [pb[5], pb[6]]
                            pO_t = [pb_t[5], pb_t[6]]
                            for bk in range(2):
                                fns = []
                                first = True
                                for c in range(bk * 4, bk * 4 + 4):
                                    h, vc = c // 2, c % 2
                                    blk = pO[bk][:, (c % 4) * 128:(c % 4 + 1) * 128]
                                    fns.append(lambda e, blk=blk, h=h, vc=vc, first=first: e.matmul(
                                        blk[:, 0:128], lhsT=vt[:, i, h * 256 + vc * 128:h * 256 + (vc + 1) * 128], rhs=sT[:, h, :],
                                        start=first, stop=False, skip_group_check=True))
                                    first = False
                                    fns.append(lambda e, blk=blk, h=h, vc=vc: e.matmul(
                                        blk[:, 0:64], lhsT=Sbf[nA % 3][:, h, vc * 128:(vc + 1) * 128],
                                        rhs=qd[:, h, t0:t0 + 64], start=False, stop=False, skip_group_check=True))
                                k.mm(fns, reads=[vt_t[i], sT_t, Sbf_t[nA % 3]] + [qd_t[h][i] for h in range(4)], writes=[pO_t[bk]])
                            yield

                            def upd(cc):
                                n_ = gsb["n"]
                                tl = t0 + cc * 64 + 63
                                for h in range(4):
                                    bk = kvb[cc][h // 2]
                                    k.op(dve, lambda e: e.scalar_tensor_tensor(out=S32[:, h, :], in0=S32[:, h, :], scalar=eb[:, h, tl:tl + 1],
                                                                               in1=pb[bk][:, (h % 2) * 256:(h % 2 + 1) * 256], op0=ALU.mult, op1=ALU.add),
                                         reads=[pb_t[bk], eb_t[h][i]], writes=[S32_t[h]])
                                k.op(act, lambda e: e.activation(out=Sbf[(n_ + 1) % 3][:], in_=S32[:], func=AF.Copy),
                                     reads=S32_t, writes=[Sbf_t[(n_ + 1) % 3]])
                                gsb["n"] = n_ + 1

                            upd(0)
                            for bk in range(2):
                                fns = []
                                for c in range(bk * 4, bk * 4 + 4):
                                    h, vc = c // 2, c % 2
                                    blk = pO[bk][:, (c % 4) * 128:(c % 4 + 1) * 128]
                                    fns.append(lambda e, blk=blk, h=h, vc=vc, c=c: e.matmul(
                                        blk[:, 64:128], lhsT=Sbf[(nA + 1) % 3][:, h, vc * 128:(vc + 1) * 128],
                                        rhs=qd[:, h, t0 + 64:t0 + 128], start=False, stop=(c % 4 == 3), skip_group_check=True))
                                if bk == 0 and NWARM_CORE:
                                    fns[0] = (fns[0], [Sbf_t[(nA + 1) % 3]])
                                    warm = [lambda e: e.matmul(pb[7][:, 0:512], lhsT=kd[:, 0, t0:t0 + 128], rhs=qd[:, 0, 0:512], start=True, stop=True,
                                                               skip_group_check=True) for _ in range(NWARM_CORE)]
                                    k.mm(fns, reads=[qd_t[h][j] for h in range(4) for j in range(4)] + [kd_t[0][i]], writes=[pO_t[bk], pb_t[7]], warm=warm)
                                else:
                                    k.mm(fns, reads=[Sbf_t[(nA + 1) % 3]] + [qd_t[h][i] for h in range(4)], writes=[pO_t[bk]])
                            yield
                            rms_gate(i, t0, rows, pO, pO_t, after=lambda: upd(1))
                            yield

                        def core_tiles_gen(tl_):
                            for i_ in tl_:
                                yield from core_tile_gen(i_)

                        if not SPLIT:
                            run(core_tiles_gen([0, 1, 2, 3]))
                        else:
                            if nsamp:
                                linear_A(r_slab, r_slab_t, xb, xb_t, evac_r, ntl=halfB)
                                run(core_tiles_gen([0, 1]))
                            else:
                                run(linear_A_gen(r_slab, r_slab_t, xb, xb_t, evac_r, ntl=halfB, nsets=2), core_tiles_gen([0, 1]))
                            wo_slab, wo_slab_t = next_slab()
                            run(linear_A_gen(wo_slab, wo_slab_t, og, og_t, make_z_evac(), ntl=halfA, nsets=2), core_tiles_gen([2, 3]))
                        chk(8)

                        if nsamp:
                            i, t0, rows = 4, NPT, nsamp
                            k.scope_barrier([t_ for row in enb_t for t_ in row] + ec_t + e1_t + spt_t + vt_t[0:4])
                            enb_fl = enb[:, :, :].rearrange("p a n -> p (a n)")
                            ec_fl = ec[:, :, :].rearrange("p a n -> p (a n)")
                            vt_fl = vt[:, 0:4, :].rearrange("p a n -> p (a n)").bitcast(F32)
                            NB0 = 4
                            S0 = [enb_fl[:, j * 1024:(j + 1) * 1024].rearrange("p (h v) -> p h v", h=4) for j in range(2)] + \
                                 [vt_fl[:, j * 1024:(j + 1) * 1024].rearrange("p (h v) -> p h v", h=4) for j in range(2)]
                            Sn = [ec_fl[:, j * 1024:(j + 1) * 1024].rearrange("p (h v) -> p h v", h=4) for j in range(2)]
                            kkm = [e1[0][:, :].bitcast(BF16), spt[0][:, :].bitcast(BF16)]
                            S0_t, Sn_t, kkm_t = TS(4), TS(2), TS(2)
                            S0_ds = [k.dsem("d_S0_%d" % j) for j in range(4)]
                            Sn_ds = [k.dsem("d_Sn_%d" % j) for j in range(2)]
                            pS, pS_t = pb[2], pb_t[2]
                            k.mm([lambda e, h=h: e.matmul(pS[0:rows, h * 64:(h + 1) * 64], lhsT=kd[:, h, t0:t0 + rows], rhs=qd[:, h, t0:t0 + rows],
                                                          start=True, stop=True) for h in range(4)],
                                 reads=[kd_t[h][i] for h in range(4)] + [qd_t[h][i] for h in range(4)], writes=[pS_t])
                            k.op(dve, lambda e: e.tensor_tensor(out=sT[0:rows, :, 0:rows], in0=pS[0:rows, 0:4 * rows].rearrange("p (h j) -> p h j", h=4),
                                                                in1=cst[0:rows, C_CS4:C_CS4 + 4 * rows].rearrange("p (h j) -> p h j", h=4), op=ALU.mult),
                                 reads=[pS_t, c_t], writes=[sT_t])
                            pO1, pO1_t = pb[5], pb_t[5]
                            fns = []
                            for c in range(NCH):
                                h, vc = c // 2, c % 2
                                fns.append(lambda e, c=c, h=h, vc=vc: e.matmul(
                                    pO1[:, c * 64:(c + 1) * 64], lhsT=vt[0:rows, i, h * 256 + vc * 128:h * 256 + (vc + 1) * 128], rhs=sT[0:rows, h, 0:rows],
                                    start=(c == 0), stop=False, skip_group_check=True))
                            k.mm(fns, reads=[vt_t[i], sT_t], writes=[pO1_t])
                            for s_ in range(NB0):
                                k.dma(sp, S0_ds[s_], S0[s_], sg[s_].rearrange("h d v -> d h v"), writes=[S0_t[s_]])
                            def sample_seq_gen():
                                for s_ in range(NSEQ):
                                    b = s_ % 2
                                    b0 = s_ % NB0
                                    fns = []
                                    for c in range(NCH):
                                        h, vc = c // 2, c % 2
                                        fns.append(lambda e, c=c, h=h, vc=vc: e.matmul(
                                            pO1[:, c * 64 + 4 * s_:c * 64 + 4 * s_ + 4], lhsT=S0[b0][:, h, vc * 128:(vc + 1) * 128],
                                            rhs=qd32[:, h, 4 * s_:4 * s_ + 4], start=False, stop=(s_ == NSEQ - 1 and c == NCH - 1),
                                            skip_group_check=True))
                                    k.mm(fns, reads=[S0_t[b0], qd32_t], writes=[pO1_t])
                                    k.op(act, lambda e: e.activation(out=kkm[b][0:rows, 0:512], in_=kkt[0:rows, i, :], func=AF.Identity,
                                                                     scale=cst[0:rows, C_SEQ + s_:C_SEQ + s_ + 1]),
                                         reads=[kkt_t[i], c_t], writes=[kkm_t[b]])
                                    for hb in range(2):
                                        bk = 3 + hb
                                        k.mm([lambda e, h=h: e.matmul(pb[bk][:, (h % 2) * 256:(h % 2 + 1) * 256], lhsT=kkm[b][0:rows, h * 128:(h + 1) * 128],
                                                                      rhs=vt[0:rows, i, h * 256:(h + 1) * 256], start=True, stop=True)
                                              for h in (2 * hb, 2 * hb + 1)],
                                             reads=[kkm_t[b], vt_t[i]], writes=[pb_t[bk]])
                                    tl = t0 + 4 * s_ + 3
                                    for h in range(4):
                                        bk = 3 + h // 2
                                        k.op(dve, lambda e: e.scalar_tensor_tensor(out=Sn[b][:, h, :], in0=S0[b0][:, h, :], scalar=eb[:, h, tl:tl + 1],
                                                                                   in1=pb[bk][:, (h % 2) * 256:(h % 2 + 1) * 256], op0=ALU.mult, op1=ALU.add),
                                             reads=[pb_t[bk], eb_t[h][i], S0_t[b0]], writes=[Sn_t[b]])
                                    k.dma(pool, Sn_ds[b], gs[s_].rearrange("h d v -> d h v"), Sn[b], reads=[Sn_t[b]])
                                    if s_ + NB0 < NSEQ:
                                        k.dma(sp, S0_ds[b0], S0[b0], sg[s_ + NB0].rearrange("h d v -> d h v"), writes=[S0_t[b0]])
                                    yield

                            wo_slab, wo_slab_t = next_slab()
                            run(linear_A_gen(wo_slab, wo_slab_t, og, og_t, make_z_evac(), ntl=ntiles[0:1], nsets=2, nwarm=NWARM_WO), sample_seq_gen())
                            pov = pO1[:, :].rearrange("p (a b) -> p a b", a=8)
                            k.op(act, lambda e: e.activation(out=osq[:, :, 0:rows], in_=pov, func=AF.Square), reads=[pO1_t], writes=osq_t)
                            pr, pr_t = pb[7], pb_t[7]
                            k.mm([lambda e, h=h, vc=vc: e.matmul(pr[:, h * 128:h * 128 + rows], lhsT=ones_rms[:], rhs=osq[:, h * 2 + vc, 0:rows],
                                                                 start=(vc == 0), stop=(vc == 1)) for h in range(4) for vc in range(2)],
                                 reads=osq_t + [c_t], writes=[pr_t])
                            k.op(dve, lambda e: e.tensor_tensor(out=o32[:, :, 0:rows], in0=pov, in1=prm[:, 64:72].unsqueeze(2).broadcast_to([128, NCH, rows]), op=ALU.mult),
                                 reads=[pO1_t, c_t], writes=o32_t)
                            prv = pr[:, :].rearrange("p (a b) -> p a b", a=4)[:, :, 0:rows]
                            rsv = rsd[:, :].rearrange("p (a b) -> p a b", a=4)[:, :, 0:rows]
                            k.op(act, lambda e: e.activation(out=rsv, in_=prv, func=AF.Ln, bias=prm_col(97)), reads=[pr_t, c_t], writes=[rsd_t])
                            k.op(act, lambda e: e.activation(out=rsv, in_=rsv, func=AF.Exp, scale=-0.5), reads=[rsd_t], writes=[rsd_t])
                            k.op(dve, lambda e: e.tensor_tensor(out=otm[:, :, 0:rows].rearrange("p (h v) i -> p h v i", h=4),
                                                                in0=o32[:, :, 0:rows].rearrange("p (h v) i -> p h v i", h=4),
                                                                in1=rsv.unsqueeze(2).broadcast_to([128, 4, 2, rows]), op=ALU.mult),
                                 reads=o32_t + [rsd_t], writes=[otm_t])
                            k.op(pool, lambda e: e.tensor_tensor(out=og[:, :, t0:t0 + rows], in0=otm[:, :, 0:rows], in1=og[:, :, t0:t0 + rows], op=ALU.mult),
                                 reads=[otm_t], writes=[og_t[c][i] for c in range(NCH)])
                        if stn == NST - 1:
                            gp_ds = k.dsem("d_gp")
                            k.dma(sp, gp_ds, gp.rearrange("h d v -> d h v"), S32[:], reads=S32_t)
                if SPLIT:
                    run(linear_A_gen(wo_slab, wo_slab_t, og, og_t, make_z_evac(), ntl=halfB), layer_norm_gen(halfA, 0 * 8, 1 * 8))
                    pend = layer_norm_gen(halfB, 0 * 8, 1 * 8)
                else:
                    if nsamp:
                        linear_A(wo_slab, wo_slab_t, og, og_t, make_z_evac(), ntl=ntiles[1:])
                    else:
                        wo_slab, wo_slab_t = next_slab()
                        linear_A(wo_slab, wo_slab_t, og, og_t, make_z_evac(), nwarm=NWARM_WO)
                    pend = layer_norm_gen(ntiles, 0 * 8, 1 * 8)
                chk(9)
                pend = mlp_stage(0, pre=pend)
                chk(10)

            with Scope(k) as ss:
                yb = sb(ss, "yb", (128, NCH, NTMAX), BF16)
                yb_t = TS(NCH, 5)
                with Scope(k) as s2:
                    cgs = sb(s2, "cgs", (128, NCH, NTMAX), F32)
                    Ub = sb(s2, "Ub", (128, NCH, NPT + 2), F32)
                    Us = sb(s2, "Us", (128, NCH, NSEQ, 6), F32)
                    acc = [sb(s2, "acc", (128, NTMAX), F32) for _ in range(NCH)]
                    cgs_t = TS(NCH, 5)
                    Ub_t, Us_t = TS(NCH), TS(NCH)
                    acc_t = TS(NCH)
                    k.op(pool, lambda e: e.tensor_copy(out=Ub[:, :, 0:2], in_=tailb[:]), reads=[tail_t], writes=Ub_t)
                    if nsamp:
                        scin = sb(s2, "scin", (2 * NSEQ, D), F32)
                        scin_t = T()
                        sc_ds = k.dsem("d_scin")
                        k.dma(sp, sc_ds, scin[:], scv[:, :], writes=[scin_t])
                        for half in range(2):
                            pt, pt_t = misc_bank()
                            k.mm([lambda e, j=j: e.transpose(out=pt[:, j * 32:(j + 1) * 32], in_=scin[:, (half * 4 + j) * 128:(half * 4 + j + 1) * 128],
                                                             identity=cst[0:32, C_ID:C_ID + 32]) for j in range(4)],
                                 reads=[scin_t, c_t], writes=[pt_t])
                            src_ps = pt[:, 0:128].rearrange("p (a s r) -> p a s r", a=4, s=NSEQ)
                            k.op(dve, lambda e: e.tensor_copy(out=Us[:, half * 4:half * 4 + 4, :, 0:2], in_=src_ps),
                                 reads=[pt_t], writes=[Us_t[c] for c in range(half * 4, half * 4 + 4)])
                    slab, slab_t = next_slab()

                    def evac_cg(m, ni, n0, nn, ps, ps_t):
                        k.op(act, lambda e: e.activation(out=cgs[:, m, n0:n0 + nn], in_=ps[:, 0:nn], func=AF.Copy), reads=[ps_t], writes=xtr(cgs_t, [m], n0, nn))

                    if SPLIT:
                        run(linear_A_gen(slab, slab_t, xb, xb_t, evac_cg, ntl=halfA), pend)
                        linear_A(slab, slab_t, xb, xb_t, evac_cg, ntl=halfB)
                    else:
                        run(pend)
                        linear_after_ln(slab, slab_t, xb, xb_t, evac_cg)
                    pend = None
                    slab, slab_t = next_slab()

                    def evac_h(m, ni, n0, nn, ps, ps_t):
                        a_, a_t = acc[m], acc_t[m]
                        w0, w1, w2 = prm_col(72 + m), prm_col(80 + m), prm_col(88 + m)
                        if ni == 0:
                            k.op(dve, lambda e: e.tensor_tensor(out=Ub[:, m, 2:2 + NPT], in0=ps[:, 0:nn], in1=cgs[:, m, 0:NPT], op=ALU.mult),
                                 reads=[ps_t] + xtr(cgs_t, [m], n0, nn), writes=[Ub_t[m]])
                            k.op(act, lambda e: e.activation(out=a_[:, 0:NPT], in_=Ub[:, m, 2:2 + NPT], func=AF.Identity, scale=w2), reads=[Ub_t[m], c_t], writes=[a_t])
                            k.op(dve, lambda e: e.scalar_tensor_tensor(out=a_[:, 0:NPT], in0=Ub[:, m, 1:1 + NPT], scalar=w1, in1=a_[:, 0:NPT], op0=ALU.mult, op1=ALU.add),
                                 reads=[Ub_t[m], c_t, a_t], writes=[a_t])
                            k.op(dve, lambda e: e.scalar_tensor_tensor(out=a_[:, 0:NPT], in0=Ub[:, m, 0:NPT], scalar=w0, in1=a_[:, 0:NPT], op0=ALU.mult, op1=ALU.add),
                                 reads=[Ub_t[m], c_t, a_t], writes=[a_t])
                        else:
                            k.op(dve, lambda e: e.tensor_tensor(out=Us[:, m, :, 2:6], in0=ps[:, 0:nn].rearrange("p (s r) -> p s r", s=NSEQ),
                                                                in1=cgs[:, m, n0:n0 + nn].rearrange("p (s r) -> p s r", s=NSEQ), op=ALU.mult),
                                 reads=[ps_t] + xtr(cgs_t, [m], n0, nn), writes=[Us_t[m]])
                            av = a_[:, NPT:NPT + nsamp].rearrange("p (s r) -> p s r", s=NSEQ)
                            k.op(act, lambda e: e.activation(out=av, in_=Us[:, m, :, 2:6], func=AF.Identity, scale=w2), reads=[Us_t[m], c_t], writes=[a_t])
                            k.op(dve, lambda e: e.scalar_tensor_tensor(out=av, in0=Us[:, m, :, 1:5], scalar=w1, in1=av, op0=ALU.mult, op1=ALU.add),
                                 reads=[Us_t[m], c_t, a_t], writes=[a_t])
                            k.op(dve, lambda e: e.scalar_tensor_tensor(out=av, in0=Us[:, m, :, 0:4], scalar=w0, in1=av, op0=ALU.mult, op1=ALU.add),
                                 reads=[Us_t[m], c_t, a_t], writes=[a_t])

                    linear_A(slab, slab_t, xb, xb_t, evac_h)
                    slab, slab_t = next_slab()

                    def evac_bg(m, ni, n0, nn, ps, ps_t):
                        k.op(dve, lambda e: e.tensor_tensor(out=yb[:, m, n0:n0 + nn], in0=acc[m][:, n0:n0 + nn], in1=ps[:, 0:nn], op=ALU.mult),
                             reads=[ps_t, acc_t[m]], writes=xtr(yb_t, [m], n0, nn))

                    linear_A(slab, slab_t, xb, xb_t, evac_bg)
                    if stn < NST - 1:
                        k.op(pool, lambda e: e.tensor_copy(out=tailb[:], in_=Ub[:, :, NPT:NPT + 2]), reads=Ub_t, writes=[tail_t])
                    else:
                        cpo = sb(s2, "cpo", (2, D), F32)
                        cso = sb(s2, "cso", (2 * NSEQ, D), F32)
                        ustg = sb(s2, "ustg", (128, NCH, 2 * NSEQ), F32)
                        cpo_t, cso_t, ustg_t = T(), T(), T()
                        k.op(pool, lambda e: e.tensor_copy(out=ustg[:, :, :].rearrange("p c (s r) -> p c s r", s=NSEQ), in_=Us[:, :, :, 4:6]),
                             reads=Us_t, writes=[ustg_t])
                        for half in range(2):
                            pt, pt_t = misc_bank()
                            k.mm([lambda e, j=j: e.transpose(out=pt[0:2, j * 128:(j + 1) * 128], in_=Ub[:, half * 4 + j, NPT:NPT + 2], identity=ident)
                                  for j in range(4)], reads=Ub_t + [c_t], writes=[pt_t])
                            k.op(dve, lambda e: e.tensor_copy(out=cpo[:, half * 512:(half + 1) * 512], in_=pt[0:2, :]), reads=[pt_t], writes=[cpo_t])
                            pt2, pt2_t = misc_bank()
                            k.mm([lambda e, j=j: e.transpose(out=pt2[0:32, j * 128:(j + 1) * 128], in_=ustg[:, half * 4 + j, :], identity=ident)
                                  for j in range(4)], reads=[ustg_t, c_t], writes=[pt2_t])
                            k.op(dve, lambda e: e.tensor_copy(out=cso[:, half * 512:(half + 1) * 512], in_=pt2[0:32, :]), reads=[pt2_t], writes=[cso_t])
                        co_ds = k.dsem("d_co")
                        co2_ds = k.dsem("d_co2")
                        k.dma(sp, co_ds, cp[:, :], cpo[:], reads=[cpo_t])
                        k.dma(sp, co2_ds, cs[:, :], cso[:], reads=[cso_t])
                slab, slab_t = next_slab()
                if SPLIT:
                    linear_A(slab, slab_t, yb, yb_t, make_z_evac(), ntl=halfA)
                    run(linear_A_gen(slab, slab_t, yb, yb_t, make_z_evac(), ntl=halfB), layer_norm_gen(halfA, 4 * 8, 5 * 8))
                    pend = layer_norm_gen(halfB, 4 * 8, 5 * 8)
                else:
                    linear_A(slab, slab_t, yb, yb_t, make_z_evac())
                    pend = layer_norm_gen(ntiles, 4 * 8, 5 * 8)
                chk(11)
                pend = mlp_stage(1, final=True, pre=pend)
                chk(12)

            with Scope(k) as ss:
                nxt = [s_ for s_ in sts if s_ > stn]
                if nxt:
                    prefetch_x(nxt[0])
                yo = sb(ss, "yo", (128, 5, D), F32)
                yo_t = TS(5)
                yo_ds = [k.dsem("d_yo%d_%d" % (stn, i)) for i in range(5)]
                run(pend)
                pend = None
                for c in range(NCH):
                    pt, pt_t = misc_bank()
                    k.mm([lambda e, i=i: e.transpose(out=pt[:, i * 128:(i + 1) * 128], in_=xf[:, c, i * 128:(i + 1) * 128], identity=ident)
                          for i in range(4)], reads=[xf_t[c][i] for i in range(4)] + [c_t], writes=[pt_t])
                    ptv = pt[:, :].rearrange("p (a b) -> p a b", a=4)
                    if c % 2 == 0:
                        k.op(dve, lambda e: e.tensor_copy(out=yo[:, 0:4, c * 128:(c + 1) * 128], in_=ptv), reads=[pt_t], writes=yo_t[0:4])
                    else:
                        k.op(act, lambda e: e.activation(out=yo[:, 0:4, c * 128:(c + 1) * 128], in_=ptv, func=AF.Copy), reads=[pt_t], writes=yo_t[0:4])
                    if nsamp:
                        pt2, pt2_t = misc_bank()
                        k.mm([lambda e: e.transpose(out=pt2[0:nsamp, 0:128], in_=xf[:, c, NPT:NPT + nsamp], identity=ident)],
                             reads=[xf_t[c][4], c_t], writes=[pt2_t])
                        k.op(dve, lambda e: e.tensor_copy(out=yo[0:nsamp, 4, c * 128:(c + 1) * 128], in_=pt2[0:nsamp, 0:128]), reads=[pt2_t], writes=[yo_t[4]])
                nxt = [s_ for s_ in sts if s_ > stn]
                if nxt:
                    stage_X(nxt[0])
                for i, (t0, rows) in enumerate(ttiles):
                    dst = yp[stn * NPT + t0: stn * NPT + t0 + rows, :] if i < 4 else ys[0:rows, :]
                    k.dma(sp, yo_ds[i], dst, yo[0:rows, i, :], reads=[yo_t[i]])

          except _Stop:
            break

        for d in k.dsems:
            if d.n > 0:
                sp.e.wait_ge(d.sem, d.n)
    return nc


def _consts():
    cst = np.zeros((128, C_TOT), np.float32)
    cst[:, C_ID:C_ID + 128] = np.eye(128, dtype=np.float32)
    s = np.arange(128)[:, None]
    t = np.arange(128)[None, :]
    same = (s // 64) == (t // 64)
    cst[:, C_UP:C_UP + 128] = np.where(same & (s <= t), -1.0 / 16.0, 0.0)
    cst[:, C_M2P:C_M2P + 128] = np.where(same & (s > t), -1.0 / 16.0, 0.0)
    cst[:, C_CP:C_CP + 128] = np.where(same & (s <= t), 1.0, 0.0)
    same4 = ((s // 4) == (t // 4)) & (s < 64) & (t < 64)
    cst[:, C_US:C_US + 128] = np.where(same4 & (s <= t), -1.0 / 16.0, 0.0)
    cst[:, C_M2S:C_M2S + 128] = np.where(same4 & (s > t), -1.0 / 16.0, 0.0)
    cst[:, C_CS:C_CS + 128] = np.where(same4 & (s <= t), 1.0, 0.0)
    tt = np.arange(128)[:, None]
    ss = np.arange(16)[None, :]
    cst[:, C_SEQ:C_SEQ + 16] = np.where(((tt // 4) == ss) & (tt < 64), 1.0, 0.0)
    for h in range(4):
        cst[:, C_CP4 + h * 128:C_CP4 + (h + 1) * 128] = cst[:, C_CP:C_CP + 128]
        cst[:, C_CS4 + h * 64:C_CS4 + (h + 1) * 64] = cst[:, C_CS:C_CS + 64]
    return cst


def _fm(v):
    return np.ascontiguousarray(np.asarray(v, np.float32).reshape(8, 128).T)


_NC_CACHE = {}


def kernel(x_prompt, x_sample, state_gla, state_conv, gla_w_in, gla_w_gate_up, gla_b_gate, gla_norm_g,
           gla_w_o, conv_w_in, conv_w_conv, conv_w_out, mlp_w_up, mlp_w_down, ln1_g, ln1_b, ln2_g, ln2_b):
    f = lambda a: np.ascontiguousarray(np.asarray(a, dtype=np.float32))
    prm = np.zeros((128, 98), np.float32)
    prm[:, 96] = LN_EPS
    prm[:, 97] = RMS_EPS
    for l in range(2):
        for w_, arr in enumerate((ln1_g, ln1_b, ln2_g, ln2_b)):
            prm[:, (l * 4 + w_) * 8:(l * 4 + w_ + 1) * 8] = _fm(np.asarray(arr)[l])
    prm[:, 64:72] = _fm(np.asarray(gla_norm_g)[0].reshape(-1))
    for j in range(3):
        prm[:, 72 + j * 8:72 + (j + 1) * 8] = _fm(np.asarray(conv_w_conv)[0, j])
    cst = _consts()
    shared = {
        "gla_w_in": f(gla_w_in[0]), "gla_w_gate_up": f(gla_w_gate_up[0]), "gla_b_gate": f(gla_b_gate[0]).reshape(1, 512),
        "gla_w_o": f(gla_w_o[0]), "conv_w_in": f(conv_w_in[0]), "conv_w_out": f(conv_w_out[0]),
        "mlp_w_up": f(mlp_w_up), "mlp_w_down": f(mlp_w_down), "prm": prm, "cst": cst,
    }
    xpr = f(x_prompt)
    xsm = f(x_sample)
    sgl = f(state_gla)
    scn = f(state_conv)
    in_maps = []
    for c in range(8):
        m = dict(shared)
        m["xp"] = xpr[c]
        m["xs"] = xsm[16 * c:16 * c + 16].reshape(64, D)
        m["sg"] = sgl[0, 16 * c:16 * c + 16]
        m["sc"] = scn[0, 16 * c:16 * c + 16].reshape(32, D)
        in_maps.append(m)
    if "nc" not in _NC_CACHE:
        _NC_CACHE["nc"] = build_program()
    nc = _NC_CACHE["nc"]
    res = run_bass_kernel_spmd(nc, in_maps, core_ids=list(range(8)))
    r = res.results
    y_prompt = np.stack([r[c]["yp"] for c in range(8)], 0).astype(np.float32)
    y_sample = np.concatenate([r[c]["ys"].reshape(16, 4, D) for c in range(8)], 0).astype(np.float32)
    gla_p = np.stack([r[c]["gp"] for c in range(8)], 0)[None].astype(np.float32)
    gla_s = np.concatenate([r[c]["gs"] for c in range(8)], 0)[None].astype(np.float32)
    conv_p = np.stack([r[c]["cp"] for c in range(8)], 0)[None].astype(np.float32)
    conv_s = np.concatenate([r[c]["cs"].reshape(16, 2, D) for c in range(8)], 0)[None].astype(np.float32)
    return (y_prompt, y_sample, gla_p, gla_s, conv_p, conv_s)
```

```python
import os
import numpy as np
from contextlib import ExitStack
import concourse.bass as bass
import concourse.mybir as mybir
from concourse.bass_utils import run_bass_kernel_spmd

F32 = mybir.dt.float32
BF16 = mybir.dt.bfloat16
AF = mybir.ActivationFunctionType
ALU = mybir.AluOpType

D = 1024
NCH = 8
NPT = 512
NSAMP = 64
NSEQ = 16
NST = 4
NTMAX = NPT + NSAMP
ALPHA = 4.0 ** 0.25
LN_EPS = 1e-5
RMS_EPS = 1e-6
QSCALE = 128.0 ** -0.5
NSLOT = 3
NWARM_LN = int(os.environ.get("KWLN", "12"))
NWARM_WO = int(os.environ.get("KWWO", "14"))
NWARM_CORE = int(os.environ.get("KWCORE", "0"))
SPLIT = 0

C_ID, C_UP, C_M2P, C_CP, C_US, C_M2S, C_CS, C_SEQ = 0, 128, 256, 384, 512, 640, 768, 896
C_CP4 = 912
C_CS4 = 1424
C_TOT = 1680


LEVEL = int(os.environ.get("KLEVEL", "99"))


class _Stop(Exception):
    pass


def chk(l):
    if LEVEL < l:
        raise _Stop()


class Own:
    def __init__(self, sem):
        self.sem = sem
        self.n = 0


class Eng(Own):
    def __init__(self, e, sem, is_pe=False):
        super().__init__(sem)
        self.e = e
        self.seen = {}
        self.is_pe = is_pe


SCOPES = []


class T:
    __slots__ = ("w", "r", "excl")

    def __init__(self):
        self.w = None
        self.r = {}
        self.excl = False
        if SCOPES:
            SCOPES[-1].append(self)


class Scope(ExitStack):
    def __init__(self, k):
        super().__init__()
        self.k = k

    def __enter__(self):
        SCOPES.append([])
        return super().__enter__()

    def __exit__(self, *a):
        lst = SCOPES.pop()
        if a[0] is None:
            self.k.scope_barrier(lst)
        return super().__exit__(*a)


def run(*gens):
    gens = [g for g in gens if g is not None]
    while gens:
        for g in list(gens):
            try:
                next(g)
            except StopIteration:
                gens.remove(g)


def TS(*shape):
    if len(shape) == 1:
        return [T() for _ in range(shape[0])]
    return [TS(*shape[1:]) for _ in range(shape[0])]


class K:
    def __init__(self, nc, st):
        self.nc = nc
        self.st = st
        self._nm = 0
        self.pe = Eng(nc.tensor, self.sem("s_pe"), is_pe=True)
        self.act = Eng(nc.scalar, self.sem("s_act"))
        self.dve = Eng(nc.vector, self.sem("s_dve"))
        self.pool = Eng(nc.gpsimd, self.sem("s_pool"))
        self.sp = Eng(nc.sync, self.sem("s_sp"))
        self.engs = [self.pe, self.act, self.dve, self.pool, self.sp]
        self.dsems = []

    def sem(self, name):
        return self.st.enter_context(self.nc.semaphore(name))

    def dsem(self, name, bar=True):
        d = Own(self.sem(name))
        d.bar = bar
        self.dsems.append(d)
        return d

    def uname(self, base):
        self._nm += 1
        return "%s_%d" % (base, self._nm)

    def _wait(self, E, tok):
        if tok is None:
            return
        o, c = tok
        if o is E and E.is_pe:
            return
        if E.seen.get(o, 0) >= c:
            return
        E.e.wait_ge(o.sem, c)
        E.seen[o] = c

    def _deps(self, E, reads, writes):
        ex = [t for t in reads if t.excl]
        if ex:
            reads = [t for t in reads if not t.excl]
            writes = list(writes) + ex
        for t in reads:
            self._wait(E, t.w)
        for t in writes:
            self._wait(E, t.w)
            for o, c in list(t.r.items()):
                self._wait(E, (o, c))

    def _commit(self, tok, reads, writes):
        o, c = tok
        ex = [t for t in reads if t.excl]
        if ex:
            reads = [t for t in reads if not t.excl]
            writes = list(writes) + ex
        for t in reads:
            if t.r.get(o, 0) < c:
                t.r[o] = c
        for t in writes:
            t.w = tok
            t.r = {}

    def op(self, E, fn, reads=(), writes=()):
        self._deps(E, reads, writes)
        ins = fn(E.e)
        E.n += 1
        ins.then_inc(E.sem, 1)
        tok = (E, E.n)
        self._commit(tok, reads, writes)
        return tok

    def mm(self, fns, reads=(), writes=(), warm=()):
        E = self.pe
        self._deps(E, reads, writes)
        for f in warm:
            f(E.e)
        ins = None
        late = []
        for f in fns:
            if isinstance(f, tuple):
                f, lr = f
                for t in lr:
                    self._wait(E, t.w)
                late += lr
            ins = f(E.e)
        E.n += 1
        ins.then_inc(E.sem, 1)
        tok = (E, E.n)
        self._commit(tok, list(reads) + late, writes)
        return tok

    def dma(self, Q, ds, out, in_, reads=(), writes=()):
        self._deps(Q, reads, writes)
        Q.e.dma_start(out=out, in_=in_).then_inc(ds.sem, 16)
        ds.n += 16
        tok = (ds, ds.n)
        self._commit(tok, reads, writes)
        return tok

    def scope_barrier(self, lst):
        for E in self.engs:
            for t in lst:
                self._wait(E, t.w)
                for o, c in list(t.r.items()):
                    self._wait(E, (o, c))

    def barrier(self):
        comp = [self.pe, self.act, self.dve, self.pool]
        for E in [self.pe, self.act, self.dve, self.pool]:
            for P in comp:
                if P.n > 0:
                    self._wait(E, (P, P.n))
            for d in self.dsems:
                if d.n > 0 and d.bar:
                    self._wait(E, (d, d.n))


def build_program(sts=tuple(range(NST))):
    del SCOPES[:]
    nc = bass.Bass("TRN2", target_bir_lowering=False)
    din = lambda name, shape: nc.dram_tensor(name, list(shape), F32, kind="ExternalInput").ap()
    dout = lambda name, shape: nc.dram_tensor(name, list(shape), F32, kind="ExternalOutput").ap()
    xp = din("xp", (2048, D))
    xs = din("xs", (NSAMP, D))
    sg = din("sg", (NSEQ, 4, 128, 256))
    scv = din("sc", (2 * NSEQ, D))
    w_in0 = din("gla_w_in", (D, 3088))
    w_gu = din("gla_w_gate_up", (16, 512))
    b_gt = din("gla_b_gate", (1, 512))
    w_o0 = din("gla_w_o", (D, D))
    cw_in = din("conv_w_in", (D, 3 * D))
    cw_out = din("conv_w_out", (D, D))
    w_up = din("mlp_w_up", (2, D, 4 * D))
    w_dn = din("mlp_w_down", (2, 4 * D, D))
    prm_d = din("prm", (128, 98))
    cst_d = din("cst", (128, C_TOT))
    yp = dout("yp", (2048, D))
    ys = dout("ys", (NSAMP, D))
    gp = dout("gp", (4, 128, 256))
    gs = dout("gs", (NSEQ, 4, 128, 256))
    cp = dout("cp", (2, D))
    cs = dout("cs", (2 * NSEQ, D))

    with ExitStack() as st:
        k = K(nc, st)
        pe, act, dve, pool, sp = k.pe, k.act, k.dve, k.pool, k.sp

        def sb(stack, name, shape, dt):
            return stack.enter_context(nc.sbuf_tensor(k.uname(name), list(shape), dt))

        xf = sb(st, "xf", (128, NCH, NTMAX), F32)
        xb = sb(st, "xb", (128, NCH, NTMAX), BF16)
        ring = sb(st, "ring", (128, NSLOT, NCH, 1024), BF16)
        glw = sb(st, "glw", (128, NCH, 16), BF16)
        wg = sb(st, "wg", (17, 512), F32)
        bgr = sb(st, "bgr", (1, 512), F32)
        onesr = sb(st, "onesr", (1, 128), F32)
        prm = sb(st, "prm", (128, 98), F32)
        cst = sb(st, "cst", (128, C_TOT), F32)
        ones_ln = sb(st, "ones_ln", (128, 128), BF16)
        ones_rms = sb(st, "ones_rms", (128, 128), BF16)
        zsq = sb(st, "zsq", (128, NCH, NTMAX), BF16)
        ln_mean = sb(st, "ln_mean", (128, NTMAX), F32)
        ln_var = sb(st, "ln_var", (128, NTMAX), F32)
        ln_rstd = sb(st, "ln_rstd", (128, NTMAX), F32)
        NTMP = 2
        ln_tmp = [sb(st, "ln_tmp", (128, NTMAX), F32) for _ in range(NTMP)]
        S32 = sb(st, "S32", (128, 4, 256), F32)
        Sbf = [sb(st, "Sbf", (128, 4, 256), BF16) for _ in range(3)]
        tailb = sb(st, "tailb", (128, NCH, 2), F32)
        xf_t = TS(NCH, 5)
        xb_t = TS(NCH, 5)
        ring_t = TS(NSLOT)
        ring_ds = [k.dsem("d_ring%d" % i, bar=False) for i in range(NSLOT)]
        c_t = T()
        c_ds = k.dsem("d_cst")
        S32_t = TS(4)
        Sbf_t = TS(3)
        gsb = {"n": 0}
        tail_t = T()
        zsq_t = TS(NCH, 5)
        mean_t, var_t, rstd_t = T(), T(), T()
        tmp_t = TS(NTMP)

        pb = [st.enter_context(nc.psum_tensor("pb%d" % i, [128, 512], F32)) for i in range(8)]
        pb_t = TS(8)
        for t_ in pb_t:
            t_.excl = True

        ident = cst[:, C_ID:C_ID + 128]

        def prm_col(idx):
            return prm[:, idx:idx + 1]

        k.dma(sp, c_ds, cst[:], cst_d[:, :], writes=[c_t])
        k.dma(sp, c_ds, prm[:], prm_d[:, :], writes=[c_t])
        k.dma(sp, c_ds, wg[0:16, :], w_gu[:, :], writes=[c_t])
        k.dma(sp, c_ds, wg[16:17, :], b_gt[:, :], writes=[c_t])
        c2_ds = k.dsem("d_cst2")
        k.dma(pool, c2_ds, glw[:], w_in0[:, 3072:3088].rearrange("(k p) n -> p k n", p=128), writes=[c_t])
        k.op(dve, lambda e: e.memset(onesr[:], 1.0), writes=[c_t])
        k.op(dve, lambda e: e.memset(ones_ln[:], 1.0 / 1024.0), writes=[c_t])
        k.op(dve, lambda e: e.memset(ones_rms[:], 1.0 / 256.0), writes=[c_t])
        k.op(dve, lambda e: e.memset(S32[:], 0.0), writes=S32_t)
        k.op(dve, lambda e: e.memset(Sbf[0][:], 0.0), writes=[Sbf_t[0]])
        k.op(dve, lambda e: e.memset(tailb[:], 0.0), writes=[tail_t])
        k.barrier()

        def slab_list():
            L = []
            for s_ in sts:
                L += [w_in0[:, 1024:2048], w_in0[:, 0:1024], w_in0[:, 2048:3072], w_o0[:, :]]
                for q in range(4):
                    L += [w_up[0, :, q * 1024:(q + 1) * 1024], w_dn[0, q * 1024:(q + 1) * 1024, :]]
                L += [cw_in[:, 1024:2048], cw_in[:, 2048:3072], cw_in[:, 0:1024], cw_out[:, :]]
                for q in range(4):
                    L += [w_up[1, :, q * 1024:(q + 1) * 1024], w_dn[1, q * 1024:(q + 1) * 1024, :]]
            return L

        slabs = slab_list()
        sl = {"next_load": 0, "next_use": 0}

        def load_next():
            j = sl["next_load"]
            if j >= len(slabs):
                return
            s_ = j % NSLOT
            k.dma(pool, ring_ds[s_], ring[:, s_], slabs[j].rearrange("(k p) n -> p k n", p=128), writes=[ring_t[s_]])
            sl["next_load"] = j + 1

        def next_slab():
            j = sl["next_use"]
            while sl["next_load"] < min(j + NSLOT, len(slabs)):
                load_next()
            sl["next_use"] = j + 1
            s_ = j % NSLOT
            return ring[:, s_], ring_t[s_]

        misc_rr = {"i": 0}

        def misc_bank():
            i = 4 + (misc_rr["i"] % 4)
            misc_rr["i"] += 1
            return pb[i], pb_t[i]

        xin = [sb(st, "xin", (128, D), F32) for _ in range(2)]
        xin_t = TS(2)
        xin_ds = [k.dsem("d_xin%d" % i) for i in range(2)]
        x_pref = {}

        def x_load(stn_, i):
            nsamp_ = NSAMP if stn_ == NST - 1 else 0
            t0, rows = (i * 128, 128) if i < 4 else (NPT, nsamp_)
            src = xp[stn_ * NPT + t0: stn_ * NPT + t0 + rows, :] if i < 4 else xs[0:rows, :]
            k.dma(sp, xin_ds[i % 2], xin[i % 2][0:rows, :], src, writes=[xin_t[i % 2]])
            x_pref[(stn_, i)] = True

        def prefetch_x(stn_):
            for i in range(2):
                if (stn_, i) not in x_pref:
                    x_load(stn_, i)

        def stage_X(stn_):
            nsamp_ = NSAMP if stn_ == NST - 1 else 0
            ttiles_ = [(i * 128, 128) for i in range(4)] + ([(NPT, nsamp_)] if nsamp_ else [])
            if True:
                for i, (t0, rows) in enumerate(ttiles_):
                    b = i % 2
                    if (stn_, i) not in x_pref:
                        x_load(stn_, i)
                    for half in range(2):
                        pt, pt_t = misc_bank()
                        k.mm([lambda e, j=j: e.transpose(out=pt[:, j * 128:j * 128 + rows],
                                                         in_=xin[b][0:rows, (half * 4 + j) * 128:(half * 4 + j + 1) * 128],
                                                         identity=cst[0:rows, C_ID:C_ID + rows]) for j in range(4)],
                             reads=[xin_t[b], c_t], writes=[pt_t])
                        cs_ = range(half * 4, half * 4 + 4)
                        src_ps = pt[:, :].rearrange("p (a b) -> p a b", a=4)[:, :, 0:rows]
                        if half == 0:
                            k.op(act, lambda e: e.activation(out=xf[:, half * 4:half * 4 + 4, t0:t0 + rows], in_=src_ps, func=AF.Copy),
                                 reads=[pt_t], writes=[xf_t[c][i] for c in cs_])
                            k.op(dve, lambda e: e.tensor_copy(out=xb[:, half * 4:half * 4 + 4, t0:t0 + rows], in_=xf[:, half * 4:half * 4 + 4, t0:t0 + rows]),
                                 reads=[xf_t[c][i] for c in cs_], writes=[xb_t[c][i] for c in cs_])
                        else:
                            k.op(dve, lambda e: e.tensor_copy(out=xf[:, half * 4:half * 4 + 4, t0:t0 + rows], in_=src_ps),
                                 reads=[pt_t], writes=[xf_t[c][i] for c in cs_])
                            k.op(act, lambda e: e.activation(out=xb[:, half * 4:half * 4 + 4, t0:t0 + rows], in_=xf[:, half * 4:half * 4 + 4, t0:t0 + rows], func=AF.Copy),
                                 reads=[xf_t[c][i] for c in cs_], writes=[xb_t[c][i] for c in cs_])

        pend = None
        for stn in sts:
          try:
            nsamp = NSAMP if stn == NST - 1 else 0
            NT = NPT + nsamp
            ntiles = [(0, NPT)] + ([(NPT, nsamp)] if nsamp else [])
            ttiles = [(i * 128, 128) for i in range(4)] + ([(NPT, nsamp)] if nsamp else [])
            if SPLIT:
                halfA = [(0, 256)]
                halfB = [(256, 256)] + ([(NPT, nsamp)] if nsamp else [])
            else:
                halfA, halfB = ntiles, []

            def tiles_of(n0, nn):
                return [i for i, (t0_, r_) in enumerate(ttiles) if n0 <= t0_ < n0 + nn]

            def xtr(tr, cs_, n0, nn):
                return [tr[c][i] for c in cs_ for i in tiles_of(n0, nn)]

            def linear_A_gen(slab, slab_t, src, src_tr, evac, ntl=None, nsets=None, nwarm=0):
                ntl = ntiles if ntl is None else ntl
                if not ntl:
                    return
                if nsets is None:
                    nsets = 4 if len(ntl) == 1 else 2
                for m in range(NCH):
                    set_ = m % nsets
                    banks = [(pb[set_ + 2 * ni], pb_t[set_ + 2 * ni]) for ni in range(len(ntl))]
                    fns = []
                    for kk_ in range(NCH):
                        for ni, (n0, nn) in enumerate(ntl):
                            fns.append((lambda e, kk_=kk_, ni=ni, n0=n0, nn=nn, m=m: e.matmul(
                                banks[ni][0][:, 0:nn], lhsT=slab[:, kk_, m * 128:(m + 1) * 128],
                                rhs=src[:, kk_, n0:n0 + nn], start=(kk_ == 0), stop=(kk_ == NCH - 1)),
                                [src_tr[kk_][i] for i in tiles_of(n0, nn)]))
                    warm = []
                    if m == 0 and nwarm:
                        warm = [lambda e: e.matmul(banks[0][0][:, 0:512], lhsT=slab[:, 0, 0:128], rhs=slab[:, 1, 0:512], start=True, stop=True,
                                                   skip_group_check=True) for _ in range(nwarm)]
                    k.mm(fns, reads=[slab_t], writes=[b[1] for b in banks], warm=warm)
                    for ni, (n0, nn) in enumerate(ntl):
                        evac(m, ni, n0, nn, banks[ni][0], banks[ni][1])
                    yield

            def linear_A(*a, **kw):
                for _ in linear_A_gen(*a, **kw):
                    pass

            def linear_after_ln(slab, slab_t, src, src_tr, evac):
                n0, nn = ntiles[0]
                til = tiles_of(n0, nn)
                for kk_ in range(NCH):
                    warm = []
                    if kk_ == 0:
                        warm = [lambda e, j=j: e.matmul(pb[j % NCH][:, 0:512], lhsT=slab[:, 0, 0:128], rhs=slab[:, 1, 0:512], start=True, stop=True,
                                                        skip_group_check=True) for j in range(NWARM_LN)]
                    k.mm([((lambda e, kk_=kk_, m=m: e.matmul(pb[m][:, 0:nn], lhsT=slab[:, kk_, m * 128:(m + 1) * 128], rhs=src[:, kk_, n0:n0 + nn],
                                                           start=(kk_ == 0), stop=(kk_ == NCH - 1), skip_group_check=True)),
                           ([src_tr[kk_][i] for i in til] if m == 0 else [])) for m in range(NCH)],
                         reads=[slab_t], writes=pb_t, warm=warm)
                for m in range(NCH):
                    evac(m, 0, n0, nn, pb[m], pb_t[m])
                if len(ntiles) > 1:
                    linear_A(slab, slab_t, src, src_tr, evac, ntl=ntiles[1:])

            def make_z_evac(first=True, last=True):
                def evac(m, ni, n0, nn, ps, ps_t):
                    xt = xtr(xf_t, [m], n0, nn)
                    if first:
                        k.op(dve, lambda e: e.scalar_tensor_tensor(out=xf[:, m, n0:n0 + nn], in0=xf[:, m, n0:n0 + nn], scalar=ALPHA,
                                                                   in1=ps[:, 0:nn], op0=ALU.mult, op1=ALU.add),
                             reads=[ps_t], writes=xt)
                    else:
                        k.op(dve, lambda e: e.tensor_tensor(out=xf[:, m, n0:n0 + nn], in0=xf[:, m, n0:n0 + nn], in1=ps[:, 0:nn], op=ALU.add),
                             reads=[ps_t], writes=xt)
                    if last:
                        k.op(act, lambda e: e.activation(out=xb[:, m, n0:n0 + nn], in_=xf[:, m, n0:n0 + nn], func=AF.Copy),
                             reads=xt, writes=xtr(xb_t, [m], n0, nn))
                        k.op(act, lambda e: e.activation(out=zsq[:, m, n0:n0 + nn], in_=xf[:, m, n0:n0 + nn], func=AF.Square),
                             reads=xt, writes=xtr(zsq_t, [m], n0, nn))
                return evac

            def layer_norm_gen(ntl, gcol, bcol, want_bf=True):
                mean, var, rstd, tmp = ln_mean, ln_var, ln_rstd, ln_tmp
                if not ntl:
                    return
                a0 = ntl[0][0]
                tot = sum(nn for _, nn in ntl)
                for ni, (n0, nn) in enumerate(ntl):
                    o0 = n0 - a0
                    pm, pm_t = misc_bank()
                    pq, pq_t = misc_bank()
                    k.mm([(lambda e, c=c: e.matmul(pm[:, 0:nn], lhsT=ones_ln[:], rhs=xb[:, c, n0:n0 + nn], start=(c == 0), stop=(c == NCH - 1)),
                           xtr(xb_t, [c], n0, nn)) for c in range(NCH)], reads=[c_t], writes=[pm_t])
                    k.mm([(lambda e, c=c: e.matmul(pq[:, 0:nn], lhsT=ones_ln[:], rhs=zsq[:, c, n0:n0 + nn], start=(c == 0), stop=(c == NCH - 1)),
                           xtr(zsq_t, [c], n0, nn)) for c in range(NCH)], reads=[c_t], writes=[pq_t])
                    k.op(act, lambda e: e.activation(out=var[:, o0:o0 + nn], in_=pm[:, 0:nn], func=AF.Square), reads=[pm_t], writes=[var_t])
                    k.op(dve, lambda e: e.tensor_tensor(out=var[:, o0:o0 + nn], in0=pq[:, 0:nn], in1=var[:, o0:o0 + nn], op=ALU.subtract),
                         reads=[pq_t, var_t], writes=[var_t])
                    k.op(dve, lambda e: e.tensor_copy(out=mean[:, o0:o0 + nn], in_=pm[:, 0:nn]), reads=[pm_t], writes=[mean_t])
                n0, nn = a0, tot
                k.op(act, lambda e: e.activation(out=rstd[:, 0:nn], in_=var[:, 0:nn], func=AF.Ln, bias=prm_col(96)),
                     reads=[var_t, c_t], writes=[rstd_t])
                k.op(act, lambda e: e.activation(out=rstd[:, 0:nn], in_=rstd[:, 0:nn], func=AF.Exp, scale=-0.5), reads=[rstd_t], writes=[rstd_t])
                yield
                for c in range(NCH):
                    tb, tb_t = tmp[c % NTMP], tmp_t[c % NTMP]
                    xt = xtr(xf_t, [c], n0, nn)
                    k.op(dve, lambda e: e.tensor_tensor(out=tb[:, 0:nn], in0=xf[:, c, n0:n0 + nn], in1=mean[:, 0:nn], op=ALU.subtract),
                         reads=xt + [mean_t], writes=[tb_t])
                    k.op(dve, lambda e: e.scalar_tensor_tensor(out=tb[:, 0:nn], in0=tb[:, 0:nn], scalar=prm_col(gcol + c), in1=rstd[:, 0:nn],
                                                               op0=ALU.mult, op1=ALU.mult),
                         reads=[rstd_t, tb_t, c_t], writes=[tb_t])
                    if want_bf:
                        k.op(act, lambda e: e.activation(out=xb[:, c, n0:n0 + nn], in_=tb[:, 0:nn], func=AF.Identity, bias=prm_col(bcol + c)),
                             reads=[tb_t, c_t], writes=xtr(xb_t, [c], n0, nn))
                    k.op(act, lambda e: e.activation(out=xf[:, c, n0:n0 + nn], in_=tb[:, 0:nn], func=AF.Identity, bias=prm_col(bcol + c)),
                         reads=[tb_t, c_t], writes=xt)
                    yield

            def mlp_stage(layer, final=False, pre=None):
                g2, b2 = (layer * 4 + 2) * 8, (layer * 4 + 3) * 8
                with Scope(k) as ms:
                    hq = [sb(ms, "hq", (128, NCH, NTMAX), BF16) for _ in range(2)]
                    hq_t = [TS(NCH, 5) for _ in range(2)]
                    sqt = [sb(ms, "sqt", (128, 512), F32) for _ in range(2)]
                    sqt_t = TS(2)
                    cnt = {"i": 0}
                    for q in range(4):
                        hb, hb_t = hq[q % 2], hq_t[q % 2]
                        slab, slab_t = next_slab()

                        def evac_up(m, ni, n0, nn, ps, ps_t, hb=hb, hb_t=hb_t):
                            j = cnt["i"] % 2
                            cnt["i"] += 1
                            k.op(act, lambda e: e.activation(out=sqt[j][:, 0:nn], in_=ps[:, 0:nn], func=AF.Square), reads=[ps_t], writes=[sqt_t[j]])
                            k.op(dve, lambda e: e.scalar_tensor_tensor(out=hb[:, m, n0:n0 + nn], in0=ps[:, 0:nn], scalar=0.0, in1=sqt[j][:, 0:nn],
                                                                       op0=ALU.is_gt, op1=ALU.mult),
                                 reads=[ps_t, sqt_t[j]], writes=[hb_t[m][i] for i in tiles_of(n0, nn)])

                        if q == 0 and SPLIT:
                            run(linear_A_gen(slab, slab_t, xb, xb_t, evac_up, ntl=halfA), pre)
                            linear_A(slab, slab_t, xb, xb_t, evac_up, ntl=halfB)
                        elif q == 0:
                            run(pre)
                            linear_after_ln(slab, slab_t, xb, xb_t, evac_up)
                        else:
                            linear_A(slab, slab_t, xb, xb_t, evac_up)
                        slab, slab_t = next_slab()
                        if q < 3 or not SPLIT:
                            linear_A(slab, slab_t, hb, hb_t, make_z_evac(first=(q == 0), last=(q == 3)))
                        else:
                            linear_A(slab, slab_t, hb, hb_t, make_z_evac(first=False, last=True), ntl=halfA)
                            run(linear_A_gen(slab, slab_t, hb, hb_t, make_z_evac(first=False, last=True), ntl=halfB),
                                layer_norm_gen(halfA, g2, b2, want_bf=not final))
                return layer_norm_gen(halfB if SPLIT else ntiles, g2, b2, want_bf=not final)

            chk(1)
            if stn == sts[0]:
                run(pend)
                pend = None
                stage_X(stn)
            chk(2)
            with Scope(k) as ss:
                og = sb(ss, "og", (128, NCH, NTMAX), BF16)
                og_t = TS(NCH, 5)
                with Scope(k) as s2:
                    glT = sb(s2, "glT", (32, NTMAX), F32)
                    eb = sb(s2, "eb", (128, 4, NTMAX), F32)
                    enb = sb(s2, "enb", (128, 4, NTMAX), F32)
                    e1 = [sb(s2, "e1", (128, 512), F32) for _ in range(1)]
                    spt = [sb(s2, "spt", (128, 512), F32) for _ in range(1)]
                    ec = sb(s2, "ec", (128, 5, 512), F32)
                    vt = sb(s2, "vt", (128, 5, 1024), BF16)
                    qd = sb(s2, "qd", (128, 4, NTMAX), BF16)
                    kd = sb(s2, "kd", (128, 4, NTMAX), BF16)
                    qd32 = sb(s2, "qd32", (128, 4, NSAMP), F32)
                    kkt = sb(s2, "kkt", (128, 5, 512), BF16)
                    glT_t = TS(5)
                    eb_t = TS(4, 5)
                    enb_t = TS(4, 5)
                    e1_t, spt_t = TS(1), TS(1)
                    ec_t, vt_t, kkt_t = TS(5), TS(5), TS(5)
                    qd_t, kd_t = TS(4, 5), TS(4, 5)
                    qd32_t = T()

                    k.op(dve, lambda e: e.memset(glT[:], 1.0), writes=glT_t)
                    for ni, (n0, nn) in enumerate(ntiles):
                        pg, pg_t = misc_bank()
                        k.mm([lambda e, c=c: e.matmul(pg[0:16, 0:nn], lhsT=glw[:, c, :], rhs=xb[:, c, n0:n0 + nn], start=(c == 0), stop=(c == NCH - 1))
                              for c in range(NCH)], reads=[c_t] + xtr(xb_t, range(NCH), n0, nn), writes=[pg_t])
                        k.op(dve, lambda e: e.tensor_copy(out=glT[0:16, n0:n0 + nn], in_=pg[0:16, 0:nn]), reads=[pg_t],
                             writes=[glT_t[i] for i in tiles_of(n0, nn)])
                    chk(3)
                    def gate_gen():
                        for i, (t0, rows) in enumerate(ttiles):
                            b = 0
                            co_u, co_m = (C_UP, C_M2P) if i < 4 else (C_US, C_M2S)
                            pg, pg_t = misc_bank()
                            k.mm([lambda e: e.matmul(pg[0:rows, :], lhsT=glT[0:17, t0:t0 + rows], rhs=wg[0:17, :], start=True, stop=True)],
                                 reads=[glT_t[i], c_t], writes=[pg_t])
                            k.op(act, lambda e: e.activation(out=e1[b][0:rows, :], in_=pg[0:rows, :], func=AF.Exp, scale=-1.0),
                                 reads=[pg_t], writes=[e1_t[b]])
                            k.op(act, lambda e: e.activation(out=spt[b][0:rows, :], in_=e1[b][0:rows, :], func=AF.Ln, bias=1.0),
                                 reads=[e1_t[b]], writes=[spt_t[b]])
                            pbb, pbb_t = misc_bank()
                            k.mm([lambda e, h=h: e.matmul(pbb[:, h * 128:h * 128 + rows], lhsT=spt[b][0:rows, h * 128:(h + 1) * 128],
                                                          rhs=cst[0:rows, co_u:co_u + rows], start=True, stop=True) for h in range(4)],
                                 reads=[spt_t[b], c_t], writes=[pbb_t])
                            pbv = pbb[:, :].rearrange("p (a b) -> p a b", a=4)[:, :, 0:rows]
                            k.op(act, lambda e: e.activation(out=eb[:, :, t0:t0 + rows], in_=pbv, func=AF.Exp),
                                 reads=[pbb_t], writes=[eb_t[h][i] for h in range(4)])
                            k.op(act, lambda e: e.activation(out=enb[:, :, t0:t0 + rows], in_=pbv, func=AF.Exp, scale=-1.0),
                                 reads=[pbb_t], writes=[enb_t[h][i] for h in range(4)])
                            pc, pc_t = misc_bank()
                            k.mm([lambda e: e.matmul(pc[0:rows, :], lhsT=cst[0:rows, co_m:co_m + rows], rhs=spt[b][0:rows, :], start=True, stop=True)],
                                 reads=[spt_t[b], c_t], writes=[pc_t])
                            k.op(act, lambda e: e.activation(out=ec[0:rows, i, :], in_=pc[0:rows, :], func=AF.Exp),
                                 reads=[pc_t], writes=[ec_t[i]])

                            yield

                    chk(4)
                    bank_rr = {"i": 0}

                    def acc_bank():
                        i_ = bank_rr["i"] % 4
                        bank_rr["i"] += 1
                        return pb[i_], pb_t[i_]

                    slab, slab_t = next_slab()
                    def v_gen():
                        for i, (t0, rows) in enumerate(ttiles):
                            for hh in range(2):
                                pv, pv_t = acc_bank()
                                k.mm([lambda e, c=c: e.matmul(pv[0:rows, :], lhsT=xb[:, c, t0:t0 + rows], rhs=slab[:, c, hh * 512:(hh + 1) * 512],
                                                              start=(c == 0), stop=(c == NCH - 1)) for c in range(NCH)],
                                     reads=[slab_t] + [xb_t[c][i] for c in range(NCH)], writes=[pv_t])
                                eng = act if hh == 0 else dve
                                if hh == 0:
                                    k.op(act, lambda e: e.activation(out=vt[0:rows, i, 0:512], in_=pv[0:rows, :], func=AF.Copy),
                                         reads=[pv_t], writes=[vt_t[i]])
                                else:
                                    k.op(dve, lambda e: e.tensor_copy(out=vt[0:rows, i, 512:1024], in_=pv[0:rows, :]),
                                         reads=[pv_t, vt_t[i]], writes=[vt_t[i]])
                            yield

                    run(v_gen(), gate_gen())

                    chk(5)
                    slab, slab_t = next_slab()

                    def evac_qk(m, ni, n0, nn, ps, ps_t):
                        if m < 4:
                            k.op(dve, lambda e: e.scalar_tensor_tensor(out=qd[:, m, n0:n0 + nn], in0=ps[:, 0:nn], scalar=QSCALE,
                                                                       in1=eb[:, m, n0:n0 + nn], op0=ALU.mult, op1=ALU.mult),
                                 reads=[ps_t] + [eb_t[m][i] for i in tiles_of(n0, nn)], writes=[qd_t[m][i] for i in tiles_of(n0, nn)])
                            if ni == 1:
                                k.op(dve, lambda e: e.scalar_tensor_tensor(out=qd32[:, m, 0:nn], in0=ps[:, 0:nn], scalar=QSCALE,
                                                                           in1=eb[:, m, n0:n0 + nn], op0=ALU.mult, op1=ALU.mult),
                                     reads=[ps_t] + [eb_t[m][4]], writes=[qd32_t])
                        else:
                            h = m - 4
                            k.op(dve, lambda e: e.tensor_tensor(out=kd[:, h, n0:n0 + nn], in0=ps[:, 0:nn], in1=enb[:, h, n0:n0 + nn], op=ALU.mult),
                                 reads=[ps_t] + [enb_t[h][i] for i in tiles_of(n0, nn)], writes=[kd_t[h][i] for i in tiles_of(n0, nn)])

                    linear_A(slab, slab_t, xb, xb_t, evac_qk)
                    for i, (t0, rows) in enumerate(ttiles):
                        pk, pk_t = acc_bank()
                        k.mm([lambda e, c=c: e.matmul(pk[0:rows, :], lhsT=xb[:, c, t0:t0 + rows], rhs=slab[:, c, 512:1024],
                                                      start=(c == 0), stop=(c == NCH - 1)) for c in range(NCH)],
                             reads=[slab_t] + [xb_t[c][i] for c in range(NCH)], writes=[pk_t])
                        k.op(dve, lambda e: e.tensor_tensor(out=kkt[0:rows, i, :], in0=pk[0:rows, :], in1=ec[0:rows, i, :], op=ALU.mult),
                             reads=[pk_t, ec_t[i]], writes=[kkt_t[i]])

                    chk(6)
                    slab, slab_t = next_slab()

                    def evac_r(m, ni, n0, nn, ps, ps_t):
                        k.op(act, lambda e: e.activation(out=og[:, m, n0:n0 + nn], in_=ps[:, 0:nn], func=AF.Silu),
                             reads=[ps_t], writes=[og_t[m][i] for i in tiles_of(n0, nn)])

                    linear_A(slab, slab_t, xb, xb_t, evac_r, ntl=halfA)
                    r_slab, r_slab_t = slab, slab_t

                    chk(7)
                    with Scope(k) as s3:
                        sT = sb(s3, "sT", (128, 4, 128), BF16)
                        o32 = sb(s3, "o32", (128, NCH, 128), F32)
                        osq = sb(s3, "osq", (128, NCH, 128), BF16)
                        rsd = sb(s3, "rsd", (128, 512), F32)
                        otm = sb(s3, "otm", (128, NCH, 128), F32)
                        sT_t, o32_t, osq_t, rsd_t, otm_t = T(), TS(2), TS(2), T(), T()

                        def rms_gate(i, t0, rows, pO, pO_t, after=None):
                            povs = [pO[bk][:, :].rearrange("p (a b) -> p a b", a=4)[:, :, 0:rows] for bk in range(2)]
                            for bk in range(2):
                                k.op(act, lambda e: e.activation(out=osq[:, bk * 4:bk * 4 + 4, 0:rows], in_=povs[bk], func=AF.Square),
                                     reads=[pO_t[bk]], writes=[osq_t[bk]])
                            if after is not None:
                                after()
                            pr, pr_t = pb[7], pb_t[7]
                            k.mm([lambda e, h=h, vc=vc: e.matmul(pr[:, h * 128:h * 128 + rows], lhsT=ones_rms[:], rhs=osq[:, h * 2 + vc, 0:rows],
                                                                 start=(vc == 0), stop=(vc == 1)) for h in range(4) for vc in range(2)],
                                 reads=osq_t + [c_t], writes=[pr_t])
                            for bk in range(2):
                                k.op(dve, lambda e: e.tensor_tensor(out=o32[:, bk * 4:bk * 4 + 4, 0:rows], in0=povs[bk],
                                                                    in1=prm[:, 64 + bk * 4:68 + bk * 4].unsqueeze(2).broadcast_to([128, 4, rows]), op=ALU.mult),
                                     reads=[pO_t[bk], c_t], writes=[o32_t[bk]])
                            prv = pr[:, :].rearrange("p (a b) -> p a b", a=4)[:, :, 0:rows]
                            rsv = rsd[:, :].rearrange("p (a b) -> p a b", a=4)[:, :, 0:rows]
                            k.op(act, lambda e: e.activation(out=rsv, in_=prv, func=AF.Ln, bias=prm_col(97)), reads=[pr_t, c_t], writes=[rsd_t])
                            k.op(act, lambda e: e.activation(out=rsv, in_=rsv, func=AF.Exp, scale=-0.5), reads=[rsd_t], writes=[rsd_t])
                            k.op(dve, lambda e: e.tensor_tensor(out=otm[:, :, 0:rows].rearrange("p (h v) i -> p h v i", h=4),
                                                                in0=o32[:, :, 0:rows].rearrange("p (h v) i -> p h v i", h=4),
                                                                in1=rsv.unsqueeze(2).broadcast_to([128, 4, 2, rows]), op=ALU.mult),
                                 reads=o32_t + [rsd_t], writes=[otm_t])
                            k.op(pool, lambda e: e.tensor_tensor(out=og[:, :, t0:t0 + rows], in0=otm[:, :, 0:rows], in1=og[:, :, t0:t0 + rows], op=ALU.mult),
                                 reads=[otm_t], writes=[og_t[c][i] for c in range(NCH)])

                        def core_tile_gen(i):
                            t0, rows = i * 128, 128
                            pS, pS_t = pb[2], pb_t[2]
                            k.mm([lambda e, h=h: e.matmul(pS[:, h * 128:(h + 1) * 128], lhsT=kd[:, h, t0:t0 + 128], rhs=qd[:, h, t0:t0 + 128],
                                                          start=True, stop=True) for h in range(4)],
                                 reads=[kd_t[h][i] for h in range(4)] + [qd_t[h][i] for h in range(4)], writes=[pS_t])
                            k.op(dve, lambda e: e.tensor_tensor(out=sT[:, :, :].rearrange("p h j -> p (h j)"), in0=pS[:, :], in1=cst[:, C_CP4:C_CP4 + 512], op=ALU.mult),
                                 reads=[pS_t, c_t], writes=[sT_t])
                            yield
                            kvb = [(3, 4), (0, 1)]
                            for cc in range(2):
                                r0 = cc * 64
                                for hb in range(2):
                                    bk = kvb[cc][hb]
                                    k.mm([lambda e, h=h: e.matmul(pb[bk][:, (h % 2) * 256:(h % 2 + 1) * 256], lhsT=kkt[r0:r0 + 64, i, h * 128:(h + 1) * 128],
                                                                  rhs=vt[r0:r0 + 64, i, h * 256:(h + 1) * 256], start=True, stop=True)
                                          for h in (2 * hb, 2 * hb + 1)],
                                         reads=[kkt_t[i], vt_t[i]], writes=[pb_t[bk]])
                            nA = gsb["n"]
                            pO = [pb[5], pb[6]]
                            pO_t = [pb_t[5], pb_t[6]]
                            for bk in range(2):
                                fns = []
                                first = True
                                for c in range(bk * 4, bk * 4 + 4):
                                    h, vc = c // 2, c % 2
                                    blk = pO[bk][:, (c % 4) * 128:(c % 4 + 1) * 128]
                                    fns.append(lambda e, blk=blk, h=h, vc=vc, first=first: e.matmul(
                                        blk[:, 0:128], lhsT=vt[:, i, h * 256 + vc * 128:h * 256 + (vc + 1) * 128], rhs=sT[:, h, :],
                                        start=first, stop=False, skip_group_check=True))
                                    first = False
                                    fns.append(lambda e, blk=blk, h=h, vc=vc: e.matmul(
                                        blk[:, 0:64], lhsT=Sbf[nA % 3][:, h, vc * 128:(vc + 1) * 128],
                                        rhs=qd[:, h, t0:t0 + 64], start=False, stop=False, skip_group_check=True))
                                k.mm(fns, reads=[vt_t[i], sT_t, Sbf_t[nA % 3]] + [qd_t[h][i] for h in range(4)], writes=[pO_t[bk]])
                            yield

                            def upd(cc):
                                n_ = gsb["n"]
                                tl = t0 + cc * 64 + 63
                                for h in range(4):
                                    bk = kvb[cc][h // 2]
                                    k.op(dve, lambda e: e.scalar_tensor_tensor(out=S32[:, h, :], in0=S32[:, h, :], scalar=eb[:, h, tl:tl + 1],
                                                                               in1=pb[bk][:, (h % 2) * 256:(h % 2 + 1) * 256], op0=ALU.mult, op1=ALU.add),
                                         reads=[pb_t[bk], eb_t[h][i]], writes=[S32_t[h]])
                                k.op(act, lambda e: e.activation(out=Sbf[(n_ + 1) % 3][:], in_=S32[:], func=AF.Copy),
                                     reads=S32_t, writes=[Sbf_t[(n_ + 1) % 3]])
                                gsb["n"] = n_ + 1

                            upd(0)
                            for bk in range(2):
                                fns = []
                                for c in range(bk * 4, bk * 4 + 4):
                                    h, vc = c // 2, c % 2
                                    blk = pO[bk][:, (c % 4) * 128:(c % 4 + 1) * 128]
                                    fns.append(lambda e, blk=blk, h=h, vc=vc, c=c: e.matmul(
                                        blk[:, 64:128], lhsT=Sbf[(nA + 1) % 3][:, h, vc * 128:(vc + 1) * 128],
                                        rhs=qd[:, h, t0 + 64:t0 + 128], start=False, stop=(c % 4 == 3), skip_group_check=True))
                                if bk == 0 and NWARM_CORE:
                                    fns[0] = (fns[0], [Sbf_t[(nA + 1) % 3]])
                                    warm = [lambda e: e.matmul(pb[7][:, 0:512], lhsT=kd[:, 0, t0:t0 + 128], rhs=qd[:, 0, 0:512], start=True, stop=True,
                                                               skip_group_check=True) for _ in range(NWARM_CORE)]
                                    k.mm(fns, reads=[qd_t[h][j] for h in range(4) for j in range(4)] + [kd_t[0][i]], writes=[pO_t[bk], pb_t[7]], warm=warm)
                                else:
                                    k.mm(fns, reads=[Sbf_t[(nA + 1) % 3]] + [qd_t[h][i] for h in range(4)], writes=[pO_t[bk]])
                            yield
                            rms_gate(i, t0, rows, pO, pO_t, after=lambda: upd(1))
                            yield

                        def core_tiles_gen(tl_):
                            for i_ in tl_:
                                yield from core_tile_gen(i_)

                        if not SPLIT:
                            run(core_tiles_gen([0, 1, 2, 3]))
                        else:
                            if nsamp:
                                linear_A(r_slab, r_slab_t, xb, xb_t, evac_r, ntl=halfB)
                                run(core_tiles_gen([0, 1]))
                            else:
                                run(linear_A_gen(r_slab, r_slab_t, xb, xb_t, evac_r, ntl=halfB, nsets=2), core_tiles_gen([0, 1]))
                            wo_slab, wo_slab_t = next_slab()
                            run(linear_A_gen(wo_slab, wo_slab_t, og, og_t, make_z_evac(), ntl=halfA, nsets=2), core_tiles_gen([2, 3]))
                        chk(8)

                        if nsamp:
                            i, t0, rows = 4, NPT, nsamp
                            k.scope_barrier([t_ for row in enb_t for t_ in row] + ec_t + e1_t + spt_t + vt_t[0:4])
                            enb_fl = enb[:, :, :].rearrange("p a n -> p (a n)")
                            ec_fl = ec[:, :, :].rearrange("p a n -> p (a n)")
                            vt_fl = vt[:, 0:4, :].rearrange("p a n -> p (a n)").bitcast(F32)
                            NB0 = 4
                            S0 = [enb_fl[:, j * 1024:(j + 1) * 1024].rearrange("p (h v) -> p h v", h=4) for j in range(2)] + \
                                 [vt_fl[:, j * 1024:(j + 1) * 1024].rearrange("p (h v) -> p h v", h=4) for j in range(2)]
                            Sn = [ec_fl[:, j * 1024:(j + 1) * 1024].rearrange("p (h v) -> p h v", h=4) for j in range(2)]
                            kkm = [e1[0][:, :].bitcast(BF16), spt[0][:, :].bitcast(BF16)]
                            S0_t, Sn_t, kkm_t = TS(4), TS(2), TS(2)
                            S0_ds = [k.dsem("d_S0_%d" % j) for j in range(4)]
                            Sn_ds = [k.dsem("d_Sn_%d" % j) for j in range(2)]
                            pS, pS_t = pb[2], pb_t[2]
                            k.mm([lambda e, h=h: e.matmul(pS[0:rows, h * 64:(h + 1) * 64], lhsT=kd[:, h, t0:t0 + rows], rhs=qd[:, h, t0:t0 + rows],
                                                          start=True, stop=True) for h in range(4)],
                                 reads=[kd_t[h][i] for h in range(4)] + [qd_t[h][i] for h in range(4)], writes=[pS_t])
                            k.op(dve, lambda e: e.tensor_tensor(out=sT[0:rows, :, 0:rows], in0=pS[0:rows, 0:4 * rows].rearrange("p (h j) -> p h j", h=4),
                                                                in1=cst[0:rows, C_CS4:C_CS4 + 4 * rows].rearrange("p (h j) -> p h j", h=4), op=ALU.mult),
                                 reads=[pS_t, c_t], writes=[sT_t])
                            pO1, pO1_t = pb[5], pb_t[5]
                            fns = []
                            for c in range(NCH):
                                h, vc = c // 2, c % 2
                                fns.append(lambda e, c=c, h=h, vc=vc: e.matmul(
                                    pO1[:, c * 64:(c + 1) * 64], lhsT=vt[0:rows, i, h * 256 + vc * 128:h * 256 + (vc + 1) * 128], rhs=sT[0:rows, h, 0:rows],
                                    start=(c == 0), stop=False, skip_group_check=True))
                            k.mm(fns, reads=[vt_t[i], sT_t], writes=[pO1_t])
                            for s_ in range(NB0):
                                k.dma(sp, S0_ds[s_], S0[s_], sg[s_].rearrange("h d v -> d h v"), writes=[S0_t[s_]])
                            def sample_seq_gen():
                                for s_ in range(NSEQ):
                                    b = s_ % 2
                                    b0 = s_ % NB0
                                    fns = []
                                    for c in range(NCH):
                                        h, vc = c // 2, c % 2
                                        fns.append(lambda e, c=c, h=h, vc=vc: e.matmul(
                                            pO1[:, c * 64 + 4 * s_:c * 64 + 4 * s_ + 4], lhsT=S0[b0][:, h, vc * 128:(vc + 1) * 128],
                                            rhs=qd32[:, h, 4 * s_:4 * s_ + 4], start=False, stop=(s_ == NSEQ - 1 and c == NCH - 1),
                                            skip_group_check=True))
                                    k.mm(fns, reads=[S0_t[b0], qd32_t], writes=[pO1_t])
                                    k.op(act, lambda e: e.activation(out=kkm[b][0:rows, 0:512], in_=kkt[0:rows, i, :], func=AF.Identity,
                                                                     scale=cst[0:rows, C_SEQ + s_:C_SEQ + s_ + 1]),
                                         reads=[kkt_t[i], c_t], writes=[kkm_t[b]])
                                    for hb in range(2):
                                        bk = 3 + hb
                                        k.mm([lambda e, h=h: e.matmul(pb[bk][:, (h % 2) * 256:(h % 2 + 1) * 256], lhsT=kkm[b][0:rows, h * 128:(h + 1) * 128],
                                                                      rhs=vt[0:rows, i, h * 256:(h + 1) * 256], start=True, stop=True)
                                              for h in (2 * hb, 2 * hb + 1)],
                                             reads=[kkm_t[b], vt_t[i]], writes=[pb_t[bk]])
                                    tl = t0 + 4 * s_ + 3
                                    for h in range(4):
                                        bk = 3 + h // 2
                                        k.op(dve, lambda e: e.scalar_tensor_tensor(out=Sn[b][:, h, :], in0=S0[b0][:, h, :], scalar=eb[:, h, tl:tl + 1],
                                                                                   in1=pb[bk][:, (h % 2) * 256:(h % 2 + 1) * 256], op0=ALU.mult, op1=ALU.add),
                                             reads=[pb_t[bk], eb_t[h][i], S0_t[b0]], writes=[Sn_t[b]])
                                    k.dma(pool, Sn_ds[b], gs[s_].rearrange("h d v -> d h v"), Sn[b], reads=[Sn_t[b]])
                                    if s_ + NB0 < NSEQ:
                                        k.dma(sp, S0_ds[b0], S0[b0], sg[s_ + NB0].rearrange("h d v -> d h v"), writes=[S0_t[b0]])
                                    yield

                            wo_slab, wo_slab_t = next_slab()
                            run(linear_A_gen(wo_slab, wo_slab_t, og, og_t, make_z_evac(), ntl=ntiles[0:1], nsets=2, nwarm=NWARM_WO), sample_seq_gen())
                            pov = pO1[:, :].rearrange("p (a b) -> p a b", a=8)
                            k.op(act, lambda e: e.activation(out=osq[:, :, 0:rows], in_=pov, func=AF.Square), reads=[pO1_t], writes=osq_t)
                            pr, pr_t = pb[7], pb_t[7]
                            k.mm([lambda e, h=h, vc=vc: e.matmul(pr[:, h * 128:h * 128 + rows], lhsT=ones_rms[:], rhs=osq[:, h * 2 + vc, 0:rows],
                                                                 start=(vc == 0), stop=(vc == 1)) for h in range(4) for vc in range(2)],
                                 reads=osq_t + [c_t], writes=[pr_t])
                            k.op(dve, lambda e: e.tensor_tensor(out=o32[:, :, 0:rows], in0=pov, in1=prm[:, 64:72].unsqueeze(2).broadcast_to([128, NCH, rows]), op=ALU.mult),
                                 reads=[pO1_t, c_t], writes=o32_t)
                            prv = pr[:, :].rearrange("p (a b) -> p a b", a=4)[:, :, 0:rows]
                            rsv = rsd[:, :].rearrange("p (a b) -> p a b", a=4)[:, :, 0:rows]
                            k.op(act, lambda e: e.activation(out=rsv, in_=prv, func=AF.Ln, bias=prm_col(97)), reads=[pr_t, c_t], writes=[rsd_t])
                            k.op(act, lambda e: e.activation(out=rsv, in_=rsv, func=AF.Exp, scale=-0.5), reads=[rsd_t], writes=[rsd_t])
                            k.op(dve, lambda e: e.tensor_tensor(out=otm[:, :, 0:rows].rearrange("p (h v) i -> p h v i", h=4),
                                                                in0=o32[:, :, 0:rows].rearrange("p (h v) i -> p h v i", h=4),
                                                                in1=rsv.unsqueeze(2).broadcast_to([128, 4, 2, rows]), op=ALU.mult),
                                 reads=o32_t + [rsd_t], writes=[otm_t])
                            k.op(pool, lambda e: e.tensor_tensor(out=og[:, :, t0:t0 + rows], in0=otm[:, :, 0:rows], in1=og[:, :, t0:t0 + rows], op=ALU.mult),
                                 reads=[otm_t], writes=[og_t[c][i] for c in range(NCH)])
                        if stn == NST - 1:
                            gp_ds = k.dsem("d_gp")
                            k.dma(sp, gp_ds, gp.rearrange("h d v -> d h v"), S32[:], reads=S32_t)
                if SPLIT:
                    run(linear_A_gen(wo_slab, wo_slab_t, og, og_t, make_z_evac(), ntl=halfB), layer_norm_gen(halfA, 0 * 8, 1 * 8))
                    pend = layer_norm_gen(halfB, 0 * 8, 1 * 8)
                else:
                    if nsamp:
                        linear_A(wo_slab, wo_slab_t, og, og_t, make_z_evac(), ntl=ntiles[1:])
                    else:
                        wo_slab, wo_slab_t = next_slab()
                        linear_A(wo_slab, wo_slab_t, og, og_t, make_z_evac(), nwarm=NWARM_WO)
                    pend = layer_norm_gen(ntiles, 0 * 8, 1 * 8)
                chk(9)
                pend = mlp_stage(0, pre=pend)
                chk(10)

            with Scope(k) as ss:
                yb = sb(ss, "yb", (128, NCH, NTMAX), BF16)
                yb_t = TS(NCH, 5)
                with Scope(k) as s2:
                    cgs = sb(s2, "cgs", (128, NCH, NTMAX), F32)
                    Ub = sb(s2, "Ub", (128, NCH, NPT + 2), F32)
                    Us = sb(s2, "Us", (128, NCH, NSEQ, 6), F32)
                    acc = [sb(s2, "acc", (128, NTMAX), F32) for _ in range(NCH)]
                    cgs_t = TS(NCH, 5)
                    Ub_t, Us_t = TS(NCH), TS(NCH)
                    acc_t = TS(NCH)
                    k.op(pool, lambda e: e.tensor_copy(out=Ub[:, :, 0:2], in_=tailb[:]), reads=[tail_t], writes=Ub_t)
                    if nsamp:
                        scin = sb(s2, "scin", (2 * NSEQ, D), F32)
                        scin_t = T()
                        sc_ds = k.dsem("d_scin")
                        k.dma(sp, sc_ds, scin[:], scv[:, :], writes=[scin_t])
                        for half in range(2):
                            pt, pt_t = misc_bank()
                            k.mm([lambda e, j=j: e.transpose(out=pt[:, j * 32:(j + 1) * 32], in_=scin[:, (half * 4 + j) * 128:(half * 4 + j + 1) * 128],
                                                             identity=cst[0:32, C_ID:C_ID + 32]) for j in range(4)],
                                 reads=[scin_t, c_t], writes=[pt_t])
                            src_ps = pt[:, 0:128].rearrange("p (a s r) -> p a s r", a=4, s=NSEQ)
                            k.op(dve, lambda e: e.tensor_copy(out=Us[:, half * 4:half * 4 + 4, :, 0:2], in_=src_ps),
                                 reads=[pt_t], writes=[Us_t[c] for c in range(half * 4, half * 4 + 4)])
                    slab, slab_t = next_slab()

                    def evac_cg(m, ni, n0, nn, ps, ps_t):
                        k.op(act, lambda e: e.activation(out=cgs[:, m, n0:n0 + nn], in_=ps[:, 0:nn], func=AF.Copy), reads=[ps_t], writes=xtr(cgs_t, [m], n0, nn))

                    if SPLIT:
                        run(linear_A_gen(slab, slab_t, xb, xb_t, evac_cg, ntl=halfA), pend)
                        linear_A(slab, slab_t, xb, xb_t, evac_cg, ntl=halfB)
                    else:
                        run(pend)
                        linear_after_ln(slab, slab_t, xb, xb_t, evac_cg)
                    pend = None
                    slab, slab_t = next_slab()

                    def evac_h(m, ni, n0, nn, ps, ps_t):
                        a_, a_t = acc[m], acc_t[m]
                        w0, w1, w2 = prm_col(72 + m), prm_col(80 + m), prm_col(88 + m)
                        if ni == 0:
                            k.op(dve, lambda e: e.tensor_tensor(out=Ub[:, m, 2:2 + NPT], in0=ps[:, 0:nn], in1=cgs[:, m, 0:NPT], op=ALU.mult),
                                 reads=[ps_t] + xtr(cgs_t, [m], n0, nn), writes=[Ub_t[m]])
                            k.op(act, lambda e: e.activation(out=a_[:, 0:NPT], in_=Ub[:, m, 2:2 + NPT], func=AF.Identity, scale=w2), reads=[Ub_t[m], c_t], writes=[a_t])
                            k.op(dve, lambda e: e.scalar_tensor_tensor(out=a_[:, 0:NPT], in0=Ub[:, m, 1:1 + NPT], scalar=w1, in1=a_[:, 0:NPT], op0=ALU.mult, op1=ALU.add),
                                 reads=[Ub_t[m], c_t, a_t], writes=[a_t])
                            k.op(dve, lambda e: e.scalar_tensor_tensor(out=a_[:, 0:NPT], in0=Ub[:, m, 0:NPT], scalar=w0, in1=a_[:, 0:NPT], op0=ALU.mult, op1=ALU.add),
                                 reads=[Ub_t[m], c_t, a_t], writes=[a_t])
                        else:
                            k.op(dve, lambda e: e.tensor_tensor(out=Us[:, m, :, 2:6], in0=ps[:, 0:nn].rearrange("p (s r) -> p s r", s=NSEQ),
                                                                in1=cgs[:, m, n0:n0 + nn].rearrange("p (s r) -> p s r", s=NSEQ), op=ALU.mult),
                                 reads=[ps_t] + xtr(cgs_t, [m], n0, nn), writes=[Us_t[m]])
                            av = a_[:, NPT:NPT + nsamp].rearrange("p (s r) -> p s r", s=NSEQ)
                            k.op(act, lambda e: e.activation(out=av, in_=Us[:, m, :, 2:6], func=AF.Identity, scale=w2), reads=[Us_t[m], c_t], writes=[a_t])
                            k.op(dve, lambda e: e.scalar_tensor_tensor(out=av, in0=Us[:, m, :, 1:5], scalar=w1, in1=av, op0=ALU.mult, op1=ALU.add),
                                 reads=[Us_t[m], c_t, a_t], writes=[a_t])
                            k.op(dve, lambda e: e.scalar_tensor_tensor(out=av, in0=Us[:, m, :, 0:4], scalar=w0, in1=av, op0=ALU.mult, op1=ALU.add),
                                 reads=[Us_t[m], c_t, a_t], writes=[a_t])

                    linear_A(slab, slab_t, xb, xb_t, evac_h)
                    slab, slab_t = next_slab()

                    def evac_bg(m, ni, n0, nn, ps, ps_t):
                        k.op(dve, lambda e: e.tensor_tensor(out=yb[:, m, n0:n0 + nn], in0=acc[m][:, n0:n0 + nn], in1=ps[:, 0:nn], op=ALU.mult),
                             reads=[ps_t, acc_t[m]], writes=xtr(yb_t, [m], n0, nn))

                    linear_A(slab, slab_t, xb, xb_t, evac_bg)
                    if stn < NST - 1:
                        k.op(pool, lambda e: e.tensor_copy(out=tailb[:], in_=Ub[:, :, NPT:NPT + 2]), reads=Ub_t, writes=[tail_t])
                    else:
                        cpo = sb(s2, "cpo", (2, D), F32)
                        cso = sb(s2, "cso", (2 * NSEQ, D), F32)
                        ustg = sb(s2, "ustg", (128, NCH, 2 * NSEQ), F32)
                        cpo_t, cso_t, ustg_t = T(), T(), T()
                        k.op(pool, lambda e: e.tensor_copy(out=ustg[:, :, :].rearrange("p c (s r) -> p c s r", s=NSEQ), in_=Us[:, :, :, 4:6]),
                             reads=Us_t, writes=[ustg_t])
                        for half in range(2):
                            pt, pt_t = misc_bank()
                            k.mm([lambda e, j=j: e.transpose(out=pt[0:2, j * 128:(j + 1) * 128], in_=Ub[:, half * 4 + j, NPT:NPT + 2], identity=ident)
                                  for j in range(4)], reads=Ub_t + [c_t], writes=[pt_t])
                            k.op(dve, lambda e: e.tensor_copy(out=cpo[:, half * 512:(half + 1) * 512], in_=pt[0:2, :]), reads=[pt_t], writes=[cpo_t])
                            pt2, pt2_t = misc_bank()
                            k.mm([lambda e, j=j: e.transpose(out=pt2[0:32, j * 128:(j + 1) * 128], in_=ustg[:, half * 4 + j, :], identity=ident)
                                  for j in range(4)], reads=[ustg_t, c_t], writes=[pt2_t])
                            k.op(dve, lambda e: e.tensor_copy(out=cso[:, half * 512:(half + 1) * 512], in_=pt2[0:32, :]), reads=[pt2_t], writes=[cso_t])
                        co_ds = k.dsem("d_co")
                        co2_ds = k.dsem("d_co2")
                        k.dma(sp, co_ds, cp[:, :], cpo[:], reads=[cpo_t])
                        k.dma(sp, co2_ds, cs[:, :], cso[:], reads=[cso_t])
                slab, slab_t = next_slab()
                if SPLIT:
                    linear_A(slab, slab_t, yb, yb_t, make_z_evac(), ntl=halfA)
                    run(linear_A_gen(slab, slab_t, yb, yb_t, make_z_evac(), ntl=halfB), layer_norm_gen(halfA, 4 * 8, 5 * 8))
                    pend = layer_norm_gen(halfB, 4 * 8, 5 * 8)
                else:
                    linear_A(slab, slab_t, yb, yb_t, make_z_evac())
                    pend = layer_norm_gen(ntiles, 4 * 8, 5 * 8)
                chk(11)
                pend = mlp_stage(1, final=True, pre=pend)
                chk(12)

            with Scope(k) as ss:
                nxt = [s_ for s_ in sts if s_ > stn]
                if nxt:
                    prefetch_x(nxt[0])
                yo = sb(ss, "yo", (128, 5, D), F32)
                yo_t = TS(5)
                yo_ds = [k.dsem("d_yo%d_%d" % (stn, i)) for i in range(5)]
                run(pend)
                pend = None
                for c in range(NCH):
                    pt, pt_t = misc_bank()
                    k.mm([lambda e, i=i: e.transpose(out=pt[:, i * 128:(i + 1) * 128], in_=xf[:, c, i * 128:(i + 1) * 128], identity=ident)
                          for i in range(4)], reads=[xf_t[c][i] for i in range(4)] + [c_t], writes=[pt_t])
                    ptv = pt[:, :].rearrange("p (a b) -> p a b", a=4)
                    if c % 2 == 0:
                        k.op(dve, lambda e: e.tensor_copy(out=yo[:, 0:4, c * 128:(c + 1) * 128], in_=ptv), reads=[pt_t], writes=yo_t[0:4])
                    else:
                        k.op(act, lambda e: e.activation(out=yo[:, 0:4, c * 128:(c + 1) * 128], in_=ptv, func=AF.Copy), reads=[pt_t], writes=yo_t[0:4])
                    if nsamp:
                        pt2, pt2_t = misc_bank()
                        k.mm([lambda e: e.transpose(out=pt2[0:nsamp, 0:128], in_=xf[:, c, NPT:NPT + nsamp], identity=ident)],
                             reads=[xf_t[c][4], c_t], writes=[pt2_t])
                        k.op(dve, lambda e: e.tensor_copy(out=yo[0:nsamp, 4, c * 128:(c + 1) * 128], in_=pt2[0:nsamp, 0:128]), reads=[pt2_t], writes=[yo_t[4]])
                nxt = [s_ for s_ in sts if s_ > stn]
                if nxt:
                    stage_X(nxt[0])
                for i, (t0, rows) in enumerate(ttiles):
                    dst = yp[stn * NPT + t0: stn * NPT + t0 + rows, :] if i < 4 else ys[0:rows, :]
                    k.dma(sp, yo_ds[i], dst, yo[0:rows, i, :], reads=[yo_t[i]])

          except _Stop:
            break

        for d in k.dsems:
            if d.n > 0:
                sp.e.wait_ge(d.sem, d.n)
    return nc


def _consts():
    cst = np.zeros((128, C_TOT), np.float32)
    cst[:, C_ID:C_ID + 128] = np.eye(128, dtype=np.float32)
    s = np.arange(128)[:, None]
    t = np.arange(128)[None, :]
    same = (s // 64) == (t // 64)
    cst[:, C_UP:C_UP + 128] = np.where(same & (s <= t), -1.0 / 16.0, 0.0)
    cst[:, C_M2P:C_M2P + 128] = np.where(same & (s > t), -1.0 / 16.0, 0.0)
    cst[:, C_CP:C_CP + 128] = np.where(same & (s <= t), 1.0, 0.0)
    same4 = ((s // 4) == (t // 4)) & (s < 64) & (t < 64)
    cst[:, C_US:C_US + 128] = np.where(same4 & (s <= t), -1.0 / 16.0, 0.0)
    cst[:, C_M2S:C_M2S + 128] = np.where(same4 & (s > t), -1.0 / 16.0, 0.0)
    cst[:, C_CS:C_CS + 128] = np.where(same4 & (s <= t), 1.0, 0.0)
    tt = np.arange(128)[:, None]
    ss = np.arange(16)[None, :]
    cst[:, C_SEQ:C_SEQ + 16] = np.where(((tt // 4) == ss) & (tt < 64), 1.0, 0.0)
    for h in range(4):
        cst[:, C_CP4 + h * 128:C_CP4 + (h + 1) * 128] = cst[:, C_CP:C_CP + 128]
        cst[:, C_CS4 + h * 64:C_CS4 + (h + 1) * 64] = cst[:, C_CS:C_CS + 64]
    return cst


def _fm(v):
    return np.ascontiguousarray(np.asarray(v, np.float32).reshape(8, 128).T)


_NC_CACHE = {}


def kernel(x_prompt, x_sample, state_gla, state_conv, gla_w_in, gla_w_gate_up, gla_b_gate, gla_norm_g,
           gla_w_o, conv_w_in, conv_w_conv, conv_w_out, mlp_w_up, mlp_w_down, ln1_g, ln1_b, ln2_g, ln2_b):
    f = lambda a: np.ascontiguousarray(np.asarray(a, dtype=np.float32))
    prm = np.zeros((128, 98), np.float32)
    prm[:, 96] = LN_EPS
    prm[:, 97] = RMS_EPS
    for l in range(2):
        for w_, arr in enumerate((ln1_g, ln1_b, ln2_g, ln2_b)):
            prm[:, (l * 4 + w_) * 8:(l * 4 + w_ + 1) * 8] = _fm(np.asarray(arr)[l])
    prm[:, 64:72] = _fm(np.asarray(gla_norm_g)[0].reshape(-1))
    for j in range(3):
        prm[:, 72 + j * 8:72 + (j + 1) * 8] = _fm(np.asarray(conv_w_conv)[0, j])
    cst = _consts()
    shared = {
        "gla_w_in": f(gla_w_in[0]), "gla_w_gate_up": f(gla_w_gate_up[0]), "gla_b_gate": f(gla_b_gate[0]).reshape(1, 512),
        "gla_w_o": f(gla_w_o[0]), "conv_w_in": f(conv_w_in[0]), "conv_w_out": f(conv_w_out[0]),
        "mlp_w_up": f(mlp_w_up), "mlp_w_down": f(mlp_w_down), "prm": prm, "cst": cst,
    }
    xpr = f(x_prompt)
    xsm = f(x_sample)
    sgl = f(state_gla)
    scn = f(state_conv)
    in_maps = []
    for c in range(8):
        m = dict(shared)
        m["xp"] = xpr[c]
        m["xs"] = xsm[16 * c:16 * c + 16].reshape(64, D)
        m["sg"] = sgl[0, 16 * c:16 * c + 16]
        m["sc"] = scn[0, 16 * c:16 * c + 16].reshape(32, D)
        in_maps.append(m)
    if "nc" not in _NC_CACHE:
        _NC_CACHE["nc"] = build_program()
    nc = _NC_CACHE["nc"]
    res = run_bass_kernel_spmd(nc, in_maps, core_ids=list(range(8)))
    r = res.results
    y_prompt = np.stack([r[c]["yp"] for c in range(8)], 0).astype(np.float32)
    y_sample = np.concatenate([r[c]["ys"].reshape(16, 4, D) for c in range(8)], 0).astype(np.float32)
    gla_p = np.stack([r[c]["gp"] for c in range(8)], 0)[None].astype(np.float32)
    gla_s = np.concatenate([r[c]["gs"] for c in range(8)], 0)[None].astype(np.float32)
    conv_p = np.stack([r[c]["cp"] for c in range(8)], 0)[None].astype(np.float32)
    conv_s = np.concatenate([r[c]["cs"].reshape(16, 2, D) for c in range(8)], 0)[None].astype(np.float32)
    return (y_prompt, y_sample, gla_p, gla_s, conv_p, conv_s)
```
